# Optimizing a Trainium2 kernel written in Bass

```python
import math
import jax, jax.numpy as jnp
from jax import lax
import numpy as np

D_MODEL = 1024
BATCH = 16
SEQ = 2048
DEPTH = 1

SSD_HEADS = 16
SSD_HEAD_DIM = 64
SSD_INNER = SSD_HEADS * SSD_HEAD_DIM
SSD_GROUPS = 2
SSD_STATE = 128
SSD_CONV = 4
SSD_CHUNK = 128
SSD_CONV_CH = SSD_INNER + 2 * SSD_GROUPS * SSD_STATE
MLA_HEADS = 16
MLA_Q_RANK = 384
MLA_KV_RANK = 256
MLA_NOPE = 64
MLA_ROPE = 32
MLA_V = 64
MLA_QK = MLA_NOPE + MLA_ROPE
ROPE_THETA = 10000.0
Q_BLOCK = 128
MEM_LEN = 256
XA_HEADS = 4
XA_HEAD_DIM = D_MODEL // XA_HEADS
D_FF = 2816
FFN_RES_WEIGHT = 0.5
N_BRANCHES = 2
EPS = 1e-6
IN_SIZES = (SSD_INNER, SSD_CONV_CH, SSD_HEADS, MLA_Q_RANK, MLA_KV_RANK, MLA_ROPE, N_BRANCHES * D_MODEL)
D_IN = SSD_INNER + SSD_CONV_CH + SSD_HEADS + MLA_Q_RANK + MLA_KV_RANK + MLA_ROPE + N_BRANCHES * D_MODEL

kernel_name = "hybrid_ssd_mla_gated_macaron"


def _split_points(sizes):
    pts, acc = [], 0
    for sz in sizes[:-1]:
        acc += sz
        pts.append(acc)
    return pts


def rms_norm(x, g):
    xf = x.astype(jnp.float32)
    y = xf * lax.rsqrt(jnp.mean(xf * xf, axis=-1, keepdims=True) + EPS)
    return (y * g.astype(jnp.float32)).astype(x.dtype)


def swiglu(h, w_gate, w_up, w_down):
    return (jax.nn.silu(h @ w_gate) * (h @ w_up)) @ w_down


def rope_cos_sin(positions, dim):
    inv = ROPE_THETA ** (-jnp.arange(0, dim, 2, dtype=jnp.float32) / dim)
    ang = positions.astype(jnp.float32)[..., None] * inv
    return jnp.cos(ang), jnp.sin(ang)


def apply_rope(x, cos, sin):
    x1, x2 = jnp.split(x.astype(jnp.float32), 2, axis=-1)
    return jnp.concatenate([x1 * cos - x2 * sin, x1 * sin + x2 * cos], axis=-1).astype(x.dtype)


def causal_depthwise_conv(u, w, b):
    out = lax.conv_general_dilated(u, w[:, None, :], window_strides=(1,),
                                   padding=((SSD_CONV - 1, 0),),
                                   dimension_numbers=('NWC', 'WIO', 'NWC'),
                                   feature_group_count=u.shape[-1])
    return out + b


def segsum(a):
    L = a.shape[-1]
    cs = jnp.cumsum(a, axis=-1)
    diff = cs[..., :, None] - cs[..., None, :]
    mask = jnp.tril(jnp.ones((L, L), dtype=bool))
    return jnp.where(mask, diff, -jnp.inf)


def ssd_chunked(xh, dt, a, bm, cm):
    bsz, s, h, p = xh.shape
    g, n = bm.shape[-2:]
    r = h // g
    L = SSD_CHUNK
    c = s // L
    f32 = jnp.float32
    xdt = (xh.astype(f32) * dt[..., None]).reshape(bsz, c, L, g, r, p)
    adt = (dt * a).reshape(bsz, c, L, g, r).transpose(0, 3, 4, 1, 2)
    bm = bm.astype(f32).reshape(bsz, c, L, g, n)
    cm = cm.astype(f32).reshape(bsz, c, L, g, n)
    a_cs = jnp.cumsum(adt, axis=-1)
    decay = jnp.exp(segsum(adt))
    cb = jnp.einsum('bclgn,bcsgn->bcgls', cm, bm)
    y_diag = jnp.einsum('bcgls,bgrcls,bcsgrp->bclgrp', cb, decay, xdt)
    decay_states = jnp.exp(a_cs[..., -1:] - a_cs)
    states = jnp.einsum('bclgn,bgrcl,bclgrp->bcgrpn', bm, decay_states, xdt)
    chunk_decay = jnp.exp(a_cs[..., -1])

    def step(carry, inp):
        st, dec = inp
        return carry * dec[..., None, None] + st, carry

    init = jnp.zeros((bsz, g, r, p, n), f32)
    _, prev = lax.scan(step, init, (states.transpose(1, 0, 2, 3, 4, 5), chunk_decay.transpose(3, 0, 1, 2)))
    prev = prev.transpose(1, 0, 2, 3, 4, 5)
    y_off = jnp.einsum('bclgn,bcgrpn,bgrcl->bclgrp', cm, prev, jnp.exp(a_cs))
    return (y_diag + y_off).reshape(bsz, s, h, p)


def causal_block_attention(q_nope, q_rope, k_nope, k_rope, v):
    bsz, s, h, _ = q_nope.shape
    nb = s // Q_BLOCK
    scale = MLA_QK ** -0.5
    kpos = jnp.arange(s)

    def blk(args):
        qn, qr, i = args
        sc = (jnp.einsum('bqhd,bkhd->bhqk', qn, k_nope)
              + jnp.einsum('bqhd,bkd->bhqk', qr, k_rope)).astype(jnp.float32) * scale
        qpos = i * Q_BLOCK + jnp.arange(Q_BLOCK)
        sc = jnp.where(kpos[None, :] <= qpos[:, None], sc, -jnp.inf)
        pr = jax.nn.softmax(sc, axis=-1).astype(v.dtype)
        return jnp.einsum('bhqk,bkhd->bqhd', pr, v)

    def to_blocks(t):
        return t.reshape(bsz, nb, Q_BLOCK, *t.shape[2:]).swapaxes(0, 1)

    out = lax.map(blk, (to_blocks(q_nope), to_blocks(q_rope), jnp.arange(nb)))
    return out.swapaxes(0, 1).reshape(bsz, s, h, -1)


def hybrid_mixer(h, positions, w_in, conv_w, conv_b, dt_bias, a_log, d_skip, ssd_norm_g, w_ssd_proj,
                 q_norm_g, w_uq, kv_norm_g, w_uk, w_uv, w_mla_proj, gate_bias, w_out):
    bsz, s, _ = h.shape
    f32 = jnp.float32
    proj = h @ w_in
    z, xbc, dt_raw, q_c, kv_c, k_r, gate_logits = jnp.split(proj, _split_points(IN_SIZES), axis=-1)

    xbc = jax.nn.silu(causal_depthwise_conv(xbc, conv_w, conv_b))
    xs, bm, cm = jnp.split(xbc, [SSD_INNER, SSD_INNER + SSD_GROUPS * SSD_STATE], axis=-1)
    xs = xs.reshape(bsz, s, SSD_HEADS, SSD_HEAD_DIM)
    bm = bm.reshape(bsz, s, SSD_GROUPS, SSD_STATE)
    cm = cm.reshape(bsz, s, SSD_GROUPS, SSD_STATE)
    dt = jax.nn.softplus((dt_raw + dt_bias).astype(f32))
    a = -jnp.exp(a_log.astype(f32))
    y = ssd_chunked(xs, dt, a, bm, cm) + d_skip.astype(f32)[:, None] * xs.astype(f32)
    y = y.reshape(bsz, s, SSD_INNER).astype(h.dtype) * jax.nn.silu(z)
    y = rms_norm(y.reshape(bsz, s, SSD_GROUPS, -1), ssd_norm_g.reshape(SSD_GROUPS, -1)).reshape(bsz, s, SSD_INNER)
    y_ssd = y @ w_ssd_proj

    cos, sin = rope_cos_sin(positions, MLA_ROPE)
    q = (rms_norm(q_c, q_norm_g) @ w_uq).reshape(bsz, s, MLA_HEADS, MLA_QK)
    q_nope = q[..., :MLA_NOPE]
    q_rope = apply_rope(q[..., MLA_NOPE:], cos[:, :, None], sin[:, :, None])
    kv_c = rms_norm(kv_c, kv_norm_g)
    k_nope = (kv_c @ w_uk).reshape(bsz, s, MLA_HEADS, MLA_NOPE)
    v = (kv_c @ w_uv).reshape(bsz, s, MLA_HEADS, MLA_V)
    k_rope = apply_rope(k_r, cos, sin)
    o = causal_block_attention(q_nope, q_rope, k_nope, k_rope, v).reshape(bsz, s, MLA_HEADS * MLA_V)
    y_mla = o @ w_mla_proj

    gates = jax.nn.sigmoid((gate_logits + gate_bias).astype(f32)).astype(h.dtype)
    g_ssd, g_mla = jnp.split(gates, N_BRANCHES, axis=-1)
    return (g_ssd * y_ssd + g_mla * y_mla) @ w_out


def memory_cross_attention(h, mem_n, w_q, w_k, w_v, w_o):
    bsz, s, _ = h.shape
    q = (h @ w_q).reshape(bsz, s, XA_HEADS, XA_HEAD_DIM)
    k = (mem_n @ w_k).reshape(bsz, -1, XA_HEADS, XA_HEAD_DIM)
    v = (mem_n @ w_v).reshape(bsz, -1, XA_HEADS, XA_HEAD_DIM)
    sc = jnp.einsum('bqhd,bkhd->bhqk', q, k).astype(jnp.float32) * (XA_HEAD_DIM ** -0.5)
    pr = jax.nn.softmax(sc, axis=-1).astype(v.dtype)
    o = jnp.einsum('bhqk,bkhd->bqhd', pr, v).reshape(bsz, s, D_MODEL)
    return o @ w_o


def setup_inputs(seed: int = 0) -> dict:
    key = jax.random.key(seed)
    keys = iter(jax.random.split(key, 48))
    f32 = jnp.float32

    def dense(fan_in, *shape):
        return jax.random.normal(next(keys), (DEPTH,) + shape, f32) * fan_in ** -0.5

    def gain(*shape):
        return 1.0 + 0.02 * jax.random.normal(next(keys), (DEPTH,) + shape, f32)

    def small(*shape):
        return 0.01 * jax.random.normal(next(keys), (DEPTH,) + shape, f32)

    x = jax.random.normal(next(keys), (BATCH, SEQ, D_MODEL), f32)
    mem = jax.random.normal(next(keys), (BATCH, MEM_LEN, D_MODEL), f32)
    offset = jax.random.randint(next(keys), (BATCH, 1), 0, 1024, dtype=jnp.int32)
    positions = (offset + jnp.arange(SEQ, dtype=jnp.int32)[None, :]).astype(jnp.int32)

    u = jax.random.uniform(next(keys), (DEPTH, SSD_HEADS), f32)
    dt0 = jnp.exp(u * (math.log(0.1) - math.log(0.001)) + math.log(0.001))
    dt_bias = dt0 + jnp.log(-jnp.expm1(-dt0))
    a_log = jnp.log(jax.random.uniform(next(keys), (DEPTH, SSD_HEADS), f32, minval=1.0, maxval=16.0))

    return {
        "x": x, "mem": mem, "positions": positions,
        "ffn1_pre_g": gain(D_MODEL), "ffn1_w_gate": dense(D_MODEL, D_MODEL, D_FF),
        "ffn1_w_up": dense(D_MODEL, D_MODEL, D_FF), "ffn1_w_down": dense(D_FF, D_FF, D_MODEL),
        "ffn1_post_g": gain(D_MODEL),
        "mix_pre_g": gain(D_MODEL), "w_in": dense(D_MODEL, D_MODEL, D_IN),
        "conv_w": dense(SSD_CONV, SSD_CONV, SSD_CONV_CH), "conv_b": small(SSD_CONV_CH),
        "dt_bias": dt_bias, "a_log": a_log,
        "d_skip": 1.0 + 0.1 * jax.random.normal(next(keys), (DEPTH, SSD_HEADS), f32),
        "ssd_norm_g": gain(SSD_INNER), "w_ssd_proj": dense(SSD_INNER, SSD_INNER, D_MODEL),
        "q_norm_g": gain(MLA_Q_RANK), "w_uq": dense(MLA_Q_RANK, MLA_Q_RANK, MLA_HEADS * MLA_QK),
        "kv_norm_g": gain(MLA_KV_RANK), "w_uk": dense(MLA_KV_RANK, MLA_KV_RANK, MLA_HEADS * MLA_NOPE),
        "w_uv": dense(MLA_KV_RANK, MLA_KV_RANK, MLA_HEADS * MLA_V),
        "w_mla_proj": dense(MLA_HEADS * MLA_V, MLA_HEADS * MLA_V, D_MODEL),
        "gate_bias": small(N_BRANCHES * D_MODEL), "w_out": dense(D_MODEL, D_MODEL, D_MODEL),
        "mix_post_g": gain(D_MODEL),
        "xa_pre_g": gain(D_MODEL), "mem_norm_g": gain(D_MODEL),
        "w_xq": dense(D_MODEL, D_MODEL, D_MODEL), "w_xk": dense(D_MODEL, D_MODEL, D_MODEL),
        "w_xv": dense(D_MODEL, D_MODEL, D_MODEL), "w_xo": dense(D_MODEL, D_MODEL, D_MODEL),
        "xa_post_g": gain(D_MODEL),
        "ffn2_pre_g": gain(D_MODEL), "ffn2_w_gate": dense(D_MODEL, D_MODEL, D_FF),
        "ffn2_w_up": dense(D_MODEL, D_MODEL, D_FF), "ffn2_w_down": dense(D_FF, D_FF, D_MODEL),
        "ffn2_post_g": gain(D_MODEL),
    }


def reference(x, mem, positions, ffn1_pre_g, ffn1_w_gate, ffn1_w_up, ffn1_w_down, ffn1_post_g,
              mix_pre_g, w_in, conv_w, conv_b, dt_bias, a_log, d_skip, ssd_norm_g, w_ssd_proj,
              q_norm_g, w_uq, kv_norm_g, w_uk, w_uv, w_mla_proj, gate_bias, w_out, mix_post_g,
              xa_pre_g, mem_norm_g, w_xq, w_xk, w_xv, w_xo, xa_post_g,
              ffn2_pre_g, ffn2_w_gate, ffn2_w_up, ffn2_w_down, ffn2_post_g):
    for l in range(DEPTH):
        h = swiglu(rms_norm(x, ffn1_pre_g[l]), ffn1_w_gate[l], ffn1_w_up[l], ffn1_w_down[l])
        x = x + FFN_RES_WEIGHT * rms_norm(h, ffn1_post_g[l])
        h = hybrid_mixer(rms_norm(x, mix_pre_g[l]), positions, w_in[l], conv_w[l], conv_b[l], dt_bias[l],
                         a_log[l], d_skip[l], ssd_norm_g[l], w_ssd_proj[l], q_norm_g[l], w_uq[l],
                         kv_norm_g[l], w_uk[l], w_uv[l], w_mla_proj[l], gate_bias[l], w_out[l])
        x = x + rms_norm(h, mix_post_g[l])
        h = memory_cross_attention(rms_norm(x, xa_pre_g[l]), rms_norm(mem, mem_norm_g[l]),
                                   w_xq[l], w_xk[l], w_xv[l], w_xo[l])
        x = x + rms_norm(h, xa_post_g[l])
        h = swiglu(rms_norm(x, ffn2_pre_g[l]), ffn2_w_gate[l], ffn2_w_up[l], ffn2_w_down[l])
        x = x + FFN_RES_WEIGHT * rms_norm(h, ffn2_post_g[l])
    return x
```

```python
import contextlib
import numpy as np
import concourse.bass as bass
import concourse.mybir as mybir
from concourse.bass_utils import run_bass_kernel_spmd

F32 = mybir.dt.float32
BF16 = mybir.dt.bfloat16
I32 = mybir.dt.int32
AF = mybir.ActivationFunctionType
ALU = mybir.AluOpType
PE, ACT, DVE, POOL, SP = "tensor", "scalar", "vector", "gpsimd", "sync"

D = 1024
SEQ = 2048
TT = 512
DFF = 2816
NFF = DFF // 128
MEM = 256
EPS = 1e-6
NCORES = 8


class Op:
    __slots__ = ("eng", "fn", "deps", "signal", "count", "dma", "dsem", "dval", "prev_dval")

    def __init__(self, eng, fn, dma):
        self.eng = eng
        self.fn = fn
        self.deps = []
        self.signal = False
        self.count = 0
        self.dma = dma
        self.dsem = None
        self.dval = 0
        self.prev_dval = 0


class Sched:
    NPOOL = 12

    def __init__(self, nc):
        self.nc = nc
        self.ops = {PE: [], ACT: [], DVE: [], POOL: [], SP: []}
        self.lastw = {}
        self.readers = {}
        self.dma_rr = {PE: 0, ACT: 0, DVE: 0, POOL: 0, SP: 0}
        self.dma_cnt = {}
        self.dma_last = {}

    def add(self, eng, fn, reads=(), writes=(), dma=False):
        op = Op(eng, fn, dma)
        deps = {}
        for k in reads:
            w = self.lastw.get(k)
            if w is not None:
                deps[id(w)] = w
        for k in writes:
            w = self.lastw.get(k)
            if w is not None:
                deps[id(w)] = w
            for r in self.readers.get(k, ()):
                deps[id(r)] = r
        for d in deps.values():
            if d is op:
                continue
            if eng == PE and d.eng == PE and not d.dma and not dma:
                continue
            op.deps.append(d)
            if not d.dma:
                d.signal = True
        for k in reads:
            self.readers.setdefault(k, []).append(op)
        for k in writes:
            self.lastw[k] = op
            self.readers[k] = []
        if dma:
            slot = (eng, self.dma_rr[eng] % self.NPOOL)
            self.dma_rr[eng] += 1
            op.dsem = slot
            op.prev_dval = self.dma_cnt.get(slot, 0)
            op.dval = op.prev_dval + 16
            self.dma_cnt[slot] = op.dval
            self.dma_last[slot] = op
        self.ops[eng].append(op)
        return op

    def barrier(self):
        lasts = []
        for e, lst in self.ops.items():
            for o in reversed(lst):
                if o.fn is not None and not o.dma:
                    lasts.append(o)
                    break
        lasts += list(self.dma_last.values())
        for e in self.ops:
            op = Op(e, None, False)
            for d in lasts:
                op.deps.append(d)
                if not d.dma:
                    d.signal = True
            self.ops[e].append(op)
        self.lastw = {}
        self.readers = {}

    def emit(self, final_waits=()):
        nc = self.nc
        with contextlib.ExitStack() as es:
            engsem = {e: es.enter_context(nc.semaphore("s_" + e)) for e in (PE, ACT, DVE, POOL)}
            dsems = {}
            for e in self.ops:
                if any(o.dma for o in self.ops[e]):
                    for i in range(self.NPOOL):
                        dsems[(e, i)] = es.enter_context(nc.semaphore("d_%s_%d" % (e, i)))
            for e, lst in self.ops.items():
                c = 0
                for o in lst:
                    if o.signal and not o.dma:
                        c += 1
                        o.count = c
            block = es.enter_context(nc.Block())

            def run(e, lst, extra):
                def body(eng):
                    waited = {}

                    def need(sem, val):
                        if waited.get(id(sem), 0) >= val:
                            return
                        waited[id(sem)] = val
                        eng.wait_ge(sem, val)

                    for o in lst:
                        for d in o.deps:
                            if d.dma:
                                need(dsems[d.dsem], d.dval)
                            else:
                                need(engsem[d.eng], d.count)
                        if o.dma and o.prev_dval:
                            need(dsems[o.dsem], o.prev_dval)
                        if o.fn is None:
                            continue
                        inst = o.fn(eng)
                        if o.dma:
                            inst.then_inc(dsems[o.dsem], 16)
                        elif o.signal:
                            inst.then_inc(engsem[e], 1)
                    for d in extra:
                        need(dsems[d.dsem], d.dval)
                return body

            for e in (PE, ACT, DVE, POOL):
                getattr(block, e)(run(e, self.ops[e], ()))
            getattr(block, SP)(run(SP, self.ops[SP], list(final_waits)))


class Arena:
    def __init__(self, ap, nwords):
        self.ap = ap
        self.n = nwords
        self.top = 0
        self.stack = []
        self.peak = 0

    def mark(self):
        self.stack.append(self.top)

    def release(self):
        self.top = self.stack.pop()

    def _take(self, words):
        a = self.top
        self.top += words
        self.peak = max(self.peak, self.top)
        assert self.top <= self.n, ("arena overflow", self.top, self.n)
        return self.ap[:, a:a + words]

    def f32(self, shape):
        n = int(np.prod(shape))
        v = self._take(n)
        if len(shape) == 2:
            v = v.rearrange("p (a b) -> p a b", a=shape[0])
        elif len(shape) == 3:
            v = v.rearrange("p (a b c) -> p a b c", a=shape[0], b=shape[1])
        return v

    def bf16(self, shape):
        n = int(np.prod(shape))
        assert n % 2 == 0
        v = self._take(n // 2).bitcast(BF16)
        if len(shape) == 2:
            v = v.rearrange("p (a b) -> p a b", a=shape[0])
        elif len(shape) == 3:
            v = v.rearrange("p (a b c) -> p a b c", a=shape[0], b=shape[1])
        return v


def blockify(w, mb):
    k, m = w.shape
    assert k % 128 == 0 and m % mb == 0
    return np.ascontiguousarray(w.reshape(k // 128, 128, m // mb, mb).transpose(2, 1, 0, 3))


def colvec(v):
    return np.ascontiguousarray(v.reshape(-1, 128).T)


VEC_SPEC = [("ffn1_pre_g", 8), ("ffn1_post_g", 8), ("mix_pre_g", 8), ("mix_post_g", 8), ("xa_pre_g", 8),
            ("mem_norm_g", 8), ("xa_post_g", 8), ("ffn2_pre_g", 8), ("ffn2_post_g", 8),
            ("ssd_norm_g", 8), ("q_norm_g", 3), ("kv_norm_g", 2), ("conv_w", 48), ("conv_b", 12),
            ("gb_ssd", 8), ("gb_mla", 8), ("d_skip", 8), ("dt_bias", 16), ("a_log", 16),
            ("invf", 1), ("sgn", 1)]
VO = {}
_o = 0
for _n, _c in VEC_SPEC:
    VO[_n] = _o
    _o += _c
NV = _o
WSLOT = 1408
PI = float(np.pi)
SCALE_MLA = 96.0 ** -0.5
SCALE_XA = 256.0 ** -0.5


class K:
    def __init__(self, nseq, stop_after=99):
        self.nseq = nseq
        self.stop_after = stop_after
        self.nc = bass.Bass("TRN2", target_bir_lowering=False)
        self.S = Sched(self.nc)
        self.din = {}
        self.bank_i = 0
        self.wb_i = 0

    def inp(self, name, shape, dtype=F32):
        t = self.nc.dram_tensor(name, list(shape), dtype, kind="ExternalInput").ap()
        self.din[name] = t
        return t

    def bank(self):
        i = self.bank_i % 5
        self.bank_i += 1
        return self.ps[i], ("ps", i)

    def held(self, i):
        return self.ps[i], ("ps", i)

    def v(self, name, c=0, n=1):
        o = VO[name] + c
        return self.vecs[:, o:o + n]

    def wload(self, src, nelem):
        i = self.wb_i % self.NWB
        self.wb_i += 1
        key = ("wb", i)
        dst = self.wbufs[i][:, 0:nelem]
        self.S.add(POOL, lambda e: e.dma_start(out=dst, in_=src, max_dma_last_dim=4096),
                   writes=[key], dma=True)
        return dst, key

    def wload_multi(self, src, nb, n):
        i = self.wb_i % self.NWB
        self.wb_i += 1
        key = ("wb", i)
        dst = self.wbufs[i][:, 0:nb * n].rearrange("p (b n) -> p b n", b=nb)
        self.S.add(POOL, lambda e: e.dma_start(out=dst, in_=src.rearrange("b p n -> p b n"), max_dma_last_dim=4096),
                   writes=[key], dma=True)
        return dst, key

    def mm(self, out, pk, pairs):
        n = len(pairs)
        for i, (l, r, ks) in enumerate(pairs):
            self.S.add(PE, lambda e, l=l, r=r, i=i: e.matmul(out, lhsT=l, rhs=r, start=(i == 0), stop=(i == n - 1)),
                       reads=ks, writes=[pk])

    def lin(self, out, pk, wb, wk, kcn, mb, acts, m0=0, m=128):
        self.mm(out, pk, [(wb[:, kc * mb + m0:kc * mb + m0 + m], acts[kc][0], [wk, acts[kc][1]]) for kc in range(kcn)])

    def rms_rstd(self, srcs, nfeat, sq, sqname, rstd, rkey):
        S = self.S
        fc = len(srcs)
        T = rstd.shape[-1]
        sk = sqname if callable(sqname) else (lambda c: (sqname, c))
        for c, (ap, k) in enumerate(srcs):
            S.add(ACT, lambda e, c=c, ap=ap: e.activation(out=sq[:, c, :], in_=ap, func=AF.Square),
                  reads=[k], writes=[sk(c)])
        ps, pk = self.bank()
        self.mm(ps[:, 0:T], pk, [(self.ones_bf[:, :], sq[:, c, :], [sk(c), "const"]) for c in range(fc)])
        S.add(ACT, lambda e: e.activation(out=rstd, in_=ps[:, 0:T], func=AF.Sqrt, scale=1.0 / nfeat,
                                          bias=self.eps_ap[:, 0:1]),
              reads=[pk, "const"], writes=[rkey])
        S.add(DVE, lambda e: e.reciprocal(out=rstd, in_=rstd), reads=[rkey], writes=[rkey])

    def norm_apply(self, out, okey, src, skey, gcol, rstd, rkey):
        self.S.add(DVE, lambda e: e.scalar_tensor_tensor(out=out, in0=src, scalar=gcol, in1=rstd,
                                                         op0=ALU.mult, op1=ALU.mult),
                   reads=[skey, rkey, "vecs"], writes=[okey])

    def ffn(self, wg, wu, wd, gpre, gpost):
        S, A = self.S, self.AH
        x = self.x
        HT = 1024
        NH = HT // TT
        for half in range(SEQ // HT):
            A.mark()
            xn = A.bf16([8, HT])
            h = A.bf16([NFF, HT])
            y = A.f32([8, HT])
            sq = A.bf16([8, TT])
            rstd = A.f32([TT])
            sg = [A.f32([TT]) for _ in range(2)]
            for tt in range(NH):
                t0 = half * HT + tt * TT
                gt = t0 // TT
                self.rms_rstd([(x[:, c, t0:t0 + TT], ("x", c, gt)) for c in range(8)], D, sq, "sq", rstd, "rstd")
                for c in range(8):
                    self.norm_apply(xn[:, c, tt * TT:(tt + 1) * TT], ("xn", c, tt), x[:, c, t0:t0 + TT], ("x", c, gt),
                                    self.v(gpre, c), rstd, "rstd")
            for fb in range(NFF):
                wgb, wgk = self.wload(wg[fb], 1024)
                wub, wuk = self.wload(wu[fb], 1024)
                for tt in range(NH):
                    acts = [(xn[:, kc, tt * TT:(tt + 1) * TT], ("xn", kc, tt)) for kc in range(8)]
                    pg, pgk = self.bank()
                    pu, puk = self.bank()
                    self.lin(pg[:, :], pgk, wgb, wgk, 8, 128, acts)
                    self.lin(pu[:, :], puk, wub, wuk, 8, 128, acts)
                    sgb = sg[(fb * NH + tt) % 2]
                    sgk = ("sg", (fb * NH + tt) % 2)
                    S.add(ACT, lambda e, pg=pg, sgb=sgb: e.activation(out=sgb, in_=pg[:, :], func=AF.Silu),
                          reads=[pgk], writes=[sgk])
                    S.add(DVE, lambda e, pu=pu, sgb=sgb, fb=fb, tt=tt: e.tensor_tensor(
                        out=h[:, fb, tt * TT:(tt + 1) * TT], in0=pu[:, :], in1=sgb, op=ALU.mult),
                        reads=[puk, sgk], writes=[("h", fb, tt)])
            HF = NFF // 2
            for dc in range(8):
                wd0, wdk0 = self.wload(wd[dc, 0], HF * 128)
                wd1, wdk1 = self.wload(wd[dc, 1], HF * 128)
                for tt in range(NH):
                    py, pyk = self.bank()
                    pairs = []
                    for fc in range(NFF):
                        wb, wk = (wd0, wdk0) if fc < HF else (wd1, wdk1)
                        f = fc % HF
                        pairs.append((wb[:, f * 128:(f + 1) * 128], h[:, fc, tt * TT:(tt + 1) * TT], [wk, ("h", fc, tt)]))
                    self.mm(py[:, :], pyk, pairs)
                    S.add(ACT, lambda e, py=py, dc=dc, tt=tt: e.activation(
                        out=y[:, dc, tt * TT:(tt + 1) * TT], in_=py[:, :], func=AF.Copy),
                        reads=[pyk], writes=[("y", dc, tt)])
            for tt in range(NH):
                t0 = half * HT + tt * TT
                gt = t0 // TT
                self.rms_rstd([(y[:, c, tt * TT:(tt + 1) * TT], ("y", c, tt)) for c in range(8)], D, sq, "sq", rstd, "rstd")
                for c in range(8):
                    ysl = y[:, c, tt * TT:(tt + 1) * TT]
                    self.norm_apply(ysl, ("y", c, tt), ysl, ("y", c, tt), self.v(gpost, c), rstd, "rstd")
                    S.add(DVE, lambda e, c=c, ysl=ysl, t0=t0: e.scalar_tensor_tensor(
                        out=x[:, c, t0:t0 + TT], in0=ysl, scalar=0.5, in1=x[:, c, t0:t0 + TT],
                        op0=ALU.mult, op1=ALU.add),
                        reads=[("y", c, tt), ("x", c, gt)], writes=[("x", c, gt)])
            A.release()
            S.barrier()

    def rope_tables(self, s, t0, bufs, cosb, sinb):
        S = self.S
        posi, posf, ang, kf, tmp = bufs
        S.add(SP, lambda e: e.dma_start(out=posi, in_=self.pos_d[s:s + 1, t0:t0 + TT].partition_broadcast(128)),
              writes=["posi"], dma=True)
        S.add(DVE, lambda e: e.tensor_copy(out=posf, in_=posi), reads=["posi"], writes=["posf"])
        S.add(DVE, lambda e: e.tensor_scalar(out=ang, in0=posf, scalar1=self.v("invf"), scalar2=None, op0=ALU.mult),
              reads=["posf", "vecs"], writes=["ang"])
        ki = posi
        S.add(DVE, lambda e: e.tensor_scalar(out=ki, in0=ang, scalar1=1.0 / (2 * PI), scalar2=None, op0=ALU.mult),
              reads=["ang", "posf"], writes=["posi"])
        S.add(DVE, lambda e: e.tensor_copy(out=kf, in_=ki), reads=["posi"], writes=["kf"])
        S.add(DVE, lambda e: e.scalar_tensor_tensor(out=ang, in0=kf, scalar=-2 * PI, in1=ang, op0=ALU.mult, op1=ALU.add),
              reads=["kf", "ang"], writes=["ang"])
        for which, shift, dst, dk in (("sin", 0.0, sinb, "sinb"), ("cos", PI / 2, cosb, "cosb")):
            y = kf
            S.add(DVE, lambda e, shift=shift: e.tensor_scalar(out=y, in0=ang, scalar1=shift, scalar2=None, op0=ALU.add),
                  reads=["ang"], writes=["kf"])
            S.add(DVE, lambda e: e.tensor_scalar(out=tmp, in0=y, scalar1=PI, scalar2=2 * PI, op0=ALU.is_gt, op1=ALU.mult),
                  reads=["kf"], writes=["ropetmp"])
            S.add(DVE, lambda e: e.tensor_tensor(out=y, in0=y, in1=tmp, op=ALU.subtract),
                  reads=["kf", "ropetmp"], writes=["kf"])
            S.add(DVE, lambda e: e.tensor_scalar(out=tmp, in0=y, scalar1=-PI, scalar2=2 * PI, op0=ALU.is_lt, op1=ALU.mult),
                  reads=["kf"], writes=["ropetmp"])
            S.add(DVE, lambda e: e.tensor_tensor(out=y, in0=y, in1=tmp, op=ALU.add),
                  reads=["kf", "ropetmp"], writes=["kf"])
            S.add(DVE, lambda e: e.tensor_scalar(out=y, in0=y, scalar1=PI, scalar2=-PI, op0=ALU.min, op1=ALU.max),
                  reads=["kf"], writes=["kf"])
            if which == "sin":
                S.add(ACT, lambda e, dst=dst: e.activation(out=dst, in_=y, func=AF.Sin, scale=self.v("sgn")),
                      reads=["kf", "vecs"], writes=[dk])
            else:
                S.add(ACT, lambda e, dst=dst: e.activation(out=dst, in_=y, func=AF.Sin),
                      reads=["kf"], writes=[dk])

    def rope_apply(self, out, okey, psA, pkA, psB, pkB, cosb, sinb, t1, t2):
        S = self.S
        S.add(DVE, lambda e: e.tensor_tensor(out=t1, in0=psA, in1=cosb, op=ALU.mult), reads=[pkA, "cosb"], writes=["rt1"])
        S.add(DVE, lambda e: e.tensor_tensor(out=t2, in0=psB, in1=sinb, op=ALU.mult), reads=[pkB, "sinb"], writes=["rt2"])
        S.add(DVE, lambda e: e.tensor_tensor(out=out, in0=t1, in1=t2, op=ALU.add), reads=["rt1", "rt2"], writes=[okey])

    def mixer(self, s):
        S, AH, AX = self.S, self.AH, self.AX
        x = self.x
        W = self.w
        NT = SEQ // TT
        AH.mark()
        xnt = AH.bf16([8, TT])
        mgt = AH.bf16([8, TT])
        xnh = self.xn_hbm.rearrange("(c p) t -> p c t", p=128)
        mgh = self.mg_hbm.rearrange("(c p) t -> p c t", p=128)
        XK = [("xnt", c) for c in range(8)]
        AH.mark()
        sq = AH.bf16([8, TT])
        rstd = AH.f32([TT])
        for t in range(NT):
            t0 = t * TT
            self.rms_rstd([(x[:, c, t0:t0 + TT], ("x", c, t)) for c in range(8)], D, sq, "sq", rstd, "rstd")
            for c in range(8):
                self.norm_apply(xnt[:, c, :], ("xnt", c), x[:, c, t0:t0 + TT], ("x", c, t),
                                self.v("mix_pre_g", c), rstd, "rstd")
            S.add(SP, lambda e, t0=t0: e.dma_start(out=xnh[:, :, t0:t0 + TT], in_=xnt[:, :, :]),
                  reads=XK, writes=[("xnh", t)], dma=True)
        for c in range(8):
            S.add(SP, lambda e, c=c: e.dma_start(out=self.xsp[c * 128:(c + 1) * 128, :], in_=x[:, c, :]),
                  reads=[("x", c, t) for t in range(NT)], writes=[("xsp", c, t) for t in range(NT)], dma=True)
        AH.release()
        S.barrier()

        def load_xnt(t):
            S.add(SP, lambda e, t=t: e.dma_start(out=xnt[:, :, :], in_=xnh[:, :, t * TT:(t + 1) * TT]),
                  reads=[("xnh", t)], writes=XK, dma=True)

        AH.mark()
        AX.mark()
        Kn = AX.bf16([8, SEQ])
        Kr = AX.bf16([SEQ])
        qn = AX.bf16([3, SEQ])
        V = AH.bf16([16, 16, 68])
        cosb = AH.f32([TT])
        sinb = AH.f32([TT])
        rbufs = (AH.f32([TT]).bitcast(I32), AH.f32([TT]), AH.f32([TT]), AH.f32([TT]), AH.f32([TT]))
        t1 = AH.f32([TT])
        t2 = AH.f32([TT])
        S.add(DVE, lambda e: e.memset(V[:, :, :, :], 1.0), writes=[("V", b) for b in range(16)])
        AX.mark()
        AH.mark()
        kvc = AH.f32([2, TT])
        qc = AH.f32([3, TT])
        sq = AH.bf16([3, TT])
        rstd = AH.f32([TT])
        kvn = AH.bf16([2, TT])
        for t in range(NT):
            t0 = t * TT
            load_xnt(t)
            acts = [(xnt[:, kc, :], ("xnt", kc)) for kc in range(8)]
            self.rope_tables(s, t0, rbufs, cosb, sinb)
            for c in range(2):
                wb, wk = self.wload(W["w_kvc"][c], 1024)
                ps, pk = self.bank()
                self.lin(ps[:, :], pk, wb, wk, 8, 128, acts)
                S.add(ACT, lambda e, ps=ps, c=c: e.activation(out=kvc[:, c, :], in_=ps[:, :], func=AF.Copy),
                      reads=[pk], writes=[("kvc", c)])
            self.rms_rstd([(kvc[:, c, :], ("kvc", c)) for c in range(2)], 256, sq, "sq", rstd, "rstd")
            for c in range(2):
                self.norm_apply(kvn[:, c, :], ("kvn", c), kvc[:, c, :], ("kvc", c), self.v("kv_norm_g", c), rstd, "rstd")
            kacts = [(kvn[:, kc, :], ("kvn", kc)) for kc in range(2)]
            for i in range(2):
                wb, wk = self.wload_multi(W["w_uk"][4 * i:4 * i + 4], 4, 256)
                for b in range(4):
                    c = 4 * i + b
                    ps, pk = self.bank()
                    self.mm(ps[:, :], pk, [(wb[:, b, kc * 128:(kc + 1) * 128], kacts[kc][0], [wk, kacts[kc][1]]) for kc in range(2)])
                    S.add(ACT, lambda e, ps=ps, c=c, t0=t0: e.activation(out=Kn[:, c, t0:t0 + TT], in_=ps[:, :], func=AF.Copy),
                          reads=[pk], writes=[("Kn", c, t)])
            for half in range(2):
                wb, wk = self.wload(W["w_uv"][half], 1024)
                for j in range(4):
                    ps, pk = self.bank()
                    self.mm(ps[:, :], pk, [(kvn[:, kc, j * 128:(j + 1) * 128], wb[:, kc * 512:(kc + 1) * 512], [wk, ("kvn", kc)]) for kc in range(2)])
                    blk = t * 4 + j
                    S.add(DVE, lambda e, ps=ps, blk=blk, half=half: e.tensor_copy(
                        out=V[:, blk, half * 8:(half + 1) * 8, 0:64], in_=ps[:, :].rearrange("p (h d) -> p h d", h=8)),
                        reads=[pk], writes=[("V", blk)])
            wa, wak = self.wload(W["w_kra"][0], 1024)
            wbb, wbk = self.wload(W["w_krb"][0], 1024)
            psA, pkA = self.bank()
            psB, pkB = self.bank()
            self.lin(psA[:, :], pkA, wa, wak, 8, 128, acts)
            self.lin(psB[:, :], pkB, wbb, wbk, 8, 128, acts)
            self.rope_apply(Kr[:, t0:t0 + TT], ("Kr", t), psA[:, :], pkA, psB[:, :], pkB, cosb, sinb, t1, t2)
            for c in range(3):
                wb, wk = self.wload(W["w_qc"][c], 1024)
                ps, pk = self.bank()
                self.lin(ps[:, :], pk, wb, wk, 8, 128, acts)
                S.add(ACT, lambda e, ps=ps, c=c: e.activation(out=qc[:, c, :], in_=ps[:, :], func=AF.Copy),
                      reads=[pk], writes=[("qc", c)])
            self.rms_rstd([(qc[:, c, :], ("qc", c)) for c in range(3)], 384, sq, "sq", rstd, "rstd")
            for c in range(3):
                self.norm_apply(qn[:, c, t0:t0 + TT], ("qn", c, t), qc[:, c, :], ("qc", c), self.v("q_norm_g", c), rstd, "rstd")
        AX.release()
        AH.release()

        AX.mark()
        qnope = AX.bf16([8, TT])
        qrope = AX.bf16([4, TT])
        pts = [AH.bf16([TT]) for _ in range(4)]
        o = AH.bf16([8, TT])
        rden = AH.f32([TT])
        rb = AH.f32([TT])
        gbuf = [AH.f32([TT]) for _ in range(2)]
        pti = 0
        for qt in range(NT):
            t0 = qt * TT
            self.rope_tables(s, t0, rbufs, cosb, sinb)
            qacts = [(qn[:, kc, t0:t0 + TT], ("qn", kc, qt)) for kc in range(3)]
            for (b0, nb) in ((0, 3), (3, 3), (6, 2)):
                wb, wk = self.wload_multi(W["w_uqn"][b0:b0 + nb], nb, 384)
                for b in range(nb):
                    c = b0 + b
                    ps, pk = self.bank()
                    self.mm(ps[:, :], pk, [(wb[:, b, kc * 128:(kc + 1) * 128], qacts[kc][0], [wk, qacts[kc][1]]) for kc in range(3)])
                    S.add(ACT, lambda e, ps=ps, c=c: e.activation(out=qnope[:, c, :], in_=ps[:, :], func=AF.Copy),
                          reads=[pk], writes=[("qnope", c)])
            for i in range(2):
                wa, wak = self.wload_multi(W["w_uqa"][2 * i:2 * i + 2], 2, 384)
                wbb, wbk = self.wload_multi(W["w_uqb"][2 * i:2 * i + 2], 2, 384)
                for b in range(2):
                    j = 2 * i + b
                    psA, pkA = self.bank()
                    psB, pkB = self.bank()
                    self.mm(psA[:, :], pkA, [(wa[:, b, kc * 128:(kc + 1) * 128], qacts[kc][0], [wak, qacts[kc][1]]) for kc in range(3)])
                    self.mm(psB[:, :], pkB, [(wbb[:, b, kc * 128:(kc + 1) * 128], qacts[kc][0], [wbk, qacts[kc][1]]) for kc in range(3)])
                    self.rope_apply(qrope[:, j, :], ("qrope", j), psA[:, :], pkA, psB[:, :], pkB, cosb, sinb, t1, t2)
            for h in range(16):
                hb = (h % 2) * 64
                hc = h // 2
                rbp = (h % 4) * 32
                rj = h // 4
                po, pok = self.held(5 + h % 2)
                nkb = 4 * qt + 4
                for kb in range(nkb):
                    jj = kb - 4 * qt
                    c0 = 0 if jj < 0 else jj * 128
                    kt = kb // 4
                    ks = slice(kb * 128, (kb + 1) * 128)
                    ps, pk = self.bank()
                    tp = None if rbp < 96 else (96, 0)
                    S.add(PE, lambda e, ps=ps, hb=hb, hc=hc, ks=ks, c0=c0: e.matmul(
                        ps[:, c0:TT], lhsT=Kn[hb:hb + 64, hc, ks], rhs=qnope[hb:hb + 64, hc, c0:TT], start=True, stop=False),
                        reads=[("Kn", hc, kt), ("qnope", hc)], writes=[pk])
                    S.add(PE, lambda e, ps=ps, rbp=rbp, rj=rj, ks=ks, c0=c0, tp=tp: e.matmul(
                        ps[:, c0:TT], lhsT=Kr[rbp:rbp + 32, ks], rhs=qrope[rbp:rbp + 32, rj, c0:TT], start=False, stop=True,
                        **({} if tp is None else {"tile_position": tp})),
                        reads=[("Kr", kt), ("qrope", rj)], writes=[pk])
                    pt = pts[pti % 4]
                    ptk = ("pt", pti % 4)
                    pti += 1
                    S.add(ACT, lambda e, ps=ps, pt=pt, c0=c0: e.activation(out=pt[:, c0:TT], in_=ps[:, c0:TT], func=AF.Exp,
                                                                         scale=SCALE_MLA),
                          reads=[pk], writes=[ptk])
                    if jj >= 0:
                        S.add(DVE, lambda e, pt=pt, c0=c0: e.tensor_tensor(out=pt[:, c0:c0 + 128], in0=pt[:, c0:c0 + 128],
                                                                         in1=self.tri_bf[:, :], op=ALU.mult),
                              reads=[ptk, "const"], writes=[ptk])
                    S.add(PE, lambda e, po=po, pt=pt, kb=kb, h=h, c0=c0, nkb=nkb: e.matmul(
                        po[0:65, c0:TT], lhsT=V[:, kb, h, 0:65], rhs=pt[:, c0:TT], start=(kb == 0), stop=(kb == nkb - 1)),
                        reads=[("V", kb), ptk], writes=[pok])
                S.add(DVE, lambda e, po=po: e.reciprocal(out=rden[64:65, :], in_=po[64:65, :]), reads=[pok], writes=["rden"])
                pb, pbk = self.bank()
                S.add(PE, lambda e, pb=pb: e.matmul(pb[0:64, :], lhsT=self.ones_f[64:65, 0:64], rhs=rden[64:65, :],
                                                  start=True, stop=True),
                      reads=["rden", "const"], writes=[pbk])
                S.add(ACT, lambda e, pb=pb: e.activation(out=rb[0:64, :], in_=pb[0:64, :], func=AF.Copy),
                      reads=[pbk], writes=["rb"])
                S.add(DVE, lambda e, po=po, hb=hb, hc=hc: e.tensor_tensor(out=o[hb:hb + 64, hc, :], in0=po[0:64, :],
                                                                        in1=rb[0:64, :], op=ALU.mult),
                      reads=[pok, "rb"], writes=[("o", hc)])
            load_xnt(qt)
            xacts = [(xnt[:, kc, :], ("xnt", kc)) for kc in range(8)]
            oacts = [(o[:, c, :], ("o", c)) for c in range(8)]
            for c in range(8):
                wb, wk = self.wload(W["w_mla"][c], 1024)
                wg_, wgk = self.wload(W["w_gm"][c], 1024)
                ps, pk = self.bank()
                ps2, pk2 = self.bank()
                self.lin(ps2[:, :], pk2, wg_, wgk, 8, 128, xacts)
                self.lin(ps[:, :], pk, wb, wk, 8, 128, oacts)
                g = gbuf[c % 2]
                gk = ("g", c % 2)
                S.add(ACT, lambda e, ps2=ps2, g=g, c=c: e.activation(out=g, in_=ps2[:, :], func=AF.Sigmoid,
                                                                    bias=self.v("gb_mla", c)),
                      reads=[pk2, "vecs"], writes=[gk])
                S.add(DVE, lambda e, ps=ps, g=g, c=c: e.tensor_tensor(out=mgt[:, c, :], in0=ps[:, :], in1=g, op=ALU.mult),
                      reads=[pk, gk], writes=[("mgt", c)])
            S.add(SP, lambda e, t0=t0: e.dma_start(out=mgh[:, :, t0:t0 + TT], in_=mgt[:, :, :]),
                  reads=[("mgt", c) for c in range(8)], writes=[("mgh", qt)], dma=True)
        AX.release()
        AX.release()
        AH.release()
        S.barrier()

        self.ssd(s, xnt, mgt, xnh, mgh)
        AH.release()
        S.barrier()
        for c in range(8):
            S.add(SP, lambda e, c=c: e.dma_start(out=x[:, c, :], in_=self.xsp[c * 128:(c + 1) * 128, :]),
                  writes=[("x", c, t) for t in range(4)], dma=True)

    def ssd(self, s, xnt, mgt, xnh, mgh):
        S, AH, AX = self.S, self.AH, self.AX
        W = self.w
        NT = SEQ // TT
        AX.mark()
        AH.mark()
        halo = AX.f32([12, 4])
        prevT = AX.f32([1024])
        prevTb = AX.bf16([1024])
        BA = AX.f32([8, TT])
        BB = AX.f32([8, TT])
        BC = AX.f32([8, TT])
        BT = AX.bf16([2, TT])
        CT = AX.bf16([2, TT])
        Btok = AX.bf16([4, 2, 128])
        U1 = AH.bf16([4, 1024])
        U2 = AH.bf16([4, 1024])
        U3 = AH.f32([4, TT])
        U4 = AH.f32([4, 516])
        mg = U1[:, :, :].rearrange("p j (a t) -> p (j a) t", a=2)
        yn = U2[:, :, :].rearrange("p j (a t) -> p (j a) t", a=2)
        sqv = U3[:, :, :].rearrange("p a t -> p (a t)").bitcast(BF16).rearrange("p (c t) -> p c t", c=8)
        sqk = lambda c: ("U3", c // 2)
        adt_rep = U3[:, :, :].rearrange("p a (b l) -> p (a b) l", b=4)
        dtr = AH.f32([4, 16])
        dt = AH.f32([4, 16])
        adt = AH.f32([4, 16])
        cs_sb = AH.f32([4, 16])
        ecl = AH.f32([4, 16])
        d1 = AH.f32([4, 16])
        wst = AH.f32([4, 16])
        Gm = AH.f32([2, 128])
        arg = [AH.f32([4, 128]) for _ in range(2)]
        dec = [AH.f32([4, 128]) for _ in range(2)]
        Mb = [AH.bf16([4, 128]) for _ in range(2)]
        ecs = [AH.f32([4, 128]) for _ in range(2)]
        Cp = [AH.bf16([4, 128]) for _ in range(2)]
        rstd = AH.f32([TT])
        gbuf = [AH.f32([TT]) for _ in range(2)]
        ttmp = AH.f32([TT])
        tri_f = self.cst[:, 0:128]
        ident_f = self.cst[:, 128:256]

        S.add(DVE, lambda e: e.memset(halo[:, :, :], 0.0), writes=[("halo", c) for c in range(12)])
        S.add(DVE, lambda e: e.memset(prevT[:, :], 0.0), writes=[("prevT", 0), ("prevT", 1)])
        S.add(DVE, lambda e: e.memset(prevTb[:, :], 0.0), writes=[("prevTb", 0), ("prevTb", 1)])

        for t in range(NT):
            t0 = t * TT
            S.add(SP, lambda e, t=t: e.dma_start(out=xnt[:, :, :], in_=xnh[:, :, t * TT:(t + 1) * TT]),
                  reads=[("xnh", t)], writes=[("xnt", c) for c in range(8)], dma=True)
            S.add(SP, lambda e, t=t: e.dma_start(out=mgt[:, :, :], in_=mgh[:, :, t * TT:(t + 1) * TT]),
                  reads=[("mgh", t)], writes=[("mgt", c) for c in range(8)], dma=True)
            xacts = [(xnt[:, kc, :], ("xnt", kc)) for kc in range(8)]
            for c in range(8):
                wb, wk = self.wload(W["w_z"][c], 1024)
                ps, pk = self.bank()
                self.lin(ps[:, :], pk, wb, wk, 8, 128, xacts)
                S.add(ACT, lambda e, ps=ps, c=c: e.activation(out=BA[:, c, :], in_=ps[:, :], func=AF.Silu),
                      reads=[pk], writes=[("BA", c)])
            for gq in range(3):
                for i in range(4):
                    c = gq * 4 + i
                    wb, wk = self.wload(W["w_xbc"][c], 1024)
                    ps, pk = self.bank()
                    self.lin(ps[:, :], pk, wb, wk, 8, 128, xacts)
                    S.add(DVE, lambda e, i=i, c=c: e.tensor_copy(out=U4[:, i, 0:3], in_=halo[:, c, 0:3]),
                          reads=[("halo", c)], writes=[("U4", i)])
                    S.add(ACT, lambda e, ps=ps, i=i: e.activation(out=U4[:, i, 3:515], in_=ps[:, :], func=AF.Copy),
                          reads=[pk], writes=[("U4", i)])
                for i in range(4):
                    c = gq * 4 + i
                    S.add(DVE, lambda e, i=i, c=c: e.tensor_scalar(
                        out=U3[:, i, :], in0=U4[:, i, 0:512], scalar1=self.v("conv_w", c * 4 + 0),
                        scalar2=self.v("conv_b", c), op0=ALU.mult, op1=ALU.add),
                        reads=[("U4", i), "vecs"], writes=[("U3", i)])
                for k in range(1, 4):
                    for i in range(4):
                        c = gq * 4 + i
                        S.add(DVE, lambda e, i=i, c=c, k=k: e.scalar_tensor_tensor(
                            out=U3[:, i, :], in0=U4[:, i, k:k + 512], scalar=self.v("conv_w", c * 4 + k),
                            in1=U3[:, i, :], op0=ALU.mult, op1=ALU.add),
                            reads=[("U4", i), ("U3", i), "vecs"], writes=[("U3", i)])
                for i in range(4):
                    c = gq * 4 + i
                    S.add(DVE, lambda e, i=i, c=c: e.tensor_copy(out=halo[:, c, 0:3], in_=U4[:, i, 512:515]),
                          reads=[("U4", i)], writes=[("halo", c)])
                for i in range(4):
                    c = gq * 4 + i
                    if c < 8:
                        dst, dk = BB[:, c, :], ("BB", c)
                    elif c < 10:
                        dst, dk = BT[:, c - 8, :], ("BT", c - 8)
                    else:
                        dst, dk = CT[:, c - 10, :], ("CT", c - 10)
                    S.add(ACT, lambda e, i=i, dst=dst: e.activation(out=dst, in_=U3[:, i, :], func=AF.Silu),
                          reads=[("U3", i)], writes=[dk])
            wdt, wdtk = self.wload(W["w_dt"][0], 128)
            ps, pk = self.bank()
            for j in range(4):
                self.mm(ps[:, j * 16:(j + 1) * 16], pk,
                        [(xnt[:, kc, j * 128:(j + 1) * 128], wdt[:, kc * 16:(kc + 1) * 16], [wdtk, ("xnt", kc)])
                         for kc in range(8)])
            S.add(DVE, lambda e, ps=ps: e.tensor_tensor(
                out=dtr[:, :, :], in0=ps[:, 0:64].rearrange("p (j h) -> p j h", j=4),
                in1=self.v("dt_bias", 0, 16).unsqueeze(1).to_broadcast([128, 4, 16]), op=ALU.add),
                reads=[pk, "vecs"], writes=["dtr"])
            S.add(ACT, lambda e: e.activation(out=dtr[:, :, :], in_=dtr[:, :, :], func=AF.Exp), reads=["dtr"], writes=["dtr"])
            S.add(ACT, lambda e: e.activation(out=dt[:, :, :], in_=dtr[:, :, :], func=AF.Ln, bias=self.ones_f[:, 0:1]),
                  reads=["dtr", "const"], writes=["dt"])
            S.add(DVE, lambda e: e.tensor_tensor(out=adt[:, :, :], in0=dt[:, :, :],
                                                 in1=self.a_rep[:, :].unsqueeze(1).to_broadcast([128, 4, 16]), op=ALU.mult),
                  reads=["dt", "a_rep"], writes=["adt"])
            for j in range(4):
                ps, pk = self.bank()
                self.mm(ps[:, 0:16], pk, [(tri_f, adt[:, j, :], ["adt", "const"])])
                self.mm(ps[:, 16:32], pk, [(self.ones_f[:, :], adt[:, j, :], ["adt", "const"])])
                S.add(DVE, lambda e, ps=ps, j=j: e.tensor_copy(out=cs_sb[:, j, :], in_=ps[:, 0:16]), reads=[pk], writes=[("cs", j)])
                S.add(ACT, lambda e, ps=ps, j=j: e.activation(out=ecl[:, j, :], in_=ps[:, 16:32], func=AF.Exp),
                      reads=[pk], writes=[("ecl", j)])
                S.add(DVE, lambda e, ps=ps, j=j: e.tensor_tensor(out=d1[:, j, :], in0=ps[:, 16:32], in1=cs_sb[:, j, :],
                                                               op=ALU.subtract),
                      reads=[pk, ("cs", j)], writes=[("d1", j)])
                S.add(ACT, lambda e, j=j: e.activation(out=d1[:, j, :], in_=d1[:, j, :], func=AF.Exp),
                      reads=[("d1", j)], writes=[("d1", j)])
                S.add(DVE, lambda e, j=j: e.tensor_tensor(out=wst[:, j, :], in0=d1[:, j, :], in1=dt[:, j, :], op=ALU.mult),
                      reads=[("d1", j), "dt"], writes=[("wst", j)])
            for j in range(4):
                for half in range(2):
                    ps, pk = self.bank()
                    for i in range(4):
                        c = half * 4 + i
                        S.add(PE, lambda e, ps=ps, i=i, c=c, j=j: e.transpose(
                            out=ps[:, i * 128:(i + 1) * 128], in_=BB[:, c, j * 128:(j + 1) * 128], identity=ident_f),
                            reads=[("BB", c), "const"], writes=[pk])
                    psv = ps[:, :].rearrange("p (h d) -> p h d", h=8)
                    S.add(DVE, lambda e, psv=psv, j=j, half=half: e.tensor_tensor(
                        out=U1[:, j, half * 512:(half + 1) * 512].rearrange("p (h d) -> p h d", h=8), in0=psv,
                        in1=dt[:, j, half * 8:(half + 1) * 8].unsqueeze(2).to_broadcast([128, 8, 64]), op=ALU.mult),
                        reads=[pk, "dt"], writes=[("U1", j)])
                    S.add(DVE, lambda e, psv=psv, j=j, half=half: e.tensor_tensor(
                        out=U2[:, j, half * 512:(half + 1) * 512].rearrange("p (h d) -> p h d", h=8), in0=psv,
                        in1=wst[:, j, half * 8:(half + 1) * 8].unsqueeze(2).to_broadcast([128, 8, 64]), op=ALU.mult),
                        reads=[pk, ("wst", j)], writes=[("U2", j)])
            ps, pk = self.bank()
            psb = ps[:, :].bitcast(BF16)
            for j in range(4):
                for g in range(2):
                    S.add(PE, lambda e, psb=psb, j=j, g=g: e.transpose(
                        out=psb[:, (j * 2 + g) * 128:(j * 2 + g + 1) * 128], in_=BT[:, g, j * 128:(j + 1) * 128],
                        identity=self.ident_bf[:, :]),
                        reads=[("BT", g), "const"], writes=[pk])
            S.add(ACT, lambda e, psb=psb: e.activation(out=Btok[:, :, :, :].rearrange("p j g n -> p (j g n)"),
                                                       in_=psb[:, 0:1024], func=AF.Copy),
                  reads=[pk], writes=["Btok"])
            for j in range(4):
                cols = slice(j * 128, (j + 1) * 128)
                ps, pk = self.bank()
                for g in range(2):
                    self.mm(ps[:, g * 128:(g + 1) * 128], pk, [(BT[:, g, cols], CT[:, g, cols], [("BT", g), ("CT", g)])])
                S.add(DVE, lambda e, ps=ps: e.tensor_tensor(
                    out=Gm[:, :, :], in0=ps[:, 0:256].rearrange("p (g l) -> p g l", g=2),
                    in1=tri_f.unsqueeze(1).to_broadcast([128, 2, 128]), op=ALU.mult),
                    reads=[pk, "const"], writes=["Gm"])
                S.add(DVE, lambda e, j=j: e.tensor_copy(out=adt_rep, in_=adt[:, j, :].unsqueeze(2).to_broadcast([128, 16, 128])),
                      reads=["adt"], writes=[("U3", i) for i in range(4)])
                for hq in range(4):
                    h0 = hq * 4
                    g = hq // 2
                    b = hq % 2
                    pc, pck = self.bank()
                    for i in range(4):
                        self.mm(pc[:, i * 128:(i + 1) * 128], pck,
                                [(adt_rep[:, h0 + i, :], tri_f, [("U3", (h0 + i) // 4), "const"])])
                    pcv = pc[:, :].rearrange("p (i l) -> p i l", i=4)
                    S.add(DVE, lambda e, pcv=pcv, b=b, j=j, h0=h0: e.tensor_tensor(
                        out=arg[b][:, :, :], in0=pcv,
                        in1=cs_sb[:, j, h0:h0 + 4].unsqueeze(2).to_broadcast([128, 4, 128]), op=ALU.subtract),
                        reads=[pck, ("cs", j)], writes=[("arg", b)])
                    S.add(DVE, lambda e, b=b: e.tensor_tensor(
                        out=arg[b][:, :, :], in0=arg[b][:, :, :],
                        in1=tri_f.unsqueeze(1).to_broadcast([128, 4, 128]), op=ALU.mult),
                        reads=[("arg", b), "const"], writes=[("arg", b)])
                    S.add(ACT, lambda e, b=b: e.activation(out=dec[b][:, :, :], in_=arg[b][:, :, :], func=AF.Exp),
                          reads=[("arg", b)], writes=[("dec", b)])
                    S.add(DVE, lambda e, b=b, g=g: e.tensor_tensor(
                        out=Mb[b][:, :, :], in0=dec[b][:, :, :],
                        in1=Gm[:, g, :].unsqueeze(1).to_broadcast([128, 4, 128]), op=ALU.mult),
                        reads=[("dec", b), "Gm"], writes=[("Mb", b)])
                    S.add(ACT, lambda e, b=b, pcv=pcv: e.activation(out=ecs[b][:, :, :], in_=pcv, func=AF.Exp),
                          reads=[pck], writes=[("ecs", b)])
                    S.add(DVE, lambda e, b=b, g=g, cols=cols: e.tensor_tensor(
                        out=Cp[b][:, :, :], in0=ecs[b][:, :, :],
                        in1=CT[:, g, cols].unsqueeze(1).to_broadcast([128, 4, 128]), op=ALU.mult),
                        reads=[("ecs", b), ("CT", g)], writes=[("Cp", b)])
                    yb, ybk = self.held(5 + hq // 2)
                    for i in range(4):
                        h = h0 + i
                        hb = (h % 2) * 64
                        pl = (h // 2) % 4
                        self.mm(yb[hb:hb + 64, pl * 128:(pl + 1) * 128], ybk,
                                [(U1[:, j, h * 64:(h + 1) * 64], Mb[b][:, i, :], [("U1", j), ("Mb", b)]),
                                 (prevTb[:, h * 64:(h + 1) * 64], Cp[b][:, i, :], [("prevTb", g), ("Cp", b)])])
                    if hq % 2 == 1:
                        for pl in range(4):
                            c = (hq // 2) * 4 + pl
                            S.add(DVE, lambda e, yb=yb, pl=pl, c=c, cols=cols: e.scalar_tensor_tensor(
                                out=BC[:, c, cols], in0=BB[:, c, cols], scalar=self.v("d_skip", c),
                                in1=yb[:, pl * 128:(pl + 1) * 128], op0=ALU.mult, op1=ALU.add),
                                reads=[ybk, ("BB", c), "vecs"], writes=[("BC", c)])
                for g in range(2):
                    ps, pk = self.bank()
                    self.mm(ps[:, :], pk, [(Btok[:, j, g, :], U2[:, j, g * 512:(g + 1) * 512], ["Btok", ("U2", j)])])
                    pv = prevT[:, g * 512:(g + 1) * 512]
                    S.add(DVE, lambda e, pv=pv, j=j, g=g: e.tensor_tensor(
                        out=pv.rearrange("p (h d) -> p h d", h=8), in0=pv.rearrange("p (h d) -> p h d", h=8),
                        in1=ecl[:, j, g * 8:(g + 1) * 8].unsqueeze(2).to_broadcast([128, 8, 64]), op=ALU.mult),
                        reads=[("prevT", g), ("ecl", j)], writes=[("prevT", g)])
                    S.add(DVE, lambda e, pv=pv, ps=ps: e.tensor_tensor(out=pv, in0=ps[:, :], in1=pv, op=ALU.add),
                          reads=[pk, ("prevT", g)], writes=[("prevT", g)])
                    S.add(ACT, lambda e, pv=pv, g=g: e.activation(out=prevTb[:, g * 512:(g + 1) * 512], in_=pv, func=AF.Copy),
                          reads=[("prevT", g)], writes=[("prevTb", g)])
            for c in range(8):
                S.add(DVE, lambda e, c=c: e.tensor_tensor(out=BC[:, c, :], in0=BC[:, c, :], in1=BA[:, c, :], op=ALU.mult),
                      reads=[("BC", c), ("BA", c)], writes=[("BC", c)])
            for g in range(2):
                self.rms_rstd([(BC[:, 4 * g + i, :], ("BC", 4 * g + i)) for i in range(4)], 512, sqv, sqk, rstd, "rstd")
                for i in range(4):
                    c = 4 * g + i
                    self.norm_apply(yn[:, c, :], ("U2", c // 2), BC[:, c, :], ("BC", c), self.v("ssd_norm_g", c), rstd, "rstd")
            yacts = [(yn[:, c, :], ("U2", c // 2)) for c in range(8)]
            for c in range(8):
                wb, wk = self.wload(W["w_ssd"][c], 1024)
                wg_, wgk = self.wload(W["w_gs"][c], 1024)
                ps, pk = self.bank()
                ps2, pk2 = self.bank()
                self.lin(ps2[:, :], pk2, wg_, wgk, 8, 128, xacts)
                self.lin(ps[:, :], pk, wb, wk, 8, 128, yacts)
                gb = gbuf[c % 2]
                gk = ("g", c % 2)
                S.add(ACT, lambda e, ps2=ps2, gb=gb, c=c: e.activation(out=gb, in_=ps2[:, :], func=AF.Sigmoid,
                                                                      bias=self.v("gb_ssd", c)),
                      reads=[pk2, "vecs"], writes=[gk])
                S.add(DVE, lambda e, ps=ps, gb=gb: e.tensor_tensor(out=ttmp, in0=ps[:, :], in1=gb, op=ALU.mult),
                      reads=[pk, gk], writes=["ttmp"])
                S.add(DVE, lambda e, c=c: e.tensor_tensor(out=mg[:, c, :], in0=ttmp, in1=mgt[:, c, :], op=ALU.add),
                      reads=["ttmp", ("mgt", c)], writes=[("U1", c // 2)])
            macts = [(mg[:, c, :], ("U1", c // 2)) for c in range(8)]
            for c in range(8):
                wb, wk = self.wload(W["w_out"][c], 1024)
                ps, pk = self.bank()
                self.lin(ps[:, :], pk, wb, wk, 8, 128, macts)
                S.add(ACT, lambda e, ps=ps, c=c: e.activation(out=BA[:, c, :], in_=ps[:, :], func=AF.Copy),
                      reads=[pk], writes=[("BA", c)])
            self.rms_rstd([(BA[:, c, :], ("BA", c)) for c in range(8)], D, sqv, sqk, rstd, "rstd")
            xspv = self.xsp.rearrange("(c p) t -> p c t", p=128)
            S.add(SP, lambda e, t0=t0: e.dma_start(out=BB[:, :, :], in_=xspv[:, :, t0:t0 + TT]),
                  reads=[("xsp", c, t) for c in range(8)], writes=[("BB", c) for c in range(8)], dma=True)
            for c in range(8):
                self.norm_apply(BA[:, c, :], ("BA", c), BA[:, c, :], ("BA", c), self.v("mix_post_g", c), rstd, "rstd")
                S.add(DVE, lambda e, c=c: e.tensor_tensor(out=BB[:, c, :], in0=BB[:, c, :], in1=BA[:, c, :], op=ALU.add),
                      reads=[("BB", c), ("BA", c)], writes=[("BB", c)])
            S.add(SP, lambda e, t0=t0: e.dma_start(out=xspv[:, :, t0:t0 + TT], in_=BB[:, :, :]),
                  reads=[("BB", c) for c in range(8)], writes=[("xsp", c, t) for c in range(8)], dma=True)
        AH.release()
        AX.release()

    def xattn(self, s):
        S, AH = self.S, self.AH
        x = self.x
        W = self.w
        NT = SEQ // TT
        AH.mark()
        memf = AH.f32([8, MEM])
        memn = AH.bf16([8, MEM])
        kx = AH.bf16([8, MEM])
        vx = AH.bf16([2, 1024])
        sq = AH.bf16([8, TT])
        rstd = AH.f32([TT])
        xnt = AH.bf16([8, TT])
        qx = AH.bf16([8, TT])
        ptx = [AH.bf16([TT]) for _ in range(4)]
        rdx = AH.f32([TT])
        ox = AH.bf16([8, TT])
        hx = AH.f32([8, TT])
        S.add(SP, lambda e: e.dma_start(out=memf[:, :, :], in_=self.memT[s].rearrange("(c p) m -> p c m", p=128)),
              writes=[("memf", c) for c in range(8)], dma=True)
        self.rms_rstd([(memf[:, c, :], ("memf", c)) for c in range(8)], D, sq[:, :, 0:MEM], "sq", rstd[:, 0:MEM], "rstd")
        for c in range(8):
            self.norm_apply(memn[:, c, :], ("memn", c), memf[:, c, :], ("memf", c), self.v("mem_norm_g", c), rstd[:, 0:MEM], "rstd")
        macts = [(memn[:, c, :], ("memn", c)) for c in range(8)]
        for c in range(8):
            wb, wk = self.wload(W["w_xk"][c], 1024)
            ps, pk = self.bank()
            self.lin(ps[:, 0:MEM], pk, wb, wk, 8, 128, macts)
            S.add(ACT, lambda e, ps=ps, c=c: e.activation(out=kx[:, c, :], in_=ps[:, 0:MEM], func=AF.Copy),
                  reads=[pk], writes=[("kx", c)])
        for half in range(2):
            wbs = [self.wload(W["w_xv"][half, kp], 1024) for kp in range(4)]
            for mb in range(2):
                ps, pk = self.bank()
                self.mm(ps[:, :], pk, [(memn[:, kc, mb * 128:(mb + 1) * 128],
                                        wbs[kc // 2][0][:, (kc % 2) * 512:(kc % 2 + 1) * 512],
                                        [wbs[kc // 2][1], ("memn", kc)]) for kc in range(8)])
                S.add(ACT, lambda e, ps=ps, mb=mb, half=half: e.activation(
                    out=vx[:, mb, half * 512:(half + 1) * 512], in_=ps[:, :], func=AF.Copy),
                    reads=[pk], writes=[("vx", mb, half)])
        pti = 0
        for t in range(NT):
            t0 = t * TT
            self.rms_rstd([(x[:, c, t0:t0 + TT], ("x", c, t)) for c in range(8)], D, sq, "sq", rstd, "rstd")
            for c in range(8):
                self.norm_apply(xnt[:, c, :], ("xnt", c), x[:, c, t0:t0 + TT], ("x", c, t), self.v("xa_pre_g", c), rstd, "rstd")
            xacts = [(xnt[:, c, :], ("xnt", c)) for c in range(8)]
            for c in range(8):
                wb, wk = self.wload(W["w_xq"][c], 1024)
                ps, pk = self.bank()
                self.lin(ps[:, :], pk, wb, wk, 8, 128, xacts)
                S.add(ACT, lambda e, ps=ps, c=c: e.activation(out=qx[:, c, :], in_=ps[:, :], func=AF.Copy),
                      reads=[pk], writes=[("qx", c)])
            for hh in range(4):
                pp = []
                for mb in range(2):
                    ps, pk = self.bank()
                    self.mm(ps[:, :], pk, [(kx[:, 2 * hh + kc, mb * 128:(mb + 1) * 128], qx[:, 2 * hh + kc, :],
                                            [("kx", 2 * hh + kc), ("qx", 2 * hh + kc)]) for kc in range(2)])
                    pt = ptx[pti % 4]
                    ptk = ("ptx", pti % 4)
                    pti += 1
                    S.add(ACT, lambda e, ps=ps, pt=pt: e.activation(out=pt, in_=ps[:, :], func=AF.Exp, scale=SCALE_XA),
                          reads=[pk], writes=[ptk])
                    pp.append((pt, ptk))
                pd, pdk = self.bank()
                self.mm(pd[:, :], pdk, [(self.ones_bf[:, :], pp[mb][0], [pp[mb][1], "const"]) for mb in range(2)])
                S.add(DVE, lambda e, pd=pd: e.reciprocal(out=rdx, in_=pd[:, :]), reads=[pdk], writes=["rdx"])
                for dvc in range(2):
                    c = 2 * hh + dvc
                    po, pok = self.bank()
                    self.mm(po[:, :], pok, [(vx[:, mb, c * 128:(c + 1) * 128], pp[mb][0],
                                             [("vx", mb, c // 4), pp[mb][1]]) for mb in range(2)])
                    S.add(DVE, lambda e, po=po, c=c: e.tensor_tensor(out=ox[:, c, :], in0=po[:, :], in1=rdx, op=ALU.mult),
                          reads=[pok, "rdx"], writes=[("ox", c)])
            oacts = [(ox[:, c, :], ("ox", c)) for c in range(8)]
            for c in range(8):
                wb, wk = self.wload(W["w_xo"][c], 1024)
                ps, pk = self.bank()
                self.lin(ps[:, :], pk, wb, wk, 8, 128, oacts)
                S.add(ACT, lambda e, ps=ps, c=c: e.activation(out=hx[:, c, :], in_=ps[:, :], func=AF.Copy),
                      reads=[pk], writes=[("hx", c)])
            self.rms_rstd([(hx[:, c, :], ("hx", c)) for c in range(8)], D, sq, "sq", rstd, "rstd")
            for c in range(8):
                self.norm_apply(hx[:, c, :], ("hx", c), hx[:, c, :], ("hx", c), self.v("xa_post_g", c), rstd, "rstd")
                S.add(DVE, lambda e, c=c, t0=t0: e.tensor_tensor(out=x[:, c, t0:t0 + TT], in0=x[:, c, t0:t0 + TT],
                                                                in1=hx[:, c, :], op=ALU.add),
                      reads=[("hx", c), ("x", c, t)], writes=[("x", c, t)])
        AH.release()
        S.barrier()

    def build(self):
        nc, S = self.nc, self.S
        nseq = self.nseq
        xT = self.inp("xT", [nseq, D, SEQ])
        self.memT = self.inp("memT", [nseq, D, MEM])
        self.pos_d = self.inp("pos", [nseq, SEQ], I32)
        outT = nc.dram_tensor("outT", [nseq, D, SEQ], F32, kind="ExternalOutput").ap()
        self.xsp = nc.dram_tensor("xsp", [D, SEQ], F32, kind="Internal").ap()
        self.xn_hbm = nc.dram_tensor("xn_hbm", [D, SEQ], BF16, kind="Internal").ap()
        self.mg_hbm = nc.dram_tensor("mg_hbm", [D, SEQ], BF16, kind="Internal").ap()
        vecs_d = self.inp("vecs", [128, NV])
        cst_d = self.inp("cst", [128, 256])
        w = {}
        for f in ("ffn1", "ffn2"):
            w[f + "_wg"] = self.inp(f + "_wg", [NFF, 128, 1024])
            w[f + "_wu"] = self.inp(f + "_wu", [NFF, 128, 1024])
            w[f + "_wd"] = self.inp(f + "_wd", [8, 2, 128, 1408])
        for n, shp in (("w_z", [8, 128, 1024]), ("w_xbc", [12, 128, 1024]), ("w_gs", [8, 128, 1024]), ("w_gm", [8, 128, 1024]),
                       ("w_qc", [3, 128, 1024]), ("w_kvc", [2, 128, 1024]), ("w_kra", [1, 128, 1024]), ("w_krb", [1, 128, 1024]),
                       ("w_dt", [1, 128, 128]), ("w_uqn", [8, 128, 384]), ("w_uqa", [4, 128, 384]), ("w_uqb", [4, 128, 384]),
                       ("w_uk", [8, 128, 256]), ("w_uv", [2, 128, 1024]), ("w_ssd", [8, 128, 1024]), ("w_mla", [8, 128, 1024]),
                       ("w_out", [8, 128, 1024]), ("w_xq", [8, 128, 1024]), ("w_xk", [8, 128, 1024]), ("w_xo", [8, 128, 1024]),
                       ("w_xv", [2, 4, 128, 1024])):
            w[n] = self.inp(n, shp)
        self.w = w

        with contextlib.ExitStack() as es:
            sb = lambda n, s, d: es.enter_context(nc.sbuf_tensor(n, s, d))
            self.vecs = sb("vecs_sb", [128, NV], F32)
            self.cst = sb("cst_sb", [128, 256], F32)
            self.ones_bf = sb("ones_bf", [128, 128], BF16)
            self.tri_bf = sb("tri_bf", [128, 128], BF16)
            self.ident_bf = sb("ident_bf", [128, 128], BF16)
            self.ones_f = sb("ones_f", [128, 128], F32)
            self.eps_ap = sb("eps", [128, 1], F32)
            self.a_rep = sb("a_rep", [128, 16], F32)
            self.NWB = 6
            self.wbufs = [sb("wb%d" % i, [128, WSLOT], BF16) for i in range(self.NWB)]
            XW = 8 * SEQ
            ARW = 44 * 1024
            arena = sb("arena", [128, ARW], F32)
            self.x = arena[:, 0:XW].rearrange("p (c t) -> p c t", c=8)
            self.AX = Arena(arena[:, 0:XW], XW)
            self.AH = Arena(arena[:, XW:ARW], ARW - XW)
            self.ps = [es.enter_context(nc.psum_tensor("ps%d" % i, [128, 512], F32)) for i in range(8)]

            S.add(DVE, lambda e: e.memset(self.ones_bf[:, :], 1.0), writes=["ones"])
            S.add(DVE, lambda e: e.memset(self.ones_f[:, :], 1.0), writes=["onesf"])
            S.add(DVE, lambda e: e.memset(self.eps_ap[:, :], EPS), writes=["eps"])
            S.add(SP, lambda e: e.dma_start(out=self.vecs[:, :], in_=vecs_d), writes=["vecs"], dma=True)
            S.add(SP, lambda e: e.dma_start(out=self.cst[:, :], in_=cst_d), writes=["cst"], dma=True)
            S.add(DVE, lambda e: e.tensor_copy(out=self.tri_bf[:, :], in_=self.cst[:, 0:128]), reads=["cst"], writes=["tribf"])
            S.add(DVE, lambda e: e.tensor_copy(out=self.ident_bf[:, :], in_=self.cst[:, 128:256]), reads=["cst"], writes=["idbf"])
            S.add(ACT, lambda e: e.activation(out=self.a_rep[:, :], in_=self.v("a_log", 0, 16), func=AF.Exp),
                  reads=["vecs"], writes=["a_rep"])
            S.add(DVE, lambda e: e.tensor_scalar(out=self.a_rep[:, :], in0=self.a_rep[:, :], scalar1=-1.0, scalar2=None,
                                                 op0=ALU.mult), reads=["a_rep"], writes=["a_rep"])
            S.barrier()
            outs = []
            for s in range(nseq):
                for c in range(8):
                    S.add(SP, lambda e, c=c, s=s: e.dma_start(out=self.x[:, c, :], in_=xT[s, c * 128:(c + 1) * 128, :]),
                          writes=[("x", c, t) for t in range(4)], dma=True)
                if self.stop_after >= 1:
                    self.ffn(w["ffn1_wg"], w["ffn1_wu"], w["ffn1_wd"], "ffn1_pre_g", "ffn1_post_g")
                if self.stop_after >= 2:
                    self.mixer(s)
                if self.stop_after >= 3:
                    self.xattn(s)
                if self.stop_after >= 4:
                    self.ffn(w["ffn2_wg"], w["ffn2_wu"], w["ffn2_wd"], "ffn2_pre_g", "ffn2_post_g")
                for c in range(8):
                    outs.append(S.add(SP, lambda e, c=c, s=s: e.dma_start(
                        out=outT[s, c * 128:(c + 1) * 128, :], in_=self.x[:, c, :]),
                        reads=[("x", c, t) for t in range(4)], dma=True))
                S.barrier()
            S.emit(outs)
        return nc


def prep_shared(inp):
    f = lambda n: np.asarray(inp[n][0], np.float32)
    sh = {}
    vec = np.zeros((128, NV), np.float32)

    def put(name, arr):
        vec[:, VO[name]:VO[name] + arr.shape[1]] = arr

    for n in ("ffn1_pre_g", "ffn1_post_g", "mix_pre_g", "mix_post_g", "xa_pre_g", "mem_norm_g", "xa_post_g",
              "ffn2_pre_g", "ffn2_post_g", "ssd_norm_g", "q_norm_g", "kv_norm_g", "conv_b"):
        put(n, colvec(f(n)))
    put("conv_w", np.transpose(f("conv_w").reshape(4, 12, 128), (2, 1, 0)).reshape(128, 48))
    gb = f("gate_bias")
    put("gb_ssd", colvec(gb[:1024]))
    put("gb_mla", colvec(gb[1024:]))
    put("d_skip", colvec(np.repeat(f("d_skip"), 64)))
    put("dt_bias", np.tile(f("dt_bias")[None, :], (128, 1)))
    put("a_log", np.tile(f("a_log")[None, :], (128, 1)))
    r = np.arange(128) % 32
    inv = (np.float32(10000.0) ** (-(np.arange(0, 32, 2, dtype=np.float32)) / np.float32(32))).astype(np.float32)
    put("invf", inv[r % 16][:, None])
    put("sgn", np.where(r < 16, -1.0, 1.0).astype(np.float32)[:, None])
    sh["vecs"] = vec
    k = np.arange(128)
    tri = (k[:, None] <= k[None, :]).astype(np.float32)
    sh["cst"] = np.ascontiguousarray(np.concatenate([tri, np.eye(128, dtype=np.float32)], axis=1))
    for p in ("ffn1", "ffn2"):
        sh[p + "_wg"] = blockify(f(p + "_w_gate"), 128).reshape(NFF, 128, 1024)
        sh[p + "_wu"] = blockify(f(p + "_w_up"), 128).reshape(NFF, 128, 1024)
        sh[p + "_wd"] = np.ascontiguousarray(
            blockify(f(p + "_w_down"), 128).reshape(8, 128, 2, 1408).transpose(0, 2, 1, 3))
    win = f("w_in")
    b128 = lambda m: blockify(np.ascontiguousarray(m), 128).reshape(m.shape[1] // 128, 128, -1)
    sh["w_z"] = b128(win[:, 0:1024])
    sh["w_xbc"] = b128(win[:, 1024:2560])
    sh["w_dt"] = blockify(np.ascontiguousarray(win[:, 2560:2576]), 16).reshape(1, 128, 128)
    sh["w_qc"] = b128(win[:, 2576:2960])
    sh["w_kvc"] = b128(win[:, 2960:3216])
    kr = win[:, 3216:3248]
    sh["w_kra"] = b128(np.tile(kr, (1, 4)))
    sh["w_krb"] = b128(np.tile(np.concatenate([kr[:, 16:32], kr[:, 0:16]], axis=1), (1, 4)))
    sh["w_gs"] = b128(win[:, 3248:4272])
    sh["w_gm"] = b128(win[:, 4272:5296])
    wuq = f("w_uq").reshape(384, 16, 96)
    sh["w_uqn"] = b128(wuq[:, :, 0:64].reshape(384, 1024))
    sh["w_uqa"] = b128(wuq[:, :, 64:96].reshape(384, 512))
    sh["w_uqb"] = b128(np.concatenate([wuq[:, :, 80:96], wuq[:, :, 64:80]], axis=2).reshape(384, 512))
    sh["w_uk"] = b128(f("w_uk"))
    sh["w_uv"] = blockify(f("w_uv"), 512).reshape(2, 128, 1024)
    for n, src in (("w_ssd", "w_ssd_proj"), ("w_mla", "w_mla_proj"), ("w_out", "w_out"), ("w_xq", "w_xq"),
                   ("w_xk", "w_xk"), ("w_xo", "w_xo")):
        sh[n] = b128(f(src))
    sh["w_xv"] = np.ascontiguousarray(blockify(f("w_xv"), 512).reshape(2, 128, 4, 1024).transpose(0, 2, 1, 3))
    return sh


def run(inp, nseq, cores, stop_after=99):
    k = K(nseq, stop_after)
    nc = k.build()
    sh = prep_shared(inp)
    maps = []
    for ci in range(cores):
        m = dict(sh)
        sl = slice(ci * nseq, (ci + 1) * nseq)
        m["xT"] = np.ascontiguousarray(np.transpose(np.asarray(inp["x"][sl], np.float32), (0, 2, 1)))
        m["memT"] = np.ascontiguousarray(np.transpose(np.asarray(inp["mem"][sl], np.float32), (0, 2, 1)))
        m["pos"] = np.ascontiguousarray(np.asarray(inp["positions"][sl], np.int32))
        maps.append({n: m[n] for n in k.din})
    res = run_bass_kernel_spmd(nc, maps, core_ids=list(range(cores)))
    outs = [np.transpose(r["outT"], (0, 2, 1)) for r in res.results]
    return np.ascontiguousarray(np.concatenate(outs, axis=0)).astype(np.float32)


def kernel(**inputs):
    inp = {k: np.asarray(v) for k, v in inputs.items()}
    return run(inp, 2, NCORES)
```

```python
import contextlib
import numpy as np
import concourse.bass as bass
import concourse.mybir as mybir
from concourse.bass_utils import run_bass_kernel_spmd

F32 = mybir.dt.float32
BF16 = mybir.dt.bfloat16
I32 = mybir.dt.int32
AF = mybir.ActivationFunctionType
ALU = mybir.AluOpType
PE, ACT, DVE, POOL, SP = "tensor", "scalar", "vector", "gpsimd", "sync"

D = 1024
SEQ = 2048
TT = 512
DFF = 2816
NFF = DFF // 128
MEM = 256
EPS = 1e-6
NCORES = 8


class Op:
    __slots__ = ("eng", "fn", "deps", "alldeps", "signal", "count", "dma", "dsem", "dval", "prev_dval",
                 "cost", "lat", "seq", "bar", "t_end", "done")

    def __init__(self, eng, fn, dma):
        self.eng = eng
        self.fn = fn
        self.deps = []
        self.alldeps = []
        self.signal = False
        self.count = 0
        self.dma = dma
        self.dsem = None
        self.dval = 0
        self.prev_dval = 0
        self.cost = 0.0
        self.lat = 0.0
        self.seq = 0
        self.bar = False
        self.t_end = 0.0
        self.done = False


DEF_COST = {PE: 0.22, ACT: 0.55, DVE: 0.6, POOL: 1.0, SP: 0.2}


class Sched:
    NPOOL = 12
    import os as _os
    WINDOW = int(_os.environ.get('SWIN', '48'))

    def __init__(self, nc):
        self.nc = nc
        self.ops = {PE: [], ACT: [], DVE: [], POOL: [], SP: []}
        self.lastw = {}
        self.readers = {}
        self.seq = 0
        self.seg_dmas = []
        import os
        self.reorder = os.environ.get('NOREORDER') is None

    def add(self, eng, fn, reads=(), writes=(), dma=False, cost=None):
        op = Op(eng, fn, dma)
        self.seq += 1
        op.seq = self.seq
        if dma:
            op.cost = 1.2 if eng == POOL else 0.15
            op.lat = 3.0 if cost is None else cost
        else:
            op.cost = DEF_COST[eng] if cost is None else cost
        deps = {}
        for k in reads:
            w = self.lastw.get(k)
            if w is not None:
                deps[id(w)] = w
            if isinstance(k, tuple) and k[0] == "ps":
                for r in self.readers.get(k, ()):
                    if r.eng != eng:
                        deps[id(r)] = r
        for k in writes:
            w = self.lastw.get(k)
            if w is not None:
                deps[id(w)] = w
            for r in self.readers.get(k, ()):
                deps[id(r)] = r
        for d in deps.values():
            if d is op:
                continue
            op.alldeps.append(d)
            if eng == PE and d.eng == PE and not d.dma and not dma:
                continue
            op.deps.append(d)
            if not d.dma:
                d.signal = True
        for k in reads:
            self.readers.setdefault(k, []).append(op)
        for k in writes:
            self.lastw[k] = op
            self.readers[k] = []
        if dma:
            self.seg_dmas.append(op)
        self.ops[eng].append(op)
        return op

    def barrier(self):
        lasts = []
        for e, lst in self.ops.items():
            for o in reversed(lst):
                if o.bar:
                    break
                if o.fn is not None and not o.dma:
                    lasts.append(o)
                    break
        lasts += self.seg_dmas
        self.seg_dmas = []
        for e in self.ops:
            op = Op(e, None, False)
            op.bar = True
            for d in lasts:
                op.deps.append(d)
                if not d.dma:
                    d.signal = True
            self.ops[e].append(op)
        self.lastw = {}
        self.readers = {}

    def schedule(self):
        import os
        self._seng = os.environ.get('SENG').split(',') if os.environ.get('SENG') else None
        engs = list(self.ops.keys())
        segs = {e: [] for e in engs}
        for e in engs:
            cur = []
            for o in self.ops[e]:
                if o.bar:
                    segs[e].append((cur, o))
                    cur = []
                else:
                    cur.append(o)
            segs[e].append((cur, None))
        nseg = len(segs[engs[0]])
        assert all(len(segs[e]) == nseg for e in engs)
        new = {e: [] for e in engs}
        for si in range(nseg):
            pend = {e: list(segs[e][si][0]) for e in engs}
            free = {e: 0.0 for e in engs}
            for e in engs:
                for o in pend[e]:
                    o.done = False
            inseg = set()
            for e in engs:
                for o in pend[e]:
                    inseg.add(id(o))
            total = sum(len(v) for v in pend.values())
            heads = {e: 0 for e in engs}
            while total:
                best = None
                for e in engs:
                    lst = pend[e]
                    h = heads[e]
                    while h < len(lst) and lst[h].done:
                        h += 1
                    heads[e] = h
                    cnt = 0
                    i = h
                    cand = None
                    win = self.WINDOW if (self._seng is None or e in self._seng) else 1
                    while i < len(lst) and cnt < win:
                        o = lst[i]
                        i += 1
                        if o.done:
                            continue
                        cnt += 1
                        ok = True
                        rdy = 0.0
                        for d in o.alldeps:
                            if id(d) in inseg:
                                if not d.done:
                                    ok = False
                                    break
                                t = d.t_end + (0.0 if d.eng == e and not d.dma else 0.12)
                                if t > rdy:
                                    rdy = t
                        if not ok:
                            continue
                        st = rdy if rdy > free[e] else free[e]
                        if cand is None or st < cand[0] - 1e-9:
                            cand = (st, o)
                        if st <= free[e] + 1e-9:
                            break
                    if cand is not None and (best is None or cand[0] < best[0] - 1e-9):
                        best = (cand[0], cand[1], e)
                assert best is not None, "scheduler stuck"
                st, o, e = best
                o.done = True
                free[e] = st + o.cost
                o.t_end = st + o.cost + o.lat
                new[e].append(o)
                total -= 1
            lasts = []
            for e in engs:
                for o in reversed(pend[e]):
                    pass
                seg_ops = [o for o in new[e] if id(o) in inseg]
                for o in reversed(seg_ops):
                    if o.fn is not None and not o.dma:
                        lasts.append(o)
                        o.signal = True
                        break
                lasts += [o for o in seg_ops if o.dma]
            for e in engs:
                b = segs[e][si][1]
                if b is not None:
                    b.deps = list(lasts)
                    new[e].append(b)
        self.ops = new

    def emit(self, final_waits=()):
        nc = self.nc
        if self.reorder:
            self.schedule()
        for e, lst in self.ops.items():
            rr = 0
            cnt = {}
            for o in lst:
                if o.dma:
                    slot = (e, rr % self.NPOOL)
                    rr += 1
                    o.dsem = slot
                    o.prev_dval = cnt.get(slot, 0)
                    o.dval = o.prev_dval + 16
                    cnt[slot] = o.dval
        with contextlib.ExitStack() as es:
            engsem = {e: es.enter_context(nc.semaphore("s_" + e)) for e in (PE, ACT, DVE, POOL)}
            dsems = {}
            for e in self.ops:
                if any(o.dma for o in self.ops[e]):
                    for i in range(self.NPOOL):
                        dsems[(e, i)] = es.enter_context(nc.semaphore("d_%s_%d" % (e, i)))
            for e, lst in self.ops.items():
                c = 0
                for o in lst:
                    if o.signal and not o.dma:
                        c += 1
                        o.count = c
            block = es.enter_context(nc.Block())

            def run(e, lst, extra):
                def body(eng):
                    waited = {}

                    def need(sem, val):
                        if waited.get(id(sem), 0) >= val:
                            return
                        waited[id(sem)] = val
                        eng.wait_ge(sem, val)

                    for o in lst:
                        for d in o.deps:
                            if d.dma:
                                need(dsems[d.dsem], d.dval)
                            else:
                                need(engsem[d.eng], d.count)
                        if o.dma and o.prev_dval:
                            need(dsems[o.dsem], o.prev_dval)
                        if o.fn is None:
                            continue
                        inst = o.fn(eng)
                        if o.dma:
                            inst.then_inc(dsems[o.dsem], 16)
                        elif o.signal:
                            inst.then_inc(engsem[e], 1)
                    for d in extra:
                        need(dsems[d.dsem], d.dval)
                return body

            for e in (PE, ACT, DVE, POOL):
                getattr(block, e)(run(e, self.ops[e], ()))
            getattr(block, SP)(run(SP, self.ops[SP], list(final_waits)))


class Arena:
    def __init__(self, ap, nwords):
        self.ap = ap
        self.n = nwords
        self.top = 0
        self.stack = []
        self.peak = 0

    def mark(self):
        self.stack.append(self.top)

    def release(self):
        self.top = self.stack.pop()

    def _take(self, words):
        a = self.top
        self.top += words
        self.peak = max(self.peak, self.top)
        assert self.top <= self.n, ("arena overflow", self.top, self.n)
        return self.ap[:, a:a + words]

    def f32(self, shape):
        n = int(np.prod(shape))
        v = self._take(n)
        if len(shape) == 2:
            v = v.rearrange("p (a b) -> p a b", a=shape[0])
        elif len(shape) == 3:
            v = v.rearrange("p (a b c) -> p a b c", a=shape[0], b=shape[1])
        return v

    def bf16(self, shape):
        n = int(np.prod(shape))
        assert n % 2 == 0
        v = self._take(n // 2).bitcast(BF16)
        if len(shape) == 2:
            v = v.rearrange("p (a b) -> p a b", a=shape[0])
        elif len(shape) == 3:
            v = v.rearrange("p (a b c) -> p a b c", a=shape[0], b=shape[1])
        return v


def blockify(w, mb):
    k, m = w.shape
    assert k % 128 == 0 and m % mb == 0
    return np.ascontiguousarray(w.reshape(k // 128, 128, m // mb, mb).transpose(2, 1, 0, 3))


def colvec(v):
    return np.ascontiguousarray(v.reshape(-1, 128).T)


VEC_SPEC = [("ffn1_pre_g", 8), ("ffn1_post_g", 8), ("mix_pre_g", 8), ("mix_post_g", 8), ("xa_pre_g", 8),
            ("mem_norm_g", 8), ("xa_post_g", 8), ("ffn2_pre_g", 8), ("ffn2_post_g", 8),
            ("ssd_norm_g", 8), ("q_norm_g", 3), ("kv_norm_g", 2), ("conv_w", 48), ("conv_b", 12),
            ("gb_ssd", 8), ("gb_mla", 8), ("d_skip", 8), ("dt_bias", 16), ("a_log", 16),
            ("invf", 1), ("sgn", 1)]
VO = {}
_o = 0
for _n, _c in VEC_SPEC:
    VO[_n] = _o
    _o += _c
NV = _o
WSLOT = 1408
PI = float(np.pi)
SCALE_MLA = 96.0 ** -0.5
SCALE_XA = 256.0 ** -0.5


class K:
    def __init__(self, nseq, stop_after=99):
        self.nseq = nseq
        self.stop_after = stop_after
        self.nc = bass.Bass("TRN2", target_bir_lowering=False)
        self.S = Sched(self.nc)
        self.din = {}
        self.bank_i = 0
        self.wb_i = 0

    def inp(self, name, shape, dtype=F32):
        t = self.nc.dram_tensor(name, list(shape), dtype, kind="ExternalInput").ap()
        self.din[name] = t
        return t

    def bank(self):
        i = self.bank_i % 4
        self.bank_i += 1
        return self.ps[i], ("ps", i)

    def held(self, i):
        return self.ps[i], ("ps", i)

    def v(self, name, c=0, n=1):
        o = VO[name] + c
        return self.vecs[:, o:o + n]

    def wload(self, src, nelem):
        i = self.wb_i % self.NWB
        self.wb_i += 1
        key = ("wb", i)
        dst = self.wbufs[i][:, 0:nelem]
        self.S.add(POOL, lambda e: e.dma_start(out=dst, in_=src, max_dma_last_dim=4096),
                   writes=[key], dma=True)
        return dst, key

    def wload_multi(self, src, nb, n):
        i = self.wb_i % self.NWB
        self.wb_i += 1
        key = ("wb", i)
        dst = self.wbufs[i][:, 0:nb * n].rearrange("p (b n) -> p b n", b=nb)
        self.S.add(POOL, lambda e: e.dma_start(out=dst, in_=src.rearrange("b p n -> p b n"), max_dma_last_dim=4096),
                   writes=[key], dma=True)
        return dst, key

    def mm(self, out, pk, pairs):
        n = len(pairs)
        for i, (l, r, ks) in enumerate(pairs):
            self.S.add(PE, lambda e, l=l, r=r, i=i: e.matmul(out, lhsT=l, rhs=r, start=(i == 0), stop=(i == n - 1)),
                       reads=ks, writes=[pk])

    def lin(self, out, pk, wb, wk, kcn, mb, acts, m0=0, m=128):
        self.mm(out, pk, [(wb[:, kc * mb + m0:kc * mb + m0 + m], acts[kc][0], [wk, acts[kc][1]]) for kc in range(kcn)])

    def rms_rstd(self, srcs, nfeat, sq, sqname, rstd, rkey):
        S = self.S
        fc = len(srcs)
        T = rstd.shape[-1]
        sk = sqname if callable(sqname) else (lambda c: (sqname, c))
        for c, (ap, k) in enumerate(srcs):
            S.add(ACT, lambda e, c=c, ap=ap: e.activation(out=sq[:, c, :], in_=ap, func=AF.Square),
                  reads=[k], writes=[sk(c)])
        ps, pk = self.bank()
        self.mm(ps[:, 0:T], pk, [(self.ones_bf[:, :], sq[:, c, :], [sk(c), "const"]) for c in range(fc)])
        S.add(ACT, lambda e: e.activation(out=rstd, in_=ps[:, 0:T], func=AF.Sqrt, scale=1.0 / nfeat,
                                          bias=self.eps_ap[:, 0:1]),
              reads=[pk, "const"], writes=[rkey])
        S.add(DVE, lambda e: e.reciprocal(out=rstd, in_=rstd), reads=[rkey], writes=[rkey])

    def norm_apply(self, out, okey, src, skey, gcol, rstd, rkey):
        self.S.add(DVE, lambda e: e.scalar_tensor_tensor(out=out, in0=src, scalar=gcol, in1=rstd,
                                                         op0=ALU.mult, op1=ALU.mult),
                   reads=[skey, rkey, "vecs"], writes=[okey])

    def ffn(self, wg, wu, wd, gpre, gpost):
        S, A = self.S, self.AH
        x = self.x
        HT = 1024
        NH = HT // TT
        for half in range(SEQ // HT):
            A.mark()
            xn = A.bf16([8, HT])
            h = A.bf16([NFF, HT])
            y = A.f32([8, HT])
            sq = A.bf16([8, TT])
            rstd = A.f32([TT])
            sg = [A.f32([TT]) for _ in range(2)]
            for tt in range(NH):
                t0 = half * HT + tt * TT
                gt = t0 // TT
                self.rms_rstd([(x[:, c, t0:t0 + TT], ("x", c, gt)) for c in range(8)], D, sq, "sq", rstd, "rstd")
                for c in range(8):
                    self.norm_apply(xn[:, c, tt * TT:(tt + 1) * TT], ("xn", c, tt), x[:, c, t0:t0 + TT], ("x", c, gt),
                                    self.v(gpre, c), rstd, "rstd")
            for fb in range(NFF):
                wgb, wgk = self.wload(wg[fb], 1024)
                wub, wuk = self.wload(wu[fb], 1024)
                for tt in range(NH):
                    acts = [(xn[:, kc, tt * TT:(tt + 1) * TT], ("xn", kc, tt)) for kc in range(8)]
                    pg, pgk = self.bank()
                    pu, puk = self.bank()
                    self.lin(pg[:, :], pgk, wgb, wgk, 8, 128, acts)
                    self.lin(pu[:, :], puk, wub, wuk, 8, 128, acts)
                    sgb = sg[(fb * NH + tt) % 2]
                    sgk = ("sg", (fb * NH + tt) % 2)
                    S.add(ACT, lambda e, pg=pg, sgb=sgb: e.activation(out=sgb, in_=pg[:, :], func=AF.Silu),
                          reads=[pgk], writes=[sgk])
                    S.add(DVE, lambda e, pu=pu, sgb=sgb, fb=fb, tt=tt: e.tensor_tensor(
                        out=h[:, fb, tt * TT:(tt + 1) * TT], in0=pu[:, :], in1=sgb, op=ALU.mult),
                        reads=[puk, sgk], writes=[("h", fb, tt)])
            HF = NFF // 2
            for dc in range(8):
                wd0, wdk0 = self.wload(wd[dc, 0], HF * 128)
                wd1, wdk1 = self.wload(wd[dc, 1], HF * 128)
                for tt in range(NH):
                    py, pyk = self.bank()
                    pairs = []
                    for fc in range(NFF):
                        wb, wk = (wd0, wdk0) if fc < HF else (wd1, wdk1)
                        f = fc % HF
                        pairs.append((wb[:, f * 128:(f + 1) * 128], h[:, fc, tt * TT:(tt + 1) * TT], [wk, ("h", fc, tt)]))
                    self.mm(py[:, :], pyk, pairs)
                    S.add(ACT, lambda e, py=py, dc=dc, tt=tt: e.activation(
                        out=y[:, dc, tt * TT:(tt + 1) * TT], in_=py[:, :], func=AF.Copy),
                        reads=[pyk], writes=[("y", dc, tt)])
            for tt in range(NH):
                t0 = half * HT + tt * TT
                gt = t0 // TT
                self.rms_rstd([(y[:, c, tt * TT:(tt + 1) * TT], ("y", c, tt)) for c in range(8)], D, sq, "sq", rstd, "rstd")
                for c in range(8):
                    ysl = y[:, c, tt * TT:(tt + 1) * TT]
                    self.norm_apply(ysl, ("y", c, tt), ysl, ("y", c, tt), self.v(gpost, c), rstd, "rstd")
                    S.add(DVE, lambda e, c=c, ysl=ysl, t0=t0: e.scalar_tensor_tensor(
                        out=x[:, c, t0:t0 + TT], in0=ysl, scalar=0.5, in1=x[:, c, t0:t0 + TT],
                        op0=ALU.mult, op1=ALU.add),
                        reads=[("y", c, tt), ("x", c, gt)], writes=[("x", c, gt)])
            A.release()
            S.barrier()

    def rope_tables(self, s, t0, bufs, cosb, sinb):
        S = self.S
        posi, posf, ang, kf, tmp = bufs
        S.add(SP, lambda e: e.dma_start(out=posi, in_=self.pos_d[s:s + 1, t0:t0 + TT].partition_broadcast(128)),
              writes=["posi"], dma=True)
        S.add(DVE, lambda e: e.tensor_copy(out=posf, in_=posi), reads=["posi"], writes=["posf"])
        S.add(DVE, lambda e: e.tensor_scalar(out=ang, in0=posf, scalar1=self.v("invf"), scalar2=None, op0=ALU.mult),
              reads=["posf", "vecs"], writes=["ang"])
        ki = posi
        S.add(DVE, lambda e: e.tensor_scalar(out=ki, in0=ang, scalar1=1.0 / (2 * PI), scalar2=None, op0=ALU.mult),
              reads=["ang", "posf"], writes=["posi"])
        S.add(DVE, lambda e: e.tensor_copy(out=kf, in_=ki), reads=["posi"], writes=["kf"])
        S.add(DVE, lambda e: e.scalar_tensor_tensor(out=ang, in0=kf, scalar=-2 * PI, in1=ang, op0=ALU.mult, op1=ALU.add),
              reads=["kf", "ang"], writes=["ang"])
        for which, shift, dst, dk in (("sin", 0.0, sinb, "sinb"), ("cos", PI / 2, cosb, "cosb")):
            y = kf
            S.add(DVE, lambda e, shift=shift: e.tensor_scalar(out=y, in0=ang, scalar1=shift, scalar2=None, op0=ALU.add),
                  reads=["ang"], writes=["kf"])
            S.add(DVE, lambda e: e.tensor_scalar(out=tmp, in0=y, scalar1=PI, scalar2=2 * PI, op0=ALU.is_gt, op1=ALU.mult),
                  reads=["kf"], writes=["ropetmp"])
            S.add(DVE, lambda e: e.tensor_tensor(out=y, in0=y, in1=tmp, op=ALU.subtract),
                  reads=["kf", "ropetmp"], writes=["kf"])
            S.add(DVE, lambda e: e.tensor_scalar(out=tmp, in0=y, scalar1=-PI, scalar2=2 * PI, op0=ALU.is_lt, op1=ALU.mult),
                  reads=["kf"], writes=["ropetmp"])
            S.add(DVE, lambda e: e.tensor_tensor(out=y, in0=y, in1=tmp, op=ALU.add),
                  reads=["kf", "ropetmp"], writes=["kf"])
            S.add(DVE, lambda e: e.tensor_scalar(out=y, in0=y, scalar1=PI, scalar2=-PI, op0=ALU.min, op1=ALU.max),
                  reads=["kf"], writes=["kf"])
            if which == "sin":
                S.add(ACT, lambda e, dst=dst: e.activation(out=dst, in_=y, func=AF.Sin, scale=self.v("sgn")),
                      reads=["kf", "vecs"], writes=[dk])
            else:
                S.add(ACT, lambda e, dst=dst: e.activation(out=dst, in_=y, func=AF.Sin),
                      reads=["kf"], writes=[dk])

    def rope_apply(self, out, okey, psA, pkA, psB, pkB, cosb, sinb, t1, t2):
        S = self.S
        S.add(DVE, lambda e: e.tensor_tensor(out=t1, in0=psA, in1=cosb, op=ALU.mult), reads=[pkA, "cosb"], writes=["posf"])
        S.add(DVE, lambda e: e.tensor_tensor(out=t2, in0=psB, in1=sinb, op=ALU.mult), reads=[pkB, "sinb"], writes=["ropetmp"])
        S.add(DVE, lambda e: e.tensor_tensor(out=out, in0=t1, in1=t2, op=ALU.add), reads=["posf", "ropetmp"], writes=[okey])

    def mixer(self, s):
        S, AH, AX = self.S, self.AH, self.AX
        x = self.x
        W = self.w
        NT = SEQ // TT
        AH.mark()
        xnt = AH.bf16([8, TT])
        mgt = AH.bf16([8, TT])
        xnh = self.xn_hbm.rearrange("(c p) t -> p c t", p=128)
        mgh = self.mg_hbm.rearrange("(c p) t -> p c t", p=128)
        XK = [("xnt", c) for c in range(8)]
        AH.mark()
        sq = AH.bf16([8, TT])
        rstd = AH.f32([TT])
        for t in range(NT):
            t0 = t * TT
            self.rms_rstd([(x[:, c, t0:t0 + TT], ("x", c, t)) for c in range(8)], D, sq, "sq", rstd, "rstd")
            for c in range(8):
                self.norm_apply(xnt[:, c, :], ("xnt", c), x[:, c, t0:t0 + TT], ("x", c, t),
                                self.v("mix_pre_g", c), rstd, "rstd")
            S.add(SP, lambda e, t0=t0: e.dma_start(out=xnh[:, :, t0:t0 + TT], in_=xnt[:, :, :]),
                  reads=XK, writes=[("xnh", t)], dma=True)
        for c in range(8):
            S.add(SP, lambda e, c=c: e.dma_start(out=self.xsp[c * 128:(c + 1) * 128, :], in_=x[:, c, :]),
                  reads=[("x", c, t) for t in range(NT)], writes=[("xsp", c, t) for t in range(NT)], dma=True)
        AH.release()
        S.barrier()

        def load_xnt(t):
            S.add(SP, lambda e, t=t: e.dma_start(out=xnt[:, :, :], in_=xnh[:, :, t * TT:(t + 1) * TT]),
                  reads=[("xnh", t)], writes=XK, dma=True)

        AH.mark()
        AX.mark()
        Kh = AX.bf16([16, SEQ])
        qn = AH.bf16([3, SEQ])
        V = AH.bf16([16, 16, 64])
        cosb = AH.f32([TT])
        sinb = AH.f32([TT])
        rbufs = (AH.f32([TT]).bitcast(I32), AH.f32([TT]), AH.f32([TT]), AH.f32([TT]), AH.f32([TT]))
        t1 = rbufs[1]
        t2 = rbufs[4]
        rt = [AH.bf16([TT]) for _ in range(2)]
        AH.mark()
        kvc = AH.f32([2, TT])
        qc = AH.f32([3, TT])
        sq = AH.bf16([3, TT])
        rstd = AH.f32([TT])
        kvn = AH.bf16([2, TT])
        for t in range(NT):
            t0 = t * TT
            load_xnt(t)
            acts = [(xnt[:, kc, :], ("xnt", kc)) for kc in range(8)]
            self.rope_tables(s, t0, rbufs, cosb, sinb)
            for c in range(2):
                wb, wk = self.wload(W["w_kvc"][c], 1024)
                ps, pk = self.bank()
                self.lin(ps[:, :], pk, wb, wk, 8, 128, acts)
                S.add(ACT, lambda e, ps=ps, c=c: e.activation(out=kvc[:, c, :], in_=ps[:, :], func=AF.Copy),
                      reads=[pk], writes=[("kvc", c)])
            self.rms_rstd([(kvc[:, c, :], ("kvc", c)) for c in range(2)], 256, sq, "sq", rstd, "rstd")
            for c in range(2):
                self.norm_apply(kvn[:, c, :], ("kvn", c), kvc[:, c, :], ("kvc", c), self.v("kv_norm_g", c), rstd, "rstd")
            kacts = [(kvn[:, kc, :], ("kvn", kc)) for kc in range(2)]
            for i in range(2):
                wb, wk = self.wload_multi(W["w_uk"][4 * i:4 * i + 4], 4, 256)
                for b in range(4):
                    c = 4 * i + b
                    ps, pk = self.bank()
                    self.mm(ps[:, :], pk, [(wb[:, b, kc * 128:(kc + 1) * 128], kacts[kc][0], [wk, kacts[kc][1]]) for kc in range(2)])
                    S.add(ACT, lambda e, ps=ps, c=c, t0=t0: e.activation(out=Kh[0:64, 2 * c, t0:t0 + TT], in_=ps[0:64, :], func=AF.Copy),
                          reads=[pk], writes=[("Kn", 2 * c, t)])
                    S.add(DVE, lambda e, ps=ps, c=c, t0=t0: e.tensor_copy(out=Kh[0:64, 2 * c + 1, t0:t0 + TT], in_=ps[64:128, :]),
                          reads=[pk], writes=[("Kn", 2 * c + 1, t)])
            for half in range(2):
                wb, wk = self.wload(W["w_uv"][half], 1024)
                for j in range(4):
                    ps, pk = self.bank()
                    self.mm(ps[:, :], pk, [(kvn[:, kc, j * 128:(j + 1) * 128], wb[:, kc * 512:(kc + 1) * 512], [wk, ("kvn", kc)]) for kc in range(2)])
                    blk = t * 4 + j
                    S.add(DVE, lambda e, ps=ps, blk=blk, half=half: e.tensor_copy(
                        out=V[:, blk, half * 8:(half + 1) * 8, :], in_=ps[:, :].rearrange("p (h d) -> p h d", h=8)),
                        reads=[pk], writes=[("V", blk)])
            wa, wak = self.wload(W["w_kra"][0], 1024)
            wbb, wbk = self.wload(W["w_krb"][0], 1024)
            psA, pkA = self.bank()
            psB, pkB = self.bank()
            self.lin(psA[:, :], pkA, wa, wak, 8, 128, acts)
            self.lin(psB[:, :], pkB, wbb, wbk, 8, 128, acts)
            self.rope_apply(rt[0], ("rt", 0), psA[:, :], pkA, psB[:, :], pkB, cosb, sinb, t1, t2)
            S.add(POOL, lambda e, t0=t0: e.tensor_copy(out=Kh[64:96, :, t0:t0 + TT],
                                                      in_=rt[0][64:96, :].unsqueeze(1).to_broadcast([32, 16, TT])),
                  reads=[("rt", 0)], writes=[("Kr", t)])
            for c in range(3):
                wb, wk = self.wload(W["w_qc"][c], 1024)
                ps, pk = self.bank()
                self.lin(ps[:, :], pk, wb, wk, 8, 128, acts)
                S.add(ACT, lambda e, ps=ps, c=c: e.activation(out=qc[:, c, :], in_=ps[:, :], func=AF.Copy),
                      reads=[pk], writes=[("qc", c)])
            self.rms_rstd([(qc[:, c, :], ("qc", c)) for c in range(3)], 384, sq, "sq", rstd, "rstd")
            for c in range(3):
                self.norm_apply(qn[:, c, t0:t0 + TT], ("qn", c, t), qc[:, c, :], ("qc", c), self.v("q_norm_g", c), rstd, "rstd")
        AH.release()
        S.barrier()

        AH.mark()
        qh = AH.bf16([16, TT])
        pts = [AH.bf16([TT]) for _ in range(4)]
        o = AH.bf16([8, TT])
        rb = AH.f32([TT])
        gbuf = [AH.f32([TT]) for _ in range(2)]
        pti = 0
        for qt in range(NT):
            t0 = qt * TT
            self.rope_tables(s, t0, rbufs, cosb, sinb)
            qacts = [(qn[:, kc, t0:t0 + TT], ("qn", kc, qt)) for kc in range(3)]
            for (b0, nb) in ((0, 3), (3, 3), (6, 2)):
                wb, wk = self.wload_multi(W["w_uqn"][b0:b0 + nb], nb, 384)
                for b in range(nb):
                    c = b0 + b
                    ps, pk = self.bank()
                    self.mm(ps[:, :], pk, [(wb[:, b, kc * 128:(kc + 1) * 128], qacts[kc][0], [wk, qacts[kc][1]]) for kc in range(3)])
                    S.add(ACT, lambda e, ps=ps, c=c: e.activation(out=qh[0:64, 2 * c, :], in_=ps[0:64, :], func=AF.Copy),
                          reads=[pk], writes=[("qhn", 2 * c)])
                    S.add(DVE, lambda e, ps=ps, c=c: e.tensor_copy(out=qh[0:64, 2 * c + 1, :], in_=ps[64:128, :]),
                          reads=[pk], writes=[("qhn", 2 * c + 1)])
            for i in range(2):
                wa, wak = self.wload_multi(W["w_uqa"][2 * i:2 * i + 2], 2, 384)
                wbb, wbk = self.wload_multi(W["w_uqb"][2 * i:2 * i + 2], 2, 384)
                for b in range(2):
                    j = 2 * i + b
                    psA, pkA = self.bank()
                    psB, pkB = self.bank()
                    self.mm(psA[:, :], pkA, [(wa[:, b, kc * 128:(kc + 1) * 128], qacts[kc][0], [wak, qacts[kc][1]]) for kc in range(3)])
                    self.mm(psB[:, :], pkB, [(wbb[:, b, kc * 128:(kc + 1) * 128], qacts[kc][0], [wbk, qacts[kc][1]]) for kc in range(3)])
                    rtb = rt[j % 2]
                    self.rope_apply(rtb, ("rt", j % 2), psA[:, :], pkA, psB[:, :], pkB, cosb, sinb, t1, t2)
                    for ii in range(4):
                        S.add(POOL, lambda e, rtb=rtb, ii=ii, j=j: e.tensor_copy(out=qh[64:96, 4 * j + ii, :],
                                                                                in_=rtb[32 * ii:32 * ii + 32, :]),
                              reads=[("rt", j % 2)], writes=[("qhr", 4 * j + ii)])
            nkb = 4 * qt + 4
            blocks = [(h, kb) for h in range(16) for kb in range(nkb)]
            info = {}

            def s_ops(h, kb):
                nonlocal pti
                jj = kb - 4 * qt
                c0 = 0 if jj < 0 else jj * 128
                kt = kb // 4
                ps, pk = self.bank()
                S.add(PE, lambda e, ps=ps, h=h, kb=kb, c0=c0: e.matmul(
                    ps[:, c0:TT], lhsT=Kh[0:96, h, kb * 128:(kb + 1) * 128], rhs=qh[0:96, h, c0:TT], start=True, stop=True),
                    reads=[("Kn", h, kt), ("Kr", kt), ("qhn", h), ("qhr", h)], writes=[pk])
                pt = pts[pti % 4]
                ptk = ("pt", pti % 4)
                pti += 1
                S.add(ACT, lambda e, ps=ps, pt=pt, c0=c0: e.activation(out=pt[:, c0:TT], in_=ps[:, c0:TT], func=AF.Exp,
                                                                     scale=SCALE_MLA),
                      reads=[pk], writes=[ptk])
                if jj >= 0:
                    S.add(DVE, lambda e, pt=pt, c0=c0: e.tensor_tensor(out=pt[:, c0:c0 + 128], in0=pt[:, c0:c0 + 128],
                                                                     in1=self.tri_bf[:, :], op=ALU.mult),
                          reads=[ptk, "const"], writes=[ptk])
                info[(h, kb)] = (pt, ptk, c0)

            def pv_ops(h, kb):
                pt, ptk, c0 = info.pop((h, kb))
                hb = (h % 2) * 64
                hc = h // 2
                po, pok = self.held(4 + h % 2)
                pd, pdk = self.held(6 + h % 2)
                S.add(PE, lambda e, po=po, pt=pt, kb=kb, hc=hc, c0=c0: e.matmul(
                    po[:, c0:TT], lhsT=V[:, kb, 2 * hc:2 * hc + 2, :].rearrange("p a d -> p (a d)"), rhs=pt[:, c0:TT],
                    start=(kb == 0), stop=(kb == nkb - 1)),
                    reads=[("V", kb), ptk], writes=[pok])
                S.add(PE, lambda e, pd=pd, pt=pt, kb=kb, c0=c0: e.matmul(
                    pd[:, c0:TT], lhsT=self.ones_bf[:, :], rhs=pt[:, c0:TT], start=(kb == 0), stop=(kb == nkb - 1)),
                    reads=["const", ptk], writes=[pdk])
                if kb == nkb - 1:
                    S.add(DVE, lambda e, pd=pd, hb=hb: e.reciprocal(out=rb[hb:hb + 64, :], in_=pd[hb:hb + 64, :]),
                          reads=[pdk], writes=[("rb", h % 2)])
                    S.add(DVE, lambda e, po=po, hb=hb, hc=hc: e.tensor_tensor(out=o[hb:hb + 64, hc, :], in0=po[hb:hb + 64, :],
                                                                            in1=rb[hb:hb + 64, :], op=ALU.mult),
                          reads=[pok, ("rb", h % 2)], writes=[("o", hc)])

            AHEAD = 2
            for i, (h, kb) in enumerate(blocks):
                if i == 0:
                    for a in range(min(AHEAD, len(blocks))):
                        s_ops(*blocks[a])
                pv_ops(h, kb)
                if i + AHEAD < len(blocks):
                    s_ops(*blocks[i + AHEAD])
            load_xnt(qt)
            xacts = [(xnt[:, kc, :], ("xnt", kc)) for kc in range(8)]
            oacts = [(o[:, c, :], ("o", c)) for c in range(8)]
            for c in range(8):
                wb, wk = self.wload(W["w_mla"][c], 1024)
                wg_, wgk = self.wload(W["w_gm"][c], 1024)
                ps, pk = self.bank()
                ps2, pk2 = self.bank()
                self.lin(ps2[:, :], pk2, wg_, wgk, 8, 128, xacts)
                self.lin(ps[:, :], pk, wb, wk, 8, 128, oacts)
                g = gbuf[c % 2]
                gk = ("g", c % 2)
                S.add(ACT, lambda e, ps2=ps2, g=g, c=c: e.activation(out=g, in_=ps2[:, :], func=AF.Sigmoid,
                                                                    bias=self.v("gb_mla", c)),
                      reads=[pk2, "vecs"], writes=[gk])
                S.add(DVE, lambda e, ps=ps, g=g, c=c: e.tensor_tensor(out=mgt[:, c, :], in0=ps[:, :], in1=g, op=ALU.mult),
                      reads=[pk, gk], writes=[("mgt", c)])
            S.add(SP, lambda e, t0=t0: e.dma_start(out=mgh[:, :, t0:t0 + TT], in_=mgt[:, :, :]),
                  reads=[("mgt", c) for c in range(8)], writes=[("mgh", qt)], dma=True)
        AH.release()
        AX.release()
        AH.release()
        S.barrier()

        self.ssd(s, xnt, mgt, xnh, mgh)
        AH.release()
        S.barrier()
        for c in range(8):
            S.add(SP, lambda e, c=c: e.dma_start(out=x[:, c, :], in_=self.xsp[c * 128:(c + 1) * 128, :]),
                  writes=[("x", c, t) for t in range(4)], dma=True)

    def ssd(self, s, xnt, mgt, xnh, mgh):
        S, AH, AX = self.S, self.AH, self.AX
        W = self.w
        NT = SEQ // TT
        AX.mark()
        AH.mark()
        halo = AX.f32([12, 4])
        prevT = AX.f32([1024])
        prevTb = AX.bf16([1024])
        BA = AX.f32([8, TT])
        BB = AX.f32([8, TT])
        BC = AX.f32([8, TT])
        BT = AX.bf16([2, TT])
        CT = AX.bf16([2, TT])
        Btok = AX.bf16([4, 2, 128])
        U1 = AH.bf16([4, 1024])
        U2 = AH.bf16([4, 1024])
        U3 = AH.f32([4, TT])
        U4 = AH.f32([4, 516])
        mg = U1[:, :, :].rearrange("p j (a t) -> p (j a) t", a=2)
        yn = U2[:, :, :].rearrange("p j (a t) -> p (j a) t", a=2)
        sqv = U3[:, :, :].rearrange("p a t -> p (a t)").bitcast(BF16).rearrange("p (c t) -> p c t", c=8)
        sqk = lambda c: ("U3", c // 2)
        adt_rep = U3[:, :, :].rearrange("p a (b l) -> p (a b) l", b=4)
        dtr = AH.f32([4, 16])
        dt = AH.f32([4, 16])
        adt = AH.f32([4, 16])
        cs_sb = AH.f32([4, 16])
        ecl = AH.f32([4, 16])
        d1 = AH.f32([4, 16])
        wst = AH.f32([4, 16])
        Gm = AH.f32([2, 128])
        arg = [AH.f32([4, 128]) for _ in range(2)]
        dec = [AH.f32([4, 128]) for _ in range(2)]
        Mb = [AH.bf16([4, 128]) for _ in range(2)]
        ecs = [AH.f32([4, 128]) for _ in range(2)]
        Cp = [AH.bf16([4, 128]) for _ in range(2)]
        rstd = AH.f32([TT])
        gbuf = [AH.f32([TT]) for _ in range(2)]
        ttmp = AH.f32([TT])
        tri_f = self.cst[:, 0:128]
        ident_f = self.cst[:, 128:256]

        S.add(DVE, lambda e: e.memset(halo[:, :, :], 0.0), writes=[("halo", c) for c in range(12)])
        S.add(DVE, lambda e: e.memset(prevT[:, :], 0.0), writes=[("prevT", 0), ("prevT", 1)])
        S.add(DVE, lambda e: e.memset(prevTb[:, :], 0.0), writes=[("prevTb", 0), ("prevTb", 1)])

        for t in range(NT):
            t0 = t * TT
            S.add(SP, lambda e, t=t: e.dma_start(out=xnt[:, :, :], in_=xnh[:, :, t * TT:(t + 1) * TT]),
                  reads=[("xnh", t)], writes=[("xnt", c) for c in range(8)], dma=True)
            S.add(SP, lambda e, t=t: e.dma_start(out=mgt[:, :, :], in_=mgh[:, :, t * TT:(t + 1) * TT]),
                  reads=[("mgh", t)], writes=[("mgt", c) for c in range(8)], dma=True)
            xacts = [(xnt[:, kc, :], ("xnt", kc)) for kc in range(8)]
            for c in range(8):
                wb, wk = self.wload(W["w_z"][c], 1024)
                ps, pk = self.bank()
                self.lin(ps[:, :], pk, wb, wk, 8, 128, xacts)
                S.add(ACT, lambda e, ps=ps, c=c: e.activation(out=BA[:, c, :], in_=ps[:, :], func=AF.Silu),
                      reads=[pk], writes=[("BA", c)])
            for gq in range(3):
                for i in range(4):
                    c = gq * 4 + i
                    wb, wk = self.wload(W["w_xbc"][c], 1024)
                    ps, pk = self.bank()
                    self.lin(ps[:, :], pk, wb, wk, 8, 128, xacts)
                    S.add(DVE, lambda e, i=i, c=c: e.tensor_copy(out=U4[:, i, 0:3], in_=halo[:, c, 0:3]),
                          reads=[("halo", c)], writes=[("U4", i)])
                    S.add(ACT, lambda e, ps=ps, i=i: e.activation(out=U4[:, i, 3:515], in_=ps[:, :], func=AF.Copy),
                          reads=[pk], writes=[("U4", i)])
                for i in range(4):
                    c = gq * 4 + i
                    S.add(DVE, lambda e, i=i, c=c: e.tensor_scalar(
                        out=U3[:, i, :], in0=U4[:, i, 0:512], scalar1=self.v("conv_w", c * 4 + 0),
                        scalar2=self.v("conv_b", c), op0=ALU.mult, op1=ALU.add),
                        reads=[("U4", i), "vecs"], writes=[("U3", i)])
                for k in range(1, 4):
                    for i in range(4):
                        c = gq * 4 + i
                        S.add(DVE, lambda e, i=i, c=c, k=k: e.scalar_tensor_tensor(
                            out=U3[:, i, :], in0=U4[:, i, k:k + 512], scalar=self.v("conv_w", c * 4 + k),
                            in1=U3[:, i, :], op0=ALU.mult, op1=ALU.add),
                            reads=[("U4", i), ("U3", i), "vecs"], writes=[("U3", i)])
                for i in range(4):
                    c = gq * 4 + i
                    S.add(DVE, lambda e, i=i, c=c: e.tensor_copy(out=halo[:, c, 0:3], in_=U4[:, i, 512:515]),
                          reads=[("U4", i)], writes=[("halo", c)])
                for i in range(4):
                    c = gq * 4 + i
                    if c < 8:
                        dst, dk = BB[:, c, :], ("BB", c)
                    elif c < 10:
                        dst, dk = BT[:, c - 8, :], ("BT", c - 8)
                    else:
                        dst, dk = CT[:, c - 10, :], ("CT", c - 10)
                    S.add(ACT, lambda e, i=i, dst=dst: e.activation(out=dst, in_=U3[:, i, :], func=AF.Silu),
                          reads=[("U3", i)], writes=[dk])
            wdt, wdtk = self.wload(W["w_dt"][0], 128)
            ps, pk = self.bank()
            for j in range(4):
                self.mm(ps[:, j * 16:(j + 1) * 16], pk,
                        [(xnt[:, kc, j * 128:(j + 1) * 128], wdt[:, kc * 16:(kc + 1) * 16], [wdtk, ("xnt", kc)])
                         for kc in range(8)])
            S.add(DVE, lambda e, ps=ps: e.tensor_tensor(
                out=dtr[:, :, :], in0=ps[:, 0:64].rearrange("p (j h) -> p j h", j=4),
                in1=self.v("dt_bias", 0, 16).unsqueeze(1).to_broadcast([128, 4, 16]), op=ALU.add),
                reads=[pk, "vecs"], writes=["dtr"])
            S.add(ACT, lambda e: e.activation(out=dtr[:, :, :], in_=dtr[:, :, :], func=AF.Exp), reads=["dtr"], writes=["dtr"])
            S.add(ACT, lambda e: e.activation(out=dt[:, :, :], in_=dtr[:, :, :], func=AF.Ln, bias=self.ones_f[:, 0:1]),
                  reads=["dtr", "const"], writes=["dt"])
            S.add(DVE, lambda e: e.tensor_tensor(out=adt[:, :, :], in0=dt[:, :, :],
                                                 in1=self.a_rep[:, :].unsqueeze(1).to_broadcast([128, 4, 16]), op=ALU.mult),
                  reads=["dt", "a_rep"], writes=["adt"])
            for j in range(4):
                ps, pk = self.bank()
                self.mm(ps[:, 0:16], pk, [(tri_f, adt[:, j, :], ["adt", "const"])])
                self.mm(ps[:, 16:32], pk, [(self.ones_f[:, :], adt[:, j, :], ["adt", "const"])])
                S.add(DVE, lambda e, ps=ps, j=j: e.tensor_copy(out=cs_sb[:, j, :], in_=ps[:, 0:16]), reads=[pk], writes=[("cs", j)])
                S.add(ACT, lambda e, ps=ps, j=j: e.activation(out=ecl[:, j, :], in_=ps[:, 16:32], func=AF.Exp),
                      reads=[pk], writes=[("ecl", j)])
                S.add(DVE, lambda e, ps=ps, j=j: e.tensor_tensor(out=d1[:, j, :], in0=ps[:, 16:32], in1=cs_sb[:, j, :],
                                                               op=ALU.subtract),
                      reads=[pk, ("cs", j)], writes=[("d1", j)])
                S.add(ACT, lambda e, j=j: e.activation(out=d1[:, j, :], in_=d1[:, j, :], func=AF.Exp),
                      reads=[("d1", j)], writes=[("d1", j)])
                S.add(DVE, lambda e, j=j: e.tensor_tensor(out=wst[:, j, :], in0=d1[:, j, :], in1=dt[:, j, :], op=ALU.mult),
                      reads=[("d1", j), "dt"], writes=[("wst", j)])
            for j in range(4):
                for half in range(2):
                    ps, pk = self.bank()
                    for i in range(4):
                        c = half * 4 + i
                        S.add(PE, lambda e, ps=ps, i=i, c=c, j=j: e.transpose(
                            out=ps[:, i * 128:(i + 1) * 128], in_=BB[:, c, j * 128:(j + 1) * 128], identity=ident_f),
                            reads=[("BB", c), "const"], writes=[pk])
                    psv = ps[:, :].rearrange("p (h d) -> p h d", h=8)
                    S.add(DVE, lambda e, psv=psv, j=j, half=half: e.tensor_tensor(
                        out=U1[:, j, half * 512:(half + 1) * 512].rearrange("p (h d) -> p h d", h=8), in0=psv,
                        in1=dt[:, j, half * 8:(half + 1) * 8].unsqueeze(2).to_broadcast([128, 8, 64]), op=ALU.mult),
                        reads=[pk, "dt"], writes=[("U1", j)])
                    S.add(DVE, lambda e, psv=psv, j=j, half=half: e.tensor_tensor(
                        out=U2[:, j, half * 512:(half + 1) * 512].rearrange("p (h d) -> p h d", h=8), in0=psv,
                        in1=wst[:, j, half * 8:(half + 1) * 8].unsqueeze(2).to_broadcast([128, 8, 64]), op=ALU.mult),
                        reads=[pk, ("wst", j)], writes=[("U2", j)])
            ps, pk = self.bank()
            psb = ps[:, :].bitcast(BF16)
            for j in range(4):
                for g in range(2):
                    S.add(PE, lambda e, psb=psb, j=j, g=g: e.transpose(
                        out=psb[:, (j * 2 + g) * 128:(j * 2 + g + 1) * 128], in_=BT[:, g, j * 128:(j + 1) * 128],
                        identity=self.ident_bf[:, :]),
                        reads=[("BT", g), "const"], writes=[pk])
            S.add(ACT, lambda e, psb=psb: e.activation(out=Btok[:, :, :, :].rearrange("p j g n -> p (j g n)"),
                                                       in_=psb[:, 0:1024], func=AF.Copy),
                  reads=[pk], writes=["Btok"])
            for j in range(4):
                cols = slice(j * 128, (j + 1) * 128)
                ps, pk = self.bank()
                for g in range(2):
                    self.mm(ps[:, g * 128:(g + 1) * 128], pk, [(BT[:, g, cols], CT[:, g, cols], [("BT", g), ("CT", g)])])
                S.add(DVE, lambda e, ps=ps: e.tensor_tensor(
                    out=Gm[:, :, :], in0=ps[:, 0:256].rearrange("p (g l) -> p g l", g=2),
                    in1=tri_f.unsqueeze(1).to_broadcast([128, 2, 128]), op=ALU.mult),
                    reads=[pk, "const"], writes=["Gm"])
                S.add(DVE, lambda e, j=j: e.tensor_copy(out=adt_rep, in_=adt[:, j, :].unsqueeze(2).to_broadcast([128, 16, 128])),
                      reads=["adt"], writes=[("U3", i) for i in range(4)])
                for hq in range(4):
                    h0 = hq * 4
                    g = hq // 2
                    b = hq % 2
                    pc, pck = self.bank()
                    for i in range(4):
                        self.mm(pc[:, i * 128:(i + 1) * 128], pck,
                                [(adt_rep[:, h0 + i, :], tri_f, [("U3", (h0 + i) // 4), "const"])])
                    pcv = pc[:, :].rearrange("p (i l) -> p i l", i=4)
                    S.add(DVE, lambda e, pcv=pcv, b=b, j=j, h0=h0: e.tensor_tensor(
                        out=arg[b][:, :, :], in0=pcv,
                        in1=cs_sb[:, j, h0:h0 + 4].unsqueeze(2).to_broadcast([128, 4, 128]), op=ALU.subtract),
                        reads=[pck, ("cs", j)], writes=[("arg", b)])
                    S.add(DVE, lambda e, b=b: e.tensor_tensor(
                        out=arg[b][:, :, :], in0=arg[b][:, :, :],
                        in1=tri_f.unsqueeze(1).to_broadcast([128, 4, 128]), op=ALU.mult),
                        reads=[("arg", b), "const"], writes=[("arg", b)])
                    S.add(ACT, lambda e, b=b: e.activation(out=dec[b][:, :, :], in_=arg[b][:, :, :], func=AF.Exp),
                          reads=[("arg", b)], writes=[("dec", b)])
                    S.add(DVE, lambda e, b=b, g=g: e.tensor_tensor(
                        out=Mb[b][:, :, :], in0=dec[b][:, :, :],
                        in1=Gm[:, g, :].unsqueeze(1).to_broadcast([128, 4, 128]), op=ALU.mult),
                        reads=[("dec", b), "Gm"], writes=[("Mb", b)])
                    S.add(ACT, lambda e, b=b, pcv=pcv: e.activation(out=ecs[b][:, :, :], in_=pcv, func=AF.Exp),
                          reads=[pck], writes=[("ecs", b)])
                    S.add(DVE, lambda e, b=b, g=g, cols=cols: e.tensor_tensor(
                        out=Cp[b][:, :, :], in0=ecs[b][:, :, :],
                        in1=CT[:, g, cols].unsqueeze(1).to_broadcast([128, 4, 128]), op=ALU.mult),
                        reads=[("ecs", b), ("CT", g)], writes=[("Cp", b)])
                    yb, ybk = self.held(4 + hq // 2)
                    for i in range(4):
                        h = h0 + i
                        hb = (h % 2) * 64
                        pl = (h // 2) % 4
                        self.mm(yb[hb:hb + 64, pl * 128:(pl + 1) * 128], ybk,
                                [(U1[:, j, h * 64:(h + 1) * 64], Mb[b][:, i, :], [("U1", j), ("Mb", b)]),
                                 (prevTb[:, h * 64:(h + 1) * 64], Cp[b][:, i, :], [("prevTb", g), ("Cp", b)])])
                    if hq % 2 == 1:
                        for pl in range(4):
                            c = (hq // 2) * 4 + pl
                            S.add(DVE, lambda e, yb=yb, pl=pl, c=c, cols=cols: e.scalar_tensor_tensor(
                                out=BC[:, c, cols], in0=BB[:, c, cols], scalar=self.v("d_skip", c),
                                in1=yb[:, pl * 128:(pl + 1) * 128], op0=ALU.mult, op1=ALU.add),
                                reads=[ybk, ("BB", c), "vecs"], writes=[("BC", c)])
                for g in range(2):
                    ps, pk = self.bank()
                    self.mm(ps[:, :], pk, [(Btok[:, j, g, :], U2[:, j, g * 512:(g + 1) * 512], ["Btok", ("U2", j)])])
                    pv = prevT[:, g * 512:(g + 1) * 512]
                    S.add(DVE, lambda e, pv=pv, j=j, g=g: e.tensor_tensor(
                        out=pv.rearrange("p (h d) -> p h d", h=8), in0=pv.rearrange("p (h d) -> p h d", h=8),
                        in1=ecl[:, j, g * 8:(g + 1) * 8].unsqueeze(2).to_broadcast([128, 8, 64]), op=ALU.mult),
                        reads=[("prevT", g), ("ecl", j)], writes=[("prevT", g)])
                    S.add(DVE, lambda e, pv=pv, ps=ps: e.tensor_tensor(out=pv, in0=ps[:, :], in1=pv, op=ALU.add),
                          reads=[pk, ("prevT", g)], writes=[("prevT", g)])
                    S.add(ACT, lambda e, pv=pv, g=g: e.activation(out=prevTb[:, g * 512:(g + 1) * 512], in_=pv, func=AF.Copy),
                          reads=[("prevT", g)], writes=[("prevTb", g)])
            for c in range(8):
                S.add(DVE, lambda e, c=c: e.tensor_tensor(out=BC[:, c, :], in0=BC[:, c, :], in1=BA[:, c, :], op=ALU.mult),
                      reads=[("BC", c), ("BA", c)], writes=[("BC", c)])
            for g in range(2):
                self.rms_rstd([(BC[:, 4 * g + i, :], ("BC", 4 * g + i)) for i in range(4)], 512, sqv, sqk, rstd, "rstd")
                for i in range(4):
                    c = 4 * g + i
                    self.norm_apply(yn[:, c, :], ("U2", c // 2), BC[:, c, :], ("BC", c), self.v("ssd_norm_g", c), rstd, "rstd")
            yacts = [(yn[:, c, :], ("U2", c // 2)) for c in range(8)]
            for c in range(8):
                wb, wk = self.wload(W["w_ssd"][c], 1024)
                wg_, wgk = self.wload(W["w_gs"][c], 1024)
                ps, pk = self.bank()
                ps2, pk2 = self.bank()
                self.lin(ps2[:, :], pk2, wg_, wgk, 8, 128, xacts)
                self.lin(ps[:, :], pk, wb, wk, 8, 128, yacts)
                gb = gbuf[c % 2]
                gk = ("g", c % 2)
                S.add(ACT, lambda e, ps2=ps2, gb=gb, c=c: e.activation(out=gb, in_=ps2[:, :], func=AF.Sigmoid,
                                                                      bias=self.v("gb_ssd", c)),
                      reads=[pk2, "vecs"], writes=[gk])
                S.add(DVE, lambda e, ps=ps, gb=gb: e.tensor_tensor(out=ttmp, in0=ps[:, :], in1=gb, op=ALU.mult),
                      reads=[pk, gk], writes=["ttmp"])
                S.add(DVE, lambda e, c=c: e.tensor_tensor(out=mg[:, c, :], in0=ttmp, in1=mgt[:, c, :], op=ALU.add),
                      reads=["ttmp", ("mgt", c)], writes=[("U1", c // 2)])
            macts = [(mg[:, c, :], ("U1", c // 2)) for c in range(8)]
            for c in range(8):
                wb, wk = self.wload(W["w_out"][c], 1024)
                ps, pk = self.bank()
                self.lin(ps[:, :], pk, wb, wk, 8, 128, macts)
                S.add(ACT, lambda e, ps=ps, c=c: e.activation(out=BA[:, c, :], in_=ps[:, :], func=AF.Copy),
                      reads=[pk], writes=[("BA", c)])
            self.rms_rstd([(BA[:, c, :], ("BA", c)) for c in range(8)], D, sqv, sqk, rstd, "rstd")
            xspv = self.xsp.rearrange("(c p) t -> p c t", p=128)
            S.add(SP, lambda e, t0=t0: e.dma_start(out=BB[:, :, :], in_=xspv[:, :, t0:t0 + TT]),
                  reads=[("xsp", c, t) for c in range(8)], writes=[("BB", c) for c in range(8)], dma=True)
            for c in range(8):
                self.norm_apply(BA[:, c, :], ("BA", c), BA[:, c, :], ("BA", c), self.v("mix_post_g", c), rstd, "rstd")
                S.add(DVE, lambda e, c=c: e.tensor_tensor(out=BB[:, c, :], in0=BB[:, c, :], in1=BA[:, c, :], op=ALU.add),
                      reads=[("BB", c), ("BA", c)], writes=[("BB", c)])
            S.add(SP, lambda e, t0=t0: e.dma_start(out=xspv[:, :, t0:t0 + TT], in_=BB[:, :, :]),
                  reads=[("BB", c) for c in range(8)], writes=[("xsp", c, t) for c in range(8)], dma=True)
        AH.release()
        AX.release()

    def xattn(self, s):
        S, AH = self.S, self.AH
        x = self.x
        W = self.w
        NT = SEQ // TT
        AH.mark()
        memf = AH.f32([8, MEM])
        memn = AH.bf16([8, MEM])
        kx = AH.bf16([8, MEM])
        vx = AH.bf16([2, 1024])
        sq = AH.bf16([8, TT])
        rstd = AH.f32([TT])
        xnt = AH.bf16([8, TT])
        qx = AH.bf16([8, TT])
        ptx = [AH.bf16([TT]) for _ in range(4)]
        rdx = AH.f32([TT])
        ox = AH.bf16([8, TT])
        hx = AH.f32([8, TT])
        S.add(SP, lambda e: e.dma_start(out=memf[:, :, :], in_=self.memT[s].rearrange("(c p) m -> p c m", p=128)),
              writes=[("memf", c) for c in range(8)], dma=True)
        self.rms_rstd([(memf[:, c, :], ("memf", c)) for c in range(8)], D, sq[:, :, 0:MEM], "sq", rstd[:, 0:MEM], "rstd")
        for c in range(8):
            self.norm_apply(memn[:, c, :], ("memn", c), memf[:, c, :], ("memf", c), self.v("mem_norm_g", c), rstd[:, 0:MEM], "rstd")
        macts = [(memn[:, c, :], ("memn", c)) for c in range(8)]
        for c in range(8):
            wb, wk = self.wload(W["w_xk"][c], 1024)
            ps, pk = self.bank()
            self.lin(ps[:, 0:MEM], pk, wb, wk, 8, 128, macts)
            S.add(ACT, lambda e, ps=ps, c=c: e.activation(out=kx[:, c, :], in_=ps[:, 0:MEM], func=AF.Copy),
                  reads=[pk], writes=[("kx", c)])
        for half in range(2):
            wbs = [self.wload(W["w_xv"][half, kp], 1024) for kp in range(4)]
            for mb in range(2):
                ps, pk = self.bank()
                self.mm(ps[:, :], pk, [(memn[:, kc, mb * 128:(mb + 1) * 128],
                                        wbs[kc // 2][0][:, (kc % 2) * 512:(kc % 2 + 1) * 512],
                                        [wbs[kc // 2][1], ("memn", kc)]) for kc in range(8)])
                S.add(ACT, lambda e, ps=ps, mb=mb, half=half: e.activation(
                    out=vx[:, mb, half * 512:(half + 1) * 512], in_=ps[:, :], func=AF.Copy),
                    reads=[pk], writes=[("vx", mb, half)])
        pti = 0
        for t in range(NT):
            t0 = t * TT
            self.rms_rstd([(x[:, c, t0:t0 + TT], ("x", c, t)) for c in range(8)], D, sq, "sq", rstd, "rstd")
            for c in range(8):
                self.norm_apply(xnt[:, c, :], ("xnt", c), x[:, c, t0:t0 + TT], ("x", c, t), self.v("xa_pre_g", c), rstd, "rstd")
            xacts = [(xnt[:, c, :], ("xnt", c)) for c in range(8)]
            for c in range(8):
                wb, wk = self.wload(W["w_xq"][c], 1024)
                ps, pk = self.bank()
                self.lin(ps[:, :], pk, wb, wk, 8, 128, xacts)
                S.add(ACT, lambda e, ps=ps, c=c: e.activation(out=qx[:, c, :], in_=ps[:, :], func=AF.Copy),
                      reads=[pk], writes=[("qx", c)])
            for hh in range(4):
                pp = []
                for mb in range(2):
                    ps, pk = self.bank()
                    self.mm(ps[:, :], pk, [(kx[:, 2 * hh + kc, mb * 128:(mb + 1) * 128], qx[:, 2 * hh + kc, :],
                                            [("kx", 2 * hh + kc), ("qx", 2 * hh + kc)]) for kc in range(2)])
                    pt = ptx[pti % 4]
                    ptk = ("ptx", pti % 4)
                    pti += 1
                    S.add(ACT, lambda e, ps=ps, pt=pt: e.activation(out=pt, in_=ps[:, :], func=AF.Exp, scale=SCALE_XA),
                          reads=[pk], writes=[ptk])
                    pp.append((pt, ptk))
                pd, pdk = self.bank()
                self.mm(pd[:, :], pdk, [(self.ones_bf[:, :], pp[mb][0], [pp[mb][1], "const"]) for mb in range(2)])
                S.add(DVE, lambda e, pd=pd: e.reciprocal(out=rdx, in_=pd[:, :]), reads=[pdk], writes=["rdx"])
                for dvc in range(2):
                    c = 2 * hh + dvc
                    po, pok = self.bank()
                    self.mm(po[:, :], pok, [(vx[:, mb, c * 128:(c + 1) * 128], pp[mb][0],
                                             [("vx", mb, c // 4), pp[mb][1]]) for mb in range(2)])
                    S.add(DVE, lambda e, po=po, c=c: e.tensor_tensor(out=ox[:, c, :], in0=po[:, :], in1=rdx, op=ALU.mult),
                          reads=[pok, "rdx"], writes=[("ox", c)])
            oacts = [(ox[:, c, :], ("ox", c)) for c in range(8)]
            for c in range(8):
                wb, wk = self.wload(W["w_xo"][c], 1024)
                ps, pk = self.bank()
                self.lin(ps[:, :], pk, wb, wk, 8, 128, oacts)
                S.add(ACT, lambda e, ps=ps, c=c: e.activation(out=hx[:, c, :], in_=ps[:, :], func=AF.Copy),
                      reads=[pk], writes=[("hx", c)])
            self.rms_rstd([(hx[:, c, :], ("hx", c)) for c in range(8)], D, sq, "sq", rstd, "rstd")
            for c in range(8):
                self.norm_apply(hx[:, c, :], ("hx", c), hx[:, c, :], ("hx", c), self.v("xa_post_g", c), rstd, "rstd")
                S.add(DVE, lambda e, c=c, t0=t0: e.tensor_tensor(out=x[:, c, t0:t0 + TT], in0=x[:, c, t0:t0 + TT],
                                                                in1=hx[:, c, :], op=ALU.add),
                      reads=[("hx", c), ("x", c, t)], writes=[("x", c, t)])
        AH.release()
        S.barrier()

    def build(self):
        nc, S = self.nc, self.S
        nseq = self.nseq
        xT = self.inp("xT", [nseq, D, SEQ])
        self.memT = self.inp("memT", [nseq, D, MEM])
        self.pos_d = self.inp("pos", [nseq, SEQ], I32)
        outT = nc.dram_tensor("outT", [nseq, D, SEQ], F32, kind="ExternalOutput").ap()
        self.xsp = nc.dram_tensor("xsp", [D, SEQ], F32, kind="Internal").ap()
        self.xn_hbm = nc.dram_tensor("xn_hbm", [D, SEQ], BF16, kind="Internal").ap()
        self.mg_hbm = nc.dram_tensor("mg_hbm", [D, SEQ], BF16, kind="Internal").ap()
        vecs_d = self.inp("vecs", [128, NV])
        cst_d = self.inp("cst", [128, 256])
        w = {}
        for f in ("ffn1", "ffn2"):
            w[f + "_wg"] = self.inp(f + "_wg", [NFF, 128, 1024])
            w[f + "_wu"] = self.inp(f + "_wu", [NFF, 128, 1024])
            w[f + "_wd"] = self.inp(f + "_wd", [8, 2, 128, 1408])
        for n, shp in (("w_z", [8, 128, 1024]), ("w_xbc", [12, 128, 1024]), ("w_gs", [8, 128, 1024]), ("w_gm", [8, 128, 1024]),
                       ("w_qc", [3, 128, 1024]), ("w_kvc", [2, 128, 1024]), ("w_kra", [1, 128, 1024]), ("w_krb", [1, 128, 1024]),
                       ("w_dt", [1, 128, 128]), ("w_uqn", [8, 128, 384]), ("w_uqa", [4, 128, 384]), ("w_uqb", [4, 128, 384]),
                       ("w_uk", [8, 128, 256]), ("w_uv", [2, 128, 1024]), ("w_ssd", [8, 128, 1024]), ("w_mla", [8, 128, 1024]),
                       ("w_out", [8, 128, 1024]), ("w_xq", [8, 128, 1024]), ("w_xk", [8, 128, 1024]), ("w_xo", [8, 128, 1024]),
                       ("w_xv", [2, 4, 128, 1024])):
            w[n] = self.inp(n, shp)
        self.w = w

        with contextlib.ExitStack() as es:
            sb = lambda n, s, d: es.enter_context(nc.sbuf_tensor(n, s, d))
            self.vecs = sb("vecs_sb", [128, NV], F32)
            self.cst = sb("cst_sb", [128, 256], F32)
            self.ones_bf = sb("ones_bf", [128, 128], BF16)
            self.tri_bf = sb("tri_bf", [128, 128], BF16)
            self.ident_bf = sb("ident_bf", [128, 128], BF16)
            self.ones_f = sb("ones_f", [128, 128], F32)
            self.eps_ap = sb("eps", [128, 1], F32)
            self.a_rep = sb("a_rep", [128, 16], F32)
            self.NWB = 6
            self.wbufs = [sb("wb%d" % i, [128, WSLOT], BF16) for i in range(self.NWB)]
            XW = 8 * SEQ
            ARW = 45 * 1024
            arena = sb("arena", [128, ARW], F32)
            self.x = arena[:, 0:XW].rearrange("p (c t) -> p c t", c=8)
            self.AX = Arena(arena[:, 0:XW], XW)
            self.AH = Arena(arena[:, XW:ARW], ARW - XW)
            self.ps = [es.enter_context(nc.psum_tensor("ps%d" % i, [128, 512], F32)) for i in range(8)]

            S.add(DVE, lambda e: e.memset(self.ones_bf[:, :], 1.0), writes=["ones"])
            S.add(DVE, lambda e: e.memset(self.ones_f[:, :], 1.0), writes=["onesf"])
            S.add(DVE, lambda e: e.memset(self.eps_ap[:, :], EPS), writes=["eps"])
            S.add(SP, lambda e: e.dma_start(out=self.vecs[:, :], in_=vecs_d), writes=["vecs"], dma=True)
            S.add(SP, lambda e: e.dma_start(out=self.cst[:, :], in_=cst_d), writes=["cst"], dma=True)
            S.add(DVE, lambda e: e.tensor_copy(out=self.tri_bf[:, :], in_=self.cst[:, 0:128]), reads=["cst"], writes=["tribf"])
            S.add(DVE, lambda e: e.tensor_copy(out=self.ident_bf[:, :], in_=self.cst[:, 128:256]), reads=["cst"], writes=["idbf"])
            S.add(ACT, lambda e: e.activation(out=self.a_rep[:, :], in_=self.v("a_log", 0, 16), func=AF.Exp),
                  reads=["vecs"], writes=["a_rep"])
            S.add(DVE, lambda e: e.tensor_scalar(out=self.a_rep[:, :], in0=self.a_rep[:, :], scalar1=-1.0, scalar2=None,
                                                 op0=ALU.mult), reads=["a_rep"], writes=["a_rep"])
            S.barrier()
            outs = []
            for s in range(nseq):
                for c in range(8):
                    S.add(SP, lambda e, c=c, s=s: e.dma_start(out=self.x[:, c, :], in_=xT[s, c * 128:(c + 1) * 128, :]),
                          writes=[("x", c, t) for t in range(4)], dma=True)
                if self.stop_after >= 1:
                    self.ffn(w["ffn1_wg"], w["ffn1_wu"], w["ffn1_wd"], "ffn1_pre_g", "ffn1_post_g")
                if self.stop_after >= 2:
                    self.mixer(s)
                if self.stop_after >= 3:
                    self.xattn(s)
                if self.stop_after >= 4:
                    self.ffn(w["ffn2_wg"], w["ffn2_wu"], w["ffn2_wd"], "ffn2_pre_g", "ffn2_post_g")
                for c in range(8):
                    outs.append(S.add(SP, lambda e, c=c, s=s: e.dma_start(
                        out=outT[s, c * 128:(c + 1) * 128, :], in_=self.x[:, c, :]),
                        reads=[("x", c, t) for t in range(4)], dma=True))
                S.barrier()
            S.emit(outs)
        return nc


def prep_shared(inp):
    f = lambda n: np.asarray(inp[n][0], np.float32)
    sh = {}
    vec = np.zeros((128, NV), np.float32)

    def put(name, arr):
        vec[:, VO[name]:VO[name] + arr.shape[1]] = arr

    for n in ("ffn1_pre_g", "ffn1_post_g", "mix_pre_g", "mix_post_g", "xa_pre_g", "mem_norm_g", "xa_post_g",
              "ffn2_pre_g", "ffn2_post_g", "ssd_norm_g", "q_norm_g", "kv_norm_g", "conv_b"):
        put(n, colvec(f(n)))
    put("conv_w", np.transpose(f("conv_w").reshape(4, 12, 128), (2, 1, 0)).reshape(128, 48))
    gb = f("gate_bias")
    put("gb_ssd", colvec(gb[:1024]))
    put("gb_mla", colvec(gb[1024:]))
    put("d_skip", colvec(np.repeat(f("d_skip"), 64)))
    put("dt_bias", np.tile(f("dt_bias")[None, :], (128, 1)))
    put("a_log", np.tile(f("a_log")[None, :], (128, 1)))
    r = np.arange(128) % 32
    inv = (np.float32(10000.0) ** (-(np.arange(0, 32, 2, dtype=np.float32)) / np.float32(32))).astype(np.float32)
    put("invf", inv[r % 16][:, None])
    put("sgn", np.where(r < 16, -1.0, 1.0).astype(np.float32)[:, None])
    sh["vecs"] = vec
    k = np.arange(128)
    tri = (k[:, None] <= k[None, :]).astype(np.float32)
    sh["cst"] = np.ascontiguousarray(np.concatenate([tri, np.eye(128, dtype=np.float32)], axis=1))
    for p in ("ffn1", "ffn2"):
        sh[p + "_wg"] = blockify(f(p + "_w_gate"), 128).reshape(NFF, 128, 1024)
        sh[p + "_wu"] = blockify(f(p + "_w_up"), 128).reshape(NFF, 128, 1024)
        sh[p + "_wd"] = np.ascontiguousarray(
            blockify(f(p + "_w_down"), 128).reshape(8, 128, 2, 1408).transpose(0, 2, 1, 3))
    win = f("w_in")
    b128 = lambda m: blockify(np.ascontiguousarray(m), 128).reshape(m.shape[1] // 128, 128, -1)
    sh["w_z"] = b128(win[:, 0:1024])
    sh["w_xbc"] = b128(win[:, 1024:2560])
    sh["w_dt"] = blockify(np.ascontiguousarray(win[:, 2560:2576]), 16).reshape(1, 128, 128)
    sh["w_qc"] = b128(win[:, 2576:2960])
    sh["w_kvc"] = b128(win[:, 2960:3216])
    kr = win[:, 3216:3248]
    sh["w_kra"] = b128(np.tile(kr, (1, 4)))
    sh["w_krb"] = b128(np.tile(np.concatenate([kr[:, 16:32], kr[:, 0:16]], axis=1), (1, 4)))
    sh["w_gs"] = b128(win[:, 3248:4272])
    sh["w_gm"] = b128(win[:, 4272:5296])
    wuq = f("w_uq").reshape(384, 16, 96)
    sh["w_uqn"] = b128(wuq[:, :, 0:64].reshape(384, 1024))
    sh["w_uqa"] = b128(wuq[:, :, 64:96].reshape(384, 512))
    sh["w_uqb"] = b128(np.concatenate([wuq[:, :, 80:96], wuq[:, :, 64:80]], axis=2).reshape(384, 512))
    sh["w_uk"] = b128(f("w_uk"))
    sh["w_uv"] = blockify(f("w_uv"), 512).reshape(2, 128, 1024)
    for n, src in (("w_ssd", "w_ssd_proj"), ("w_mla", "w_mla_proj"), ("w_out", "w_out"), ("w_xq", "w_xq"),
                   ("w_xk", "w_xk"), ("w_xo", "w_xo")):
        sh[n] = b128(f(src))
    sh["w_xv"] = np.ascontiguousarray(blockify(f("w_xv"), 512).reshape(2, 128, 4, 1024).transpose(0, 2, 1, 3))
    return sh


def run(inp, nseq, cores, stop_after=99):
    k = K(nseq, stop_after)
    nc = k.build()
    sh = prep_shared(inp)
    maps = []
    for ci in range(cores):
        m = dict(sh)
        sl = slice(ci * nseq, (ci + 1) * nseq)
        m["xT"] = np.ascontiguousarray(np.transpose(np.asarray(inp["x"][sl], np.float32), (0, 2, 1)))
        m["memT"] = np.ascontiguousarray(np.transpose(np.asarray(inp["mem"][sl], np.float32), (0, 2, 1)))
        m["pos"] = np.ascontiguousarray(np.asarray(inp["positions"][sl], np.int32))
        maps.append({n: m[n] for n in k.din})
    res = run_bass_kernel_spmd(nc, maps, core_ids=list(range(cores)))
    outs = [np.transpose(r["outT"], (0, 2, 1)) for r in res.results]
    return np.ascontiguousarray(np.concatenate(outs, axis=0)).astype(np.float32)


def kernel(**inputs):
    inp = {k: np.asarray(v) for k, v in inputs.items()}
    return run(inp, 2, NCORES)
```

```python
import contextlib
import numpy as np
import concourse.bass as bass
import concourse.mybir as mybir
from concourse.bass_utils import run_bass_kernel_spmd

F32 = mybir.dt.float32
BF16 = mybir.dt.bfloat16
I32 = mybir.dt.int32
AF = mybir.ActivationFunctionType
ALU = mybir.AluOpType
PE, ACT, DVE, POOL, SP = "tensor", "scalar", "vector", "gpsimd", "sync"

D = 1024
SEQ = 2048
TT = 512
DFF = 2816
NFF = DFF // 128
MEM = 256
EPS = 1e-6
NCORES = 8


class Op:
    __slots__ = ("eng", "fn", "deps", "alldeps", "signal", "count", "dma", "dsem", "dval", "prev_dval",
                 "cost", "lat", "seq", "bar", "t_end", "done")

    def __init__(self, eng, fn, dma):
        self.eng = eng
        self.fn = fn
        self.deps = []
        self.alldeps = []
        self.signal = False
        self.count = 0
        self.dma = dma
        self.dsem = None
        self.dval = 0
        self.prev_dval = 0
        self.cost = 0.0
        self.lat = 0.0
        self.seq = 0
        self.bar = False
        self.t_end = 0.0
        self.done = False


DEF_COST = {PE: 0.22, ACT: 0.55, DVE: 0.6, POOL: 1.0, SP: 0.2}


class Sched:
    NPOOL = 12
    import os as _os
    WINDOW = int(_os.environ.get('SWIN', '48'))

    def __init__(self, nc):
        self.nc = nc
        self.ops = {PE: [], ACT: [], DVE: [], POOL: [], SP: []}
        self.lastw = {}
        self.readers = {}
        self.seq = 0
        self.seg_dmas = []
        import os
        self.reorder = os.environ.get('NOREORDER') is None

    def add(self, eng, fn, reads=(), writes=(), dma=False, cost=None):
        op = Op(eng, fn, dma)
        self.seq += 1
        op.seq = self.seq
        if dma:
            op.cost = 1.2 if eng == POOL else 0.15
            op.lat = 3.0 if cost is None else cost
        else:
            op.cost = DEF_COST[eng] if cost is None else cost
        deps = {}
        for k in reads:
            w = self.lastw.get(k)
            if w is not None:
                deps[id(w)] = w
            if isinstance(k, tuple) and k[0] == "ps":
                for r in self.readers.get(k, ()):
                    if r.eng != eng:
                        deps[id(r)] = r
        for k in writes:
            w = self.lastw.get(k)
            if w is not None:
                deps[id(w)] = w
            for r in self.readers.get(k, ()):
                deps[id(r)] = r
        for d in deps.values():
            if d is op:
                continue
            op.alldeps.append(d)
            if eng == PE and d.eng == PE and not d.dma and not dma:
                continue
            op.deps.append(d)
            if not d.dma:
                d.signal = True
        for k in reads:
            self.readers.setdefault(k, []).append(op)
        for k in writes:
            self.lastw[k] = op
            self.readers[k] = []
        if dma:
            self.seg_dmas.append(op)
        self.ops[eng].append(op)
        return op

    def barrier(self):
        lasts = []
        for e, lst in self.ops.items():
            for o in reversed(lst):
                if o.bar:
                    break
                if o.fn is not None and not o.dma:
                    lasts.append(o)
                    break
        lasts += self.seg_dmas
        self.seg_dmas = []
        for e in self.ops:
            op = Op(e, None, False)
            op.bar = True
            for d in lasts:
                op.deps.append(d)
                if not d.dma:
                    d.signal = True
            self.ops[e].append(op)
        self.lastw = {}
        self.readers = {}

    def schedule(self):
        import os
        self._seng = os.environ.get('SENG').split(',') if os.environ.get('SENG') else None
        engs = list(self.ops.keys())
        segs = {e: [] for e in engs}
        for e in engs:
            cur = []
            for o in self.ops[e]:
                if o.bar:
                    segs[e].append((cur, o))
                    cur = []
                else:
                    cur.append(o)
            segs[e].append((cur, None))
        nseg = len(segs[engs[0]])
        assert all(len(segs[e]) == nseg for e in engs)
        new = {e: [] for e in engs}
        for si in range(nseg):
            pend = {e: list(segs[e][si][0]) for e in engs}
            free = {e: 0.0 for e in engs}
            for e in engs:
                for o in pend[e]:
                    o.done = False
            inseg = set()
            for e in engs:
                for o in pend[e]:
                    inseg.add(id(o))
            total = sum(len(v) for v in pend.values())
            heads = {e: 0 for e in engs}
            while total:
                best = None
                for e in engs:
                    lst = pend[e]
                    h = heads[e]
                    while h < len(lst) and lst[h].done:
                        h += 1
                    heads[e] = h
                    cnt = 0
                    i = h
                    cand = None
                    win = self.WINDOW if (self._seng is None or e in self._seng) else 1
                    while i < len(lst) and cnt < win:
                        o = lst[i]
                        i += 1
                        if o.done:
                            continue
                        cnt += 1
                        ok = True
                        rdy = 0.0
                        for d in o.alldeps:
                            if id(d) in inseg:
                                if not d.done:
                                    ok = False
                                    break
                                t = d.t_end + (0.0 if d.eng == e and not d.dma else 0.12)
                                if t > rdy:
                                    rdy = t
                        if not ok:
                            continue
                        st = rdy if rdy > free[e] else free[e]
                        if cand is None or st < cand[0] - 1e-9:
                            cand = (st, o)
                        if st <= free[e] + 1e-9:
                            break
                    if cand is not None and (best is None or cand[0] < best[0] - 1e-9):
                        best = (cand[0], cand[1], e)
                assert best is not None, "scheduler stuck"
                st, o, e = best
                o.done = True
                free[e] = st + o.cost
                o.t_end = st + o.cost + o.lat
                new[e].append(o)
                total -= 1
            lasts = []
            for e in engs:
                for o in reversed(pend[e]):
                    pass
                seg_ops = [o for o in new[e] if id(o) in inseg]
                for o in reversed(seg_ops):
                    if o.fn is not None and not o.dma:
                        lasts.append(o)
                        o.signal = True
                        break
                lasts += [o for o in seg_ops if o.dma]
            for e in engs:
                b = segs[e][si][1]
                if b is not None:
                    b.deps = list(lasts)
                    new[e].append(b)
        self.ops = new

    def emit(self, final_waits=()):
        nc = self.nc
        if self.reorder:
            self.schedule()
        for e, lst in self.ops.items():
            rr = 0
            cnt = {}
            for o in lst:
                if o.dma:
                    slot = (e, rr % self.NPOOL)
                    rr += 1
                    o.dsem = slot
                    o.prev_dval = cnt.get(slot, 0)
                    o.dval = o.prev_dval + 16
                    cnt[slot] = o.dval
        with contextlib.ExitStack() as es:
            engsem = {e: es.enter_context(nc.semaphore("s_" + e)) for e in (PE, ACT, DVE, POOL)}
            dsems = {}
            for e in self.ops:
                if any(o.dma for o in self.ops[e]):
                    for i in range(self.NPOOL):
                        dsems[(e, i)] = es.enter_context(nc.semaphore("d_%s_%d" % (e, i)))
            for e, lst in self.ops.items():
                c = 0
                for o in lst:
                    if o.signal and not o.dma:
                        c += 1
                        o.count = c
            block = es.enter_context(nc.Block())

            def run(e, lst, extra):
                def body(eng):
                    waited = {}

                    def need(sem, val):
                        if waited.get(id(sem), 0) >= val:
                            return
                        waited[id(sem)] = val
                        eng.wait_ge(sem, val)

                    for o in lst:
                        for d in o.deps:
                            if d.dma:
                                need(dsems[d.dsem], d.dval)
                            else:
                                need(engsem[d.eng], d.count)
                        if o.dma and o.prev_dval:
                            need(dsems[o.dsem], o.prev_dval)
                        if o.fn is None:
                            continue
                        inst = o.fn(eng)
                        if o.dma:
                            inst.then_inc(dsems[o.dsem], 16)
                        elif o.signal:
                            inst.then_inc(engsem[e], 1)
                    for d in extra:
                        need(dsems[d.dsem], d.dval)
                return body

            for e in (PE, ACT, DVE, POOL):
                getattr(block, e)(run(e, self.ops[e], ()))
            getattr(block, SP)(run(SP, self.ops[SP], list(final_waits)))


class Arena:
    def __init__(self, ap, nwords):
        self.ap = ap
        self.n = nwords
        self.top = 0
        self.stack = []
        self.peak = 0

    def mark(self):
        self.stack.append(self.top)

    def release(self):
        self.top = self.stack.pop()

    def _take(self, words):
        a = self.top
        self.top += words
        self.peak = max(self.peak, self.top)
        assert self.top <= self.n, ("arena overflow", self.top, self.n)
        return self.ap[:, a:a + words]

    def f32(self, shape):
        n = int(np.prod(shape))
        v = self._take(n)
        if len(shape) == 2:
            v = v.rearrange("p (a b) -> p a b", a=shape[0])
        elif len(shape) == 3:
            v = v.rearrange("p (a b c) -> p a b c", a=shape[0], b=shape[1])
        return v

    def bf16(self, shape):
        n = int(np.prod(shape))
        assert n % 2 == 0
        v = self._take(n // 2).bitcast(BF16)
        if len(shape) == 2:
            v = v.rearrange("p (a b) -> p a b", a=shape[0])
        elif len(shape) == 3:
            v = v.rearrange("p (a b c) -> p a b c", a=shape[0], b=shape[1])
        return v


def blockify(w, mb):
    k, m = w.shape
    assert k % 128 == 0 and m % mb == 0
    return np.ascontiguousarray(w.reshape(k // 128, 128, m // mb, mb).transpose(2, 1, 0, 3))


def colvec(v):
    return np.ascontiguousarray(v.reshape(-1, 128).T)


VEC_SPEC = [("ffn1_pre_g", 8), ("ffn1_post_g", 8), ("mix_pre_g", 8), ("mix_post_g", 8), ("xa_pre_g", 8),
            ("mem_norm_g", 8), ("xa_post_g", 8), ("ffn2_pre_g", 8), ("ffn2_post_g", 8),
            ("ssd_norm_g", 8), ("q_norm_g", 3), ("kv_norm_g", 2), ("conv_w", 48), ("conv_b", 12),
            ("gb_ssd", 8), ("gb_mla", 8), ("d_skip", 8), ("dt_bias", 16), ("a_log", 16),
            ("invf", 1), ("sgn", 1)]
VO = {}
_o = 0
for _n, _c in VEC_SPEC:
    VO[_n] = _o
    _o += _c
NV = _o
WSLOT = 1408
PI = float(np.pi)
SCALE_MLA = 96.0 ** -0.5
SCALE_XA = 256.0 ** -0.5


class K:
    def __init__(self, nseq, stop_after=99):
        self.nseq = nseq
        self.stop_after = stop_after
        self.nc = bass.Bass("TRN2", target_bir_lowering=False)
        self.S = Sched(self.nc)
        self.din = {}
        self.bank_i = 0
        self.wb_i = 0

    def inp(self, name, shape, dtype=F32):
        t = self.nc.dram_tensor(name, list(shape), dtype, kind="ExternalInput").ap()
        self.din[name] = t
        return t

    def bank(self):
        i = self.bank_i % 4
        self.bank_i += 1
        return self.ps[i], ("ps", i)

    def held(self, i):
        return self.ps[i], ("ps", i)

    def v(self, name, c=0, n=1):
        o = VO[name] + c
        return self.vecs[:, o:o + n]

    def wload(self, src, nelem):
        i = self.wb_i % self.NWB
        self.wb_i += 1
        key = ("wb", i)
        dst = self.wbufs[i][:, 0:nelem]
        self.S.add(POOL, lambda e: e.dma_start(out=dst, in_=src, max_dma_last_dim=4096),
                   writes=[key], dma=True)
        return dst, key

    def wload_multi(self, src, nb, n):
        i = self.wb_i % self.NWB
        self.wb_i += 1
        key = ("wb", i)
        dst = self.wbufs[i][:, 0:nb * n].rearrange("p (b n) -> p b n", b=nb)
        self.S.add(POOL, lambda e: e.dma_start(out=dst, in_=src.rearrange("b p n -> p b n"), max_dma_last_dim=4096),
                   writes=[key], dma=True)
        return dst, key

    def mm(self, out, pk, pairs):
        n = len(pairs)
        for i, (l, r, ks) in enumerate(pairs):
            self.S.add(PE, lambda e, l=l, r=r, i=i: e.matmul(out, lhsT=l, rhs=r, start=(i == 0), stop=(i == n - 1)),
                       reads=ks, writes=[pk])

    def lin(self, out, pk, wb, wk, kcn, mb, acts, m0=0, m=128):
        self.mm(out, pk, [(wb[:, kc * mb + m0:kc * mb + m0 + m], acts[kc][0], [wk, acts[kc][1]]) for kc in range(kcn)])

    def rms_rstd(self, srcs, nfeat, sq, sqname, rstd, rkey):
        S = self.S
        fc = len(srcs)
        T = rstd.shape[-1]
        sk = sqname if callable(sqname) else (lambda c: (sqname, c))
        for c, (ap, k) in enumerate(srcs):
            S.add(ACT, lambda e, c=c, ap=ap: e.activation(out=sq[:, c, :], in_=ap, func=AF.Square),
                  reads=[k], writes=[sk(c)])
        ps, pk = self.bank()
        self.mm(ps[:, 0:T], pk, [(self.ones_bf[:, :], sq[:, c, :], [sk(c), "const"]) for c in range(fc)])
        S.add(ACT, lambda e: e.activation(out=rstd, in_=ps[:, 0:T], func=AF.Ln, scale=1.0 / nfeat,
                                          bias=self.eps_ap[:, 0:1]),
              reads=[pk, "const"], writes=[rkey])
        S.add(ACT, lambda e: e.activation(out=rstd, in_=rstd, func=AF.Exp, scale=-0.5), reads=[rkey], writes=[rkey])

    def norm_apply(self, out, okey, src, skey, gcol, rstd, rkey):
        self.S.add(DVE, lambda e: e.scalar_tensor_tensor(out=out, in0=src, scalar=gcol, in1=rstd,
                                                         op0=ALU.mult, op1=ALU.mult),
                   reads=[skey, rkey, "vecs"], writes=[okey])

    def ffn(self, wg, wu, wd, gpre, gpost):
        S, A = self.S, self.AH
        x = self.x
        HT = 1024
        NH = HT // TT
        for half in range(SEQ // HT):
            A.mark()
            xn = A.bf16([8, HT])
            h = A.bf16([NFF, HT])
            y = A.f32([8, HT])
            sq = A.bf16([8, TT])
            rstd = A.f32([TT])
            sg = [A.f32([TT]) for _ in range(2)]
            for tt in range(NH):
                t0 = half * HT + tt * TT
                gt = t0 // TT
                self.rms_rstd([(x[:, c, t0:t0 + TT], ("x", c, gt)) for c in range(8)], D, sq, "sq", rstd, "rstd")
                for c in range(8):
                    self.norm_apply(xn[:, c, tt * TT:(tt + 1) * TT], ("xn", c, tt), x[:, c, t0:t0 + TT], ("x", c, gt),
                                    self.v(gpre, c), rstd, "rstd")
            for fb in range(NFF):
                wgb, wgk = self.wload(wg[fb], 1024)
                wub, wuk = self.wload(wu[fb], 1024)
                for tt in range(NH):
                    acts = [(xn[:, kc, tt * TT:(tt + 1) * TT], ("xn", kc, tt)) for kc in range(8)]
                    pg, pgk = self.bank()
                    pu, puk = self.bank()
                    self.lin(pg[:, :], pgk, wgb, wgk, 8, 128, acts)
                    self.lin(pu[:, :], puk, wub, wuk, 8, 128, acts)
                    sgb = sg[(fb * NH + tt) % 2]
                    sgk = ("sg", (fb * NH + tt) % 2)
                    S.add(ACT, lambda e, pg=pg, sgb=sgb: e.activation(out=sgb, in_=pg[:, :], func=AF.Silu),
                          reads=[pgk], writes=[sgk])
                    S.add(DVE, lambda e, pu=pu, sgb=sgb, fb=fb, tt=tt: e.tensor_tensor(
                        out=h[:, fb, tt * TT:(tt + 1) * TT], in0=pu[:, :], in1=sgb, op=ALU.mult),
                        reads=[puk, sgk], writes=[("h", fb, tt)])
            HF = NFF // 2
            for dc in range(8):
                wd0, wdk0 = self.wload(wd[dc, 0], HF * 128)
                wd1, wdk1 = self.wload(wd[dc, 1], HF * 128)
                for tt in range(NH):
                    py, pyk = self.bank()
                    pairs = []
                    for fc in range(NFF):
                        wb, wk = (wd0, wdk0) if fc < HF else (wd1, wdk1)
                        f = fc % HF
                        pairs.append((wb[:, f * 128:(f + 1) * 128], h[:, fc, tt * TT:(tt + 1) * TT], [wk, ("h", fc, tt)]))
                    self.mm(py[:, :], pyk, pairs)
                    S.add(ACT, lambda e, py=py, dc=dc, tt=tt: e.activation(
                        out=y[:, dc, tt * TT:(tt + 1) * TT], in_=py[:, :], func=AF.Copy),
                        reads=[pyk], writes=[("y", dc, tt)])
            for tt in range(NH):
                t0 = half * HT + tt * TT
                gt = t0 // TT
                self.rms_rstd([(y[:, c, tt * TT:(tt + 1) * TT], ("y", c, tt)) for c in range(8)], D, sq, "sq", rstd, "rstd")
                for c in range(8):
                    ysl = y[:, c, tt * TT:(tt + 1) * TT]
                    self.norm_apply(ysl, ("y", c, tt), ysl, ("y", c, tt), self.v(gpost, c), rstd, "rstd")
                    S.add(DVE, lambda e, c=c, ysl=ysl, t0=t0: e.scalar_tensor_tensor(
                        out=x[:, c, t0:t0 + TT], in0=ysl, scalar=0.5, in1=x[:, c, t0:t0 + TT],
                        op0=ALU.mult, op1=ALU.add),
                        reads=[("y", c, tt), ("x", c, gt)], writes=[("x", c, gt)])
            A.release()
            S.barrier()

    def rope_tables(self, s, t0, bufs, cosb, sinb):
        S = self.S
        posi, posf, ang, kf, tmp = bufs
        S.add(SP, lambda e: e.dma_start(out=posi, in_=self.pos_d[s:s + 1, t0:t0 + TT].partition_broadcast(128)),
              writes=["posi"], dma=True)
        S.add(DVE, lambda e: e.tensor_copy(out=posf, in_=posi), reads=["posi"], writes=["posf"])
        S.add(DVE, lambda e: e.tensor_scalar(out=ang, in0=posf, scalar1=self.v("invf"), scalar2=None, op0=ALU.mult),
              reads=["posf", "vecs"], writes=["ang"])
        ki = posi
        S.add(DVE, lambda e: e.tensor_scalar(out=ki, in0=ang, scalar1=1.0 / (2 * PI), scalar2=None, op0=ALU.mult),
              reads=["ang", "posf"], writes=["posi"])
        S.add(DVE, lambda e: e.tensor_copy(out=kf, in_=ki), reads=["posi"], writes=["kf"])
        S.add(DVE, lambda e: e.scalar_tensor_tensor(out=ang, in0=kf, scalar=-2 * PI, in1=ang, op0=ALU.mult, op1=ALU.add),
              reads=["kf", "ang"], writes=["ang"])
        for which, shift, dst, dk in (("sin", 0.0, sinb, "sinb"), ("cos", PI / 2, cosb, "cosb")):
            y = kf
            S.add(DVE, lambda e, shift=shift: e.tensor_scalar(out=y, in0=ang, scalar1=shift, scalar2=None, op0=ALU.add),
                  reads=["ang"], writes=["kf"])
            S.add(DVE, lambda e: e.tensor_scalar(out=tmp, in0=y, scalar1=PI, scalar2=2 * PI, op0=ALU.is_gt, op1=ALU.mult),
                  reads=["kf"], writes=["ropetmp"])
            S.add(DVE, lambda e: e.tensor_tensor(out=y, in0=y, in1=tmp, op=ALU.subtract),
                  reads=["kf", "ropetmp"], writes=["kf"])
            S.add(DVE, lambda e: e.tensor_scalar(out=tmp, in0=y, scalar1=-PI, scalar2=2 * PI, op0=ALU.is_lt, op1=ALU.mult),
                  reads=["kf"], writes=["ropetmp"])
            S.add(DVE, lambda e: e.tensor_tensor(out=y, in0=y, in1=tmp, op=ALU.add),
                  reads=["kf", "ropetmp"], writes=["kf"])
            S.add(DVE, lambda e: e.tensor_scalar(out=y, in0=y, scalar1=PI, scalar2=-PI, op0=ALU.min, op1=ALU.max),
                  reads=["kf"], writes=["kf"])
            if which == "sin":
                S.add(ACT, lambda e, dst=dst: e.activation(out=dst, in_=y, func=AF.Sin, scale=self.v("sgn")),
                      reads=["kf", "vecs"], writes=[dk])
            else:
                S.add(ACT, lambda e, dst=dst: e.activation(out=dst, in_=y, func=AF.Sin),
                      reads=["kf"], writes=[dk])

    def rope_apply(self, out, okey, psA, pkA, psB, pkB, cosb, sinb, t1, t2):
        S = self.S
        S.add(DVE, lambda e: e.tensor_tensor(out=t1, in0=psA, in1=cosb, op=ALU.mult), reads=[pkA, "cosb"], writes=["posf"])
        S.add(DVE, lambda e: e.tensor_tensor(out=t2, in0=psB, in1=sinb, op=ALU.mult), reads=[pkB, "sinb"], writes=["ropetmp"])
        S.add(DVE, lambda e: e.tensor_tensor(out=out, in0=t1, in1=t2, op=ALU.add), reads=["posf", "ropetmp"], writes=[okey])

    def mixer(self, s):
        S, AH, AX = self.S, self.AH, self.AX
        x = self.x
        W = self.w
        NT = SEQ // TT
        AH.mark()
        xnt = AH.bf16([8, TT])
        mgt = AH.bf16([8, TT])
        xnh = self.xn_hbm.rearrange("(c p) t -> p c t", p=128)
        mgh = self.mg_hbm.rearrange("(c p) t -> p c t", p=128)
        XK = [("xnt", c) for c in range(8)]
        AH.mark()
        sq = AH.bf16([8, TT])
        rstd = AH.f32([TT])
        for t in range(NT):
            t0 = t * TT
            self.rms_rstd([(x[:, c, t0:t0 + TT], ("x", c, t)) for c in range(8)], D, sq, "sq", rstd, "rstd")
            for c in range(8):
                self.norm_apply(xnt[:, c, :], ("xnt", c), x[:, c, t0:t0 + TT], ("x", c, t),
                                self.v("mix_pre_g", c), rstd, "rstd")
            S.add(SP, lambda e, t0=t0: e.dma_start(out=xnh[:, :, t0:t0 + TT], in_=xnt[:, :, :]),
                  reads=XK, writes=[("xnh", t)], dma=True)
        for c in range(8):
            S.add(SP, lambda e, c=c: e.dma_start(out=self.xsp[c * 128:(c + 1) * 128, :], in_=x[:, c, :]),
                  reads=[("x", c, t) for t in range(NT)], writes=[("xsp", c, t) for t in range(NT)], dma=True)
        AH.release()
        S.barrier()

        def load_xnt(t):
            S.add(SP, lambda e, t=t: e.dma_start(out=xnt[:, :, :], in_=xnh[:, :, t * TT:(t + 1) * TT]),
                  reads=[("xnh", t)], writes=XK, dma=True)

        AH.mark()
        AX.mark()
        Kh = AX.bf16([16, SEQ])
        qn = AH.bf16([3, SEQ])
        V = AH.bf16([16, 16, 64])
        cosb = AH.f32([TT])
        sinb = AH.f32([TT])
        rbufs = (AH.f32([TT]).bitcast(I32), AH.f32([TT]), AH.f32([TT]), AH.f32([TT]), AH.f32([TT]))
        t1 = rbufs[1]
        t2 = rbufs[4]
        rt = [AH.bf16([TT]) for _ in range(2)]
        AH.mark()
        kvc = AH.f32([2, TT])
        qc = AH.f32([3, TT])
        sq = AH.bf16([3, TT])
        rstd = AH.f32([TT])
        kvn = AH.bf16([2, TT])
        for t in range(NT):
            t0 = t * TT
            load_xnt(t)
            acts = [(xnt[:, kc, :], ("xnt", kc)) for kc in range(8)]
            self.rope_tables(s, t0, rbufs, cosb, sinb)
            for c in range(2):
                wb, wk = self.wload(W["w_kvc"][c], 1024)
                ps, pk = self.bank()
                self.lin(ps[:, :], pk, wb, wk, 8, 128, acts)
                S.add(ACT, lambda e, ps=ps, c=c: e.activation(out=kvc[:, c, :], in_=ps[:, :], func=AF.Copy),
                      reads=[pk], writes=[("kvc", c)])
            self.rms_rstd([(kvc[:, c, :], ("kvc", c)) for c in range(2)], 256, sq, "sq", rstd, "rstd")
            for c in range(2):
                self.norm_apply(kvn[:, c, :], ("kvn", c), kvc[:, c, :], ("kvc", c), self.v("kv_norm_g", c), rstd, "rstd")
            kacts = [(kvn[:, kc, :], ("kvn", kc)) for kc in range(2)]
            for i in range(2):
                wb, wk = self.wload_multi(W["w_uk"][4 * i:4 * i + 4], 4, 256)
                for b in range(4):
                    c = 4 * i + b
                    ps, pk = self.bank()
                    self.mm(ps[:, :], pk, [(wb[:, b, kc * 128:(kc + 1) * 128], kacts[kc][0], [wk, kacts[kc][1]]) for kc in range(2)])
                    S.add(ACT, lambda e, ps=ps, c=c, t0=t0: e.activation(out=Kh[0:64, 2 * c, t0:t0 + TT], in_=ps[0:64, :], func=AF.Copy),
                          reads=[pk], writes=[("Kn", 2 * c, t)])
                    S.add(DVE, lambda e, ps=ps, c=c, t0=t0: e.tensor_copy(out=Kh[0:64, 2 * c + 1, t0:t0 + TT], in_=ps[64:128, :]),
                          reads=[pk], writes=[("Kn", 2 * c + 1, t)])
            for half in range(2):
                wb, wk = self.wload(W["w_uv"][half], 1024)
                for j in range(4):
                    ps, pk = self.bank()
                    self.mm(ps[:, :], pk, [(kvn[:, kc, j * 128:(j + 1) * 128], wb[:, kc * 512:(kc + 1) * 512], [wk, ("kvn", kc)]) for kc in range(2)])
                    blk = t * 4 + j
                    S.add(DVE, lambda e, ps=ps, blk=blk, half=half: e.tensor_copy(
                        out=V[:, blk, half * 8:(half + 1) * 8, :], in_=ps[:, :].rearrange("p (h d) -> p h d", h=8)),
                        reads=[pk], writes=[("V", blk)])
            wa, wak = self.wload(W["w_kra"][0], 1024)
            wbb, wbk = self.wload(W["w_krb"][0], 1024)
            psA, pkA = self.bank()
            psB, pkB = self.bank()
            self.lin(psA[:, :], pkA, wa, wak, 8, 128, acts)
            self.lin(psB[:, :], pkB, wbb, wbk, 8, 128, acts)
            self.rope_apply(rt[0], ("rt", 0), psA[:, :], pkA, psB[:, :], pkB, cosb, sinb, t1, t2)
            S.add(DVE, lambda e, t0=t0: e.tensor_copy(out=Kh[64:96, :, t0:t0 + TT],
                                                      in_=rt[0][64:96, :].unsqueeze(1).to_broadcast([32, 16, TT])),
                  reads=[("rt", 0)], writes=[("Kr", t)])
            for c in range(3):
                wb, wk = self.wload(W["w_qc"][c], 1024)
                ps, pk = self.bank()
                self.lin(ps[:, :], pk, wb, wk, 8, 128, acts)
                S.add(ACT, lambda e, ps=ps, c=c: e.activation(out=qc[:, c, :], in_=ps[:, :], func=AF.Copy),
                      reads=[pk], writes=[("qc", c)])
            self.rms_rstd([(qc[:, c, :], ("qc", c)) for c in range(3)], 384, sq, "sq", rstd, "rstd")
            for c in range(3):
                self.norm_apply(qn[:, c, t0:t0 + TT], ("qn", c, t), qc[:, c, :], ("qc", c), self.v("q_norm_g", c), rstd, "rstd")
        AH.release()
        S.barrier()

        AH.mark()
        qh = AH.bf16([16, TT])
        pts = [AH.bf16([TT]) for _ in range(4)]
        o = AH.bf16([8, TT])
        rb = AH.f32([TT])
        gbuf = [AH.f32([TT]) for _ in range(2)]
        pti = 0
        for qt in range(NT):
            t0 = qt * TT
            self.rope_tables(s, t0, rbufs, cosb, sinb)
            qacts = [(qn[:, kc, t0:t0 + TT], ("qn", kc, qt)) for kc in range(3)]
            for (b0, nb) in ((0, 3), (3, 3), (6, 2)):
                wb, wk = self.wload_multi(W["w_uqn"][b0:b0 + nb], nb, 384)
                for b in range(nb):
                    c = b0 + b
                    ps, pk = self.bank()
                    self.mm(ps[:, :], pk, [(wb[:, b, kc * 128:(kc + 1) * 128], qacts[kc][0], [wk, qacts[kc][1]]) for kc in range(3)])
                    S.add(ACT, lambda e, ps=ps, c=c: e.activation(out=qh[0:64, 2 * c, :], in_=ps[0:64, :], func=AF.Copy),
                          reads=[pk], writes=[("qhn", 2 * c)])
                    S.add(DVE, lambda e, ps=ps, c=c: e.tensor_copy(out=qh[0:64, 2 * c + 1, :], in_=ps[64:128, :]),
                          reads=[pk], writes=[("qhn", 2 * c + 1)])
            for i in range(2):
                wa, wak = self.wload_multi(W["w_uqa"][2 * i:2 * i + 2], 2, 384)
                wbb, wbk = self.wload_multi(W["w_uqb"][2 * i:2 * i + 2], 2, 384)
                for b in range(2):
                    j = 2 * i + b
                    psA, pkA = self.bank()
                    psB, pkB = self.bank()
                    self.mm(psA[:, :], pkA, [(wa[:, b, kc * 128:(kc + 1) * 128], qacts[kc][0], [wak, qacts[kc][1]]) for kc in range(3)])
                    self.mm(psB[:, :], pkB, [(wbb[:, b, kc * 128:(kc + 1) * 128], qacts[kc][0], [wbk, qacts[kc][1]]) for kc in range(3)])
                    rtb = rt[j % 2]
                    self.rope_apply(rtb, ("rt", j % 2), psA[:, :], pkA, psB[:, :], pkB, cosb, sinb, t1, t2)
                    for ii in range(4):
                        S.add(DVE, lambda e, rtb=rtb, ii=ii, j=j: e.tensor_copy(out=qh[64:96, 4 * j + ii, :],
                                                                                in_=rtb[32 * ii:32 * ii + 32, :]),
                              reads=[("rt", j % 2)], writes=[("qhr", 4 * j + ii)])
            nkb = 4 * qt + 4
            blocks = [(h, kb) for h in range(16) for kb in range(nkb)]
            info = {}

            def s_ops(h, kb):
                nonlocal pti
                jj = kb - 4 * qt
                c0 = 0 if jj < 0 else jj * 128
                kt = kb // 4
                ps, pk = self.bank()
                S.add(PE, lambda e, ps=ps, h=h, kb=kb, c0=c0: e.matmul(
                    ps[:, c0:TT], lhsT=Kh[0:96, h, kb * 128:(kb + 1) * 128], rhs=qh[0:96, h, c0:TT], start=True, stop=True),
                    reads=[("Kn", h, kt), ("Kr", kt), ("qhn", h), ("qhr", h)], writes=[pk])
                pt = pts[pti % 4]
                ptk = ("pt", pti % 4)
                pti += 1
                S.add(ACT, lambda e, ps=ps, pt=pt, c0=c0: e.activation(out=pt[:, c0:TT], in_=ps[:, c0:TT], func=AF.Exp,
                                                                     scale=SCALE_MLA),
                      reads=[pk], writes=[ptk])
                if jj >= 0:
                    S.add(DVE, lambda e, pt=pt, c0=c0: e.tensor_tensor(out=pt[:, c0:c0 + 128], in0=pt[:, c0:c0 + 128],
                                                                     in1=self.tri_bf[:, :], op=ALU.mult),
                          reads=[ptk, "const"], writes=[ptk])
                info[(h, kb)] = (pt, ptk, c0)

            def pv_ops(h, kb):
                pt, ptk, c0 = info.pop((h, kb))
                hb = (h % 2) * 64
                hc = h // 2
                po, pok = self.held(4 + h % 2)
                pd, pdk = self.held(6 + h % 2)
                S.add(PE, lambda e, po=po, pt=pt, kb=kb, hc=hc, c0=c0: e.matmul(
                    po[:, c0:TT], lhsT=V[:, kb, 2 * hc:2 * hc + 2, :].rearrange("p a d -> p (a d)"), rhs=pt[:, c0:TT],
                    start=(kb == 0), stop=(kb == nkb - 1)),
                    reads=[("V", kb), ptk], writes=[pok])
                S.add(PE, lambda e, pd=pd, pt=pt, kb=kb, c0=c0: e.matmul(
                    pd[:, c0:TT], lhsT=self.ones_bf[:, :], rhs=pt[:, c0:TT], start=(kb == 0), stop=(kb == nkb - 1)),
                    reads=["const", ptk], writes=[pdk])
                if kb == nkb - 1:
                    S.add(ACT, lambda e, pd=pd, hb=hb: e.activation(out=rb[hb:hb + 64, :], in_=pd[hb:hb + 64, :], func=AF.Ln),
                          reads=[pdk], writes=[("rb", h % 2)])
                    S.add(ACT, lambda e, hb=hb: e.activation(out=rb[hb:hb + 64, :], in_=rb[hb:hb + 64, :], func=AF.Exp, scale=-1.0),
                          reads=[("rb", h % 2)], writes=[("rb", h % 2)])
                    S.add(DVE, lambda e, po=po, hb=hb, hc=hc: e.tensor_tensor(out=o[hb:hb + 64, hc, :], in0=po[hb:hb + 64, :],
                                                                            in1=rb[hb:hb + 64, :], op=ALU.mult),
                          reads=[pok, ("rb", h % 2)], writes=[("o", hc)])

            AHEAD = 2
            for i, (h, kb) in enumerate(blocks):
                if i == 0:
                    for a in range(min(AHEAD, len(blocks))):
                        s_ops(*blocks[a])
                pv_ops(h, kb)
                if i + AHEAD < len(blocks):
                    s_ops(*blocks[i + AHEAD])
            load_xnt(qt)
            xacts = [(xnt[:, kc, :], ("xnt", kc)) for kc in range(8)]
            oacts = [(o[:, c, :], ("o", c)) for c in range(8)]
            for c in range(8):
                wb, wk = self.wload(W["w_mla"][c], 1024)
                wg_, wgk = self.wload(W["w_gm"][c], 1024)
                ps, pk = self.bank()
                ps2, pk2 = self.bank()
                self.lin(ps2[:, :], pk2, wg_, wgk, 8, 128, xacts)
                self.lin(ps[:, :], pk, wb, wk, 8, 128, oacts)
                g = gbuf[c % 2]
                gk = ("g", c % 2)
                S.add(ACT, lambda e, ps2=ps2, g=g, c=c: e.activation(out=g, in_=ps2[:, :], func=AF.Sigmoid,
                                                                    bias=self.v("gb_mla", c)),
                      reads=[pk2, "vecs"], writes=[gk])
                S.add(DVE, lambda e, ps=ps, g=g, c=c: e.tensor_tensor(out=mgt[:, c, :], in0=ps[:, :], in1=g, op=ALU.mult),
                      reads=[pk, gk], writes=[("mgt", c)])
            S.add(SP, lambda e, t0=t0: e.dma_start(out=mgh[:, :, t0:t0 + TT], in_=mgt[:, :, :]),
                  reads=[("mgt", c) for c in range(8)], writes=[("mgh", qt)], dma=True)
        AH.release()
        AX.release()
        AH.release()
        S.barrier()

        self.ssd(s, xnt, mgt, xnh, mgh)
        AH.release()
        S.barrier()
        for c in range(8):
            S.add(SP, lambda e, c=c: e.dma_start(out=x[:, c, :], in_=self.xsp[c * 128:(c + 1) * 128, :]),
                  writes=[("x", c, t) for t in range(4)], dma=True)

    def ssd(self, s, xnt, mgt, xnh, mgh):
        S, AH, AX = self.S, self.AH, self.AX
        W = self.w
        NT = SEQ // TT
        AX.mark()
        AH.mark()
        halo = AX.f32([12, 4])
        prevT = AX.f32([1024])
        prevTb = AX.bf16([1024])
        BA = AX.f32([8, TT])
        BB = AX.f32([8, TT])
        BC = AX.f32([8, TT])
        BT = AX.bf16([2, TT])
        CT = AX.bf16([2, TT])
        Btok = AX.bf16([4, 2, 128])
        U1 = AH.bf16([4, 1024])
        U2 = AH.bf16([4, 1024])
        U3 = AH.f32([4, TT])
        U4 = AH.f32([4, 516])
        mg = U1[:, :, :].rearrange("p j (a t) -> p (j a) t", a=2)
        yn = U2[:, :, :].rearrange("p j (a t) -> p (j a) t", a=2)
        sqv = U3[:, :, :].rearrange("p a t -> p (a t)").bitcast(BF16).rearrange("p (c t) -> p c t", c=8)
        sqk = lambda c: ("U3", c // 2)
        adt_rep = U3[:, :, :].rearrange("p a (b l) -> p (a b) l", b=4)
        dtr = AH.f32([4, 16])
        dt = AH.f32([4, 16])
        adt = AH.f32([4, 16])
        cs_sb = AH.f32([4, 16])
        ecl = AH.f32([4, 16])
        d1 = AH.f32([4, 16])
        wst = AH.f32([4, 16])
        Gm = AH.f32([2, 128])
        arg = [AH.f32([4, 128]) for _ in range(2)]
        dec = [AH.f32([4, 128]) for _ in range(2)]
        Mb = [AH.bf16([4, 128]) for _ in range(2)]
        ecs = [AH.f32([4, 128]) for _ in range(2)]
        Cp = [AH.bf16([4, 128]) for _ in range(2)]
        rstd = AH.f32([TT])
        gbuf = [AH.f32([TT]) for _ in range(2)]
        ttmp = AH.f32([TT])
        tri_f = self.cst[:, 0:128]
        ident_f = self.cst[:, 128:256]

        S.add(DVE, lambda e: e.memset(halo[:, :, :], 0.0), writes=[("halo", c) for c in range(12)])
        S.add(DVE, lambda e: e.memset(prevT[:, :], 0.0), writes=[("prevT", 0), ("prevT", 1)])
        S.add(DVE, lambda e: e.memset(prevTb[:, :], 0.0), writes=[("prevTb", 0), ("prevTb", 1)])

        for t in range(NT):
            t0 = t * TT
            S.add(SP, lambda e, t=t: e.dma_start(out=xnt[:, :, :], in_=xnh[:, :, t * TT:(t + 1) * TT]),
                  reads=[("xnh", t)], writes=[("xnt", c) for c in range(8)], dma=True)
            S.add(SP, lambda e, t=t: e.dma_start(out=mgt[:, :, :], in_=mgh[:, :, t * TT:(t + 1) * TT]),
                  reads=[("mgh", t)], writes=[("mgt", c) for c in range(8)], dma=True)
            xacts = [(xnt[:, kc, :], ("xnt", kc)) for kc in range(8)]
            for c in range(8):
                wb, wk = self.wload(W["w_z"][c], 1024)
                ps, pk = self.bank()
                self.lin(ps[:, :], pk, wb, wk, 8, 128, xacts)
                S.add(ACT, lambda e, ps=ps, c=c: e.activation(out=BA[:, c, :], in_=ps[:, :], func=AF.Silu),
                      reads=[pk], writes=[("BA", c)])
            for gq in range(3):
                for i in range(4):
                    c = gq * 4 + i
                    wb, wk = self.wload(W["w_xbc"][c], 1024)
                    ps, pk = self.bank()
                    self.lin(ps[:, :], pk, wb, wk, 8, 128, xacts)
                    S.add(DVE, lambda e, i=i, c=c: e.tensor_copy(out=U4[:, i, 0:3], in_=halo[:, c, 0:3]),
                          reads=[("halo", c)], writes=[("U4", i)])
                    S.add(ACT, lambda e, ps=ps, i=i: e.activation(out=U4[:, i, 3:515], in_=ps[:, :], func=AF.Copy),
                          reads=[pk], writes=[("U4", i)])
                for i in range(4):
                    c = gq * 4 + i
                    S.add(DVE, lambda e, i=i, c=c: e.tensor_scalar(
                        out=U3[:, i, :], in0=U4[:, i, 0:512], scalar1=self.v("conv_w", c * 4 + 0),
                        scalar2=self.v("conv_b", c), op0=ALU.mult, op1=ALU.add),
                        reads=[("U4", i), "vecs"], writes=[("U3", i)])
                for k in range(1, 4):
                    for i in range(4):
                        c = gq * 4 + i
                        S.add(DVE, lambda e, i=i, c=c, k=k: e.scalar_tensor_tensor(
                            out=U3[:, i, :], in0=U4[:, i, k:k + 512], scalar=self.v("conv_w", c * 4 + k),
                            in1=U3[:, i, :], op0=ALU.mult, op1=ALU.add),
                            reads=[("U4", i), ("U3", i), "vecs"], writes=[("U3", i)])
                for i in range(4):
                    c = gq * 4 + i
                    S.add(DVE, lambda e, i=i, c=c: e.tensor_copy(out=halo[:, c, 0:3], in_=U4[:, i, 512:515]),
                          reads=[("U4", i)], writes=[("halo", c)])
                for i in range(4):
                    c = gq * 4 + i
                    if c < 8:
                        dst, dk = BB[:, c, :], ("BB", c)
                    elif c < 10:
                        dst, dk = BT[:, c - 8, :], ("BT", c - 8)
                    else:
                        dst, dk = CT[:, c - 10, :], ("CT", c - 10)
                    S.add(ACT, lambda e, i=i, dst=dst: e.activation(out=dst, in_=U3[:, i, :], func=AF.Silu),
                          reads=[("U3", i)], writes=[dk])
            wdt, wdtk = self.wload(W["w_dt"][0], 128)
            ps, pk = self.bank()
            for j in range(4):
                self.mm(ps[:, j * 16:(j + 1) * 16], pk,
                        [(xnt[:, kc, j * 128:(j + 1) * 128], wdt[:, kc * 16:(kc + 1) * 16], [wdtk, ("xnt", kc)])
                         for kc in range(8)])
            S.add(DVE, lambda e, ps=ps: e.tensor_tensor(
                out=dtr[:, :, :], in0=ps[:, 0:64].rearrange("p (j h) -> p j h", j=4),
                in1=self.v("dt_bias", 0, 16).unsqueeze(1).to_broadcast([128, 4, 16]), op=ALU.add),
                reads=[pk, "vecs"], writes=["dtr"])
            S.add(ACT, lambda e: e.activation(out=dtr[:, :, :], in_=dtr[:, :, :], func=AF.Exp), reads=["dtr"], writes=["dtr"])
            S.add(ACT, lambda e: e.activation(out=dt[:, :, :], in_=dtr[:, :, :], func=AF.Ln, bias=self.ones_f[:, 0:1]),
                  reads=["dtr", "const"], writes=["dt"])
            S.add(DVE, lambda e: e.tensor_tensor(out=adt[:, :, :], in0=dt[:, :, :],
                                                 in1=self.a_rep[:, :].unsqueeze(1).to_broadcast([128, 4, 16]), op=ALU.mult),
                  reads=["dt", "a_rep"], writes=["adt"])
            for j in range(4):
                ps, pk = self.bank()
                self.mm(ps[:, 0:16], pk, [(tri_f, adt[:, j, :], ["adt", "const"])])
                self.mm(ps[:, 16:32], pk, [(self.ones_f[:, :], adt[:, j, :], ["adt", "const"])])
                S.add(DVE, lambda e, ps=ps, j=j: e.tensor_copy(out=cs_sb[:, j, :], in_=ps[:, 0:16]), reads=[pk], writes=[("cs", j)])
                S.add(ACT, lambda e, ps=ps, j=j: e.activation(out=ecl[:, j, :], in_=ps[:, 16:32], func=AF.Exp),
                      reads=[pk], writes=[("ecl", j)])
                S.add(DVE, lambda e, ps=ps, j=j: e.tensor_tensor(out=d1[:, j, :], in0=ps[:, 16:32], in1=cs_sb[:, j, :],
                                                               op=ALU.subtract),
                      reads=[pk, ("cs", j)], writes=[("d1", j)])
                S.add(ACT, lambda e, j=j: e.activation(out=d1[:, j, :], in_=d1[:, j, :], func=AF.Exp),
                      reads=[("d1", j)], writes=[("d1", j)])
                S.add(DVE, lambda e, j=j: e.tensor_tensor(out=wst[:, j, :], in0=d1[:, j, :], in1=dt[:, j, :], op=ALU.mult),
                      reads=[("d1", j), "dt"], writes=[("wst", j)])
            for j in range(4):
                for half in range(2):
                    ps, pk = self.bank()
                    for i in range(4):
                        c = half * 4 + i
                        S.add(PE, lambda e, ps=ps, i=i, c=c, j=j: e.transpose(
                            out=ps[:, i * 128:(i + 1) * 128], in_=BB[:, c, j * 128:(j + 1) * 128], identity=ident_f),
                            reads=[("BB", c), "const"], writes=[pk])
                    psv = ps[:, :].rearrange("p (h d) -> p h d", h=8)
                    S.add(DVE, lambda e, psv=psv, j=j, half=half: e.tensor_tensor(
                        out=U1[:, j, half * 512:(half + 1) * 512].rearrange("p (h d) -> p h d", h=8), in0=psv,
                        in1=dt[:, j, half * 8:(half + 1) * 8].unsqueeze(2).to_broadcast([128, 8, 64]), op=ALU.mult),
                        reads=[pk, "dt"], writes=[("U1", j)])
                    S.add(DVE, lambda e, psv=psv, j=j, half=half: e.tensor_tensor(
                        out=U2[:, j, half * 512:(half + 1) * 512].rearrange("p (h d) -> p h d", h=8), in0=psv,
                        in1=wst[:, j, half * 8:(half + 1) * 8].unsqueeze(2).to_broadcast([128, 8, 64]), op=ALU.mult),
                        reads=[pk, ("wst", j)], writes=[("U2", j)])
            ps, pk = self.bank()
            psb = ps[:, :].bitcast(BF16)
            for j in range(4):
                for g in range(2):
                    S.add(PE, lambda e, psb=psb, j=j, g=g: e.transpose(
                        out=psb[:, (j * 2 + g) * 128:(j * 2 + g + 1) * 128], in_=BT[:, g, j * 128:(j + 1) * 128],
                        identity=self.ident_bf[:, :]),
                        reads=[("BT", g), "const"], writes=[pk])
            S.add(ACT, lambda e, psb=psb: e.activation(out=Btok[:, :, :, :].rearrange("p j g n -> p (j g n)"),
                                                       in_=psb[:, 0:1024], func=AF.Copy),
                  reads=[pk], writes=["Btok"])
            for j in range(4):
                cols = slice(j * 128, (j + 1) * 128)
                ps, pk = self.bank()
                for g in range(2):
                    self.mm(ps[:, g * 128:(g + 1) * 128], pk, [(BT[:, g, cols], CT[:, g, cols], [("BT", g), ("CT", g)])])
                S.add(DVE, lambda e, ps=ps: e.tensor_tensor(
                    out=Gm[:, :, :], in0=ps[:, 0:256].rearrange("p (g l) -> p g l", g=2),
                    in1=tri_f.unsqueeze(1).to_broadcast([128, 2, 128]), op=ALU.mult),
                    reads=[pk, "const"], writes=["Gm"])
                S.add(DVE, lambda e, j=j: e.tensor_copy(out=adt_rep, in_=adt[:, j, :].unsqueeze(2).to_broadcast([128, 16, 128])),
                      reads=["adt"], writes=[("U3", i) for i in range(4)])
                for hq in range(4):
                    h0 = hq * 4
                    g = hq // 2
                    b = hq % 2
                    pc, pck = self.bank()
                    for i in range(4):
                        self.mm(pc[:, i * 128:(i + 1) * 128], pck,
                                [(adt_rep[:, h0 + i, :], tri_f, [("U3", (h0 + i) // 4), "const"])])
                    pcv = pc[:, :].rearrange("p (i l) -> p i l", i=4)
                    S.add(DVE, lambda e, pcv=pcv, b=b, j=j, h0=h0: e.tensor_tensor(
                        out=arg[b][:, :, :], in0=pcv,
                        in1=cs_sb[:, j, h0:h0 + 4].unsqueeze(2).to_broadcast([128, 4, 128]), op=ALU.subtract),
                        reads=[pck, ("cs", j)], writes=[("arg", b)])
                    S.add(DVE, lambda e, b=b: e.tensor_tensor(
                        out=arg[b][:, :, :], in0=arg[b][:, :, :],
                        in1=tri_f.unsqueeze(1).to_broadcast([128, 4, 128]), op=ALU.mult),
                        reads=[("arg", b), "const"], writes=[("arg", b)])
                    S.add(ACT, lambda e, b=b: e.activation(out=dec[b][:, :, :], in_=arg[b][:, :, :], func=AF.Exp),
                          reads=[("arg", b)], writes=[("dec", b)])
                    S.add(DVE, lambda e, b=b, g=g: e.tensor_tensor(
                        out=Mb[b][:, :, :], in0=dec[b][:, :, :],
                        in1=Gm[:, g, :].unsqueeze(1).to_broadcast([128, 4, 128]), op=ALU.mult),
                        reads=[("dec", b), "Gm"], writes=[("Mb", b)])
                    S.add(ACT, lambda e, b=b, pcv=pcv: e.activation(out=ecs[b][:, :, :], in_=pcv, func=AF.Exp),
                          reads=[pck], writes=[("ecs", b)])
                    S.add(DVE, lambda e, b=b, g=g, cols=cols: e.tensor_tensor(
                        out=Cp[b][:, :, :], in0=ecs[b][:, :, :],
                        in1=CT[:, g, cols].unsqueeze(1).to_broadcast([128, 4, 128]), op=ALU.mult),
                        reads=[("ecs", b), ("CT", g)], writes=[("Cp", b)])
                    yb, ybk = self.held(4 + hq // 2)
                    for i in range(4):
                        h = h0 + i
                        hb = (h % 2) * 64
                        pl = (h // 2) % 4
                        self.mm(yb[hb:hb + 64, pl * 128:(pl + 1) * 128], ybk,
                                [(U1[:, j, h * 64:(h + 1) * 64], Mb[b][:, i, :], [("U1", j), ("Mb", b)]),
                                 (prevTb[:, h * 64:(h + 1) * 64], Cp[b][:, i, :], [("prevTb", g), ("Cp", b)])])
                    if hq % 2 == 1:
                        for pl in range(4):
                            c = (hq // 2) * 4 + pl
                            S.add(DVE, lambda e, yb=yb, pl=pl, c=c, cols=cols: e.scalar_tensor_tensor(
                                out=BC[:, c, cols], in0=BB[:, c, cols], scalar=self.v("d_skip", c),
                                in1=yb[:, pl * 128:(pl + 1) * 128], op0=ALU.mult, op1=ALU.add),
                                reads=[ybk, ("BB", c), "vecs"], writes=[("BC", c)])
                for g in range(2):
                    ps, pk = self.bank()
                    self.mm(ps[:, :], pk, [(Btok[:, j, g, :], U2[:, j, g * 512:(g + 1) * 512], ["Btok", ("U2", j)])])
                    pv = prevT[:, g * 512:(g + 1) * 512]
                    S.add(DVE, lambda e, pv=pv, j=j, g=g: e.tensor_tensor(
                        out=pv.rearrange("p (h d) -> p h d", h=8), in0=pv.rearrange("p (h d) -> p h d", h=8),
                        in1=ecl[:, j, g * 8:(g + 1) * 8].unsqueeze(2).to_broadcast([128, 8, 64]), op=ALU.mult),
                        reads=[("prevT", g), ("ecl", j)], writes=[("prevT", g)])
                    S.add(DVE, lambda e, pv=pv, ps=ps: e.tensor_tensor(out=pv, in0=ps[:, :], in1=pv, op=ALU.add),
                          reads=[pk, ("prevT", g)], writes=[("prevT", g)])
                    S.add(ACT, lambda e, pv=pv, g=g: e.activation(out=prevTb[:, g * 512:(g + 1) * 512], in_=pv, func=AF.Copy),
                          reads=[("prevT", g)], writes=[("prevTb", g)])
            for c in range(8):
                S.add(DVE, lambda e, c=c: e.tensor_tensor(out=BC[:, c, :], in0=BC[:, c, :], in1=BA[:, c, :], op=ALU.mult),
                      reads=[("BC", c), ("BA", c)], writes=[("BC", c)])
            for g in range(2):
                self.rms_rstd([(BC[:, 4 * g + i, :], ("BC", 4 * g + i)) for i in range(4)], 512, sqv, sqk, rstd, "rstd")
                for i in range(4):
                    c = 4 * g + i
                    self.norm_apply(yn[:, c, :], ("U2", c // 2), BC[:, c, :], ("BC", c), self.v("ssd_norm_g", c), rstd, "rstd")
            yacts = [(yn[:, c, :], ("U2", c // 2)) for c in range(8)]
            for c in range(8):
                wb, wk = self.wload(W["w_ssd"][c], 1024)
                wg_, wgk = self.wload(W["w_gs"][c], 1024)
                ps, pk = self.bank()
                ps2, pk2 = self.bank()
                self.lin(ps2[:, :], pk2, wg_, wgk, 8, 128, xacts)
                self.lin(ps[:, :], pk, wb, wk, 8, 128, yacts)
                gb = gbuf[c % 2]
                gk = ("g", c % 2)
                S.add(ACT, lambda e, ps2=ps2, gb=gb, c=c: e.activation(out=gb, in_=ps2[:, :], func=AF.Sigmoid,
                                                                      bias=self.v("gb_ssd", c)),
                      reads=[pk2, "vecs"], writes=[gk])
                S.add(DVE, lambda e, ps=ps, gb=gb: e.tensor_tensor(out=ttmp, in0=ps[:, :], in1=gb, op=ALU.mult),
                      reads=[pk, gk], writes=["ttmp"])
                S.add(DVE, lambda e, c=c: e.tensor_tensor(out=mg[:, c, :], in0=ttmp, in1=mgt[:, c, :], op=ALU.add),
                      reads=["ttmp", ("mgt", c)], writes=[("U1", c // 2)])
            macts = [(mg[:, c, :], ("U1", c // 2)) for c in range(8)]
            for c in range(8):
                wb, wk = self.wload(W["w_out"][c], 1024)
                ps, pk = self.bank()
                self.lin(ps[:, :], pk, wb, wk, 8, 128, macts)
                S.add(ACT, lambda e, ps=ps, c=c: e.activation(out=BA[:, c, :], in_=ps[:, :], func=AF.Copy),
                      reads=[pk], writes=[("BA", c)])
            self.rms_rstd([(BA[:, c, :], ("BA", c)) for c in range(8)], D, sqv, sqk, rstd, "rstd")
            xspv = self.xsp.rearrange("(c p) t -> p c t", p=128)
            S.add(SP, lambda e, t0=t0: e.dma_start(out=BB[:, :, :], in_=xspv[:, :, t0:t0 + TT]),
                  reads=[("xsp", c, t) for c in range(8)], writes=[("BB", c) for c in range(8)], dma=True)
            for c in range(8):
                self.norm_apply(BA[:, c, :], ("BA", c), BA[:, c, :], ("BA", c), self.v("mix_post_g", c), rstd, "rstd")
                S.add(DVE, lambda e, c=c: e.tensor_tensor(out=BB[:, c, :], in0=BB[:, c, :], in1=BA[:, c, :], op=ALU.add),
                      reads=[("BB", c), ("BA", c)], writes=[("BB", c)])
            S.add(SP, lambda e, t0=t0: e.dma_start(out=xspv[:, :, t0:t0 + TT], in_=BB[:, :, :]),
                  reads=[("BB", c) for c in range(8)], writes=[("xsp", c, t) for c in range(8)], dma=True)
        AH.release()
        AX.release()

    def xattn(self, s):
        S, AH = self.S, self.AH
        x = self.x
        W = self.w
        NT = SEQ // TT
        AH.mark()
        memf = AH.f32([8, MEM])
        memn = AH.bf16([8, MEM])
        kx = AH.bf16([8, MEM])
        vx = AH.bf16([2, 1024])
        sq = AH.bf16([8, TT])
        rstd = AH.f32([TT])
        xnt = AH.bf16([8, TT])
        qx = AH.bf16([8, TT])
        ptx = [AH.bf16([TT]) for _ in range(4)]
        rdx = AH.f32([TT])
        ox = AH.bf16([8, TT])
        hx = AH.f32([8, TT])
        S.add(SP, lambda e: e.dma_start(out=memf[:, :, :], in_=self.memT[s].rearrange("(c p) m -> p c m", p=128)),
              writes=[("memf", c) for c in range(8)], dma=True)
        self.rms_rstd([(memf[:, c, :], ("memf", c)) for c in range(8)], D, sq[:, :, 0:MEM], "sq", rstd[:, 0:MEM], "rstd")
        for c in range(8):
            self.norm_apply(memn[:, c, :], ("memn", c), memf[:, c, :], ("memf", c), self.v("mem_norm_g", c), rstd[:, 0:MEM], "rstd")
        macts = [(memn[:, c, :], ("memn", c)) for c in range(8)]
        for c in range(8):
            wb, wk = self.wload(W["w_xk"][c], 1024)
            ps, pk = self.bank()
            self.lin(ps[:, 0:MEM], pk, wb, wk, 8, 128, macts)
            S.add(ACT, lambda e, ps=ps, c=c: e.activation(out=kx[:, c, :], in_=ps[:, 0:MEM], func=AF.Copy),
                  reads=[pk], writes=[("kx", c)])
        for half in range(2):
            wbs = [self.wload(W["w_xv"][half, kp], 1024) for kp in range(4)]
            for mb in range(2):
                ps, pk = self.bank()
                self.mm(ps[:, :], pk, [(memn[:, kc, mb * 128:(mb + 1) * 128],
                                        wbs[kc // 2][0][:, (kc % 2) * 512:(kc % 2 + 1) * 512],
                                        [wbs[kc // 2][1], ("memn", kc)]) for kc in range(8)])
                S.add(ACT, lambda e, ps=ps, mb=mb, half=half: e.activation(
                    out=vx[:, mb, half * 512:(half + 1) * 512], in_=ps[:, :], func=AF.Copy),
                    reads=[pk], writes=[("vx", mb, half)])
        pti = 0
        for t in range(NT):
            t0 = t * TT
            self.rms_rstd([(x[:, c, t0:t0 + TT], ("x", c, t)) for c in range(8)], D, sq, "sq", rstd, "rstd")
            for c in range(8):
                self.norm_apply(xnt[:, c, :], ("xnt", c), x[:, c, t0:t0 + TT], ("x", c, t), self.v("xa_pre_g", c), rstd, "rstd")
            xacts = [(xnt[:, c, :], ("xnt", c)) for c in range(8)]
            for c in range(8):
                wb, wk = self.wload(W["w_xq"][c], 1024)
                ps, pk = self.bank()
                self.lin(ps[:, :], pk, wb, wk, 8, 128, xacts)
                S.add(ACT, lambda e, ps=ps, c=c: e.activation(out=qx[:, c, :], in_=ps[:, :], func=AF.Copy),
                      reads=[pk], writes=[("qx", c)])
            for hh in range(4):
                pp = []
                for mb in range(2):
                    ps, pk = self.bank()
                    self.mm(ps[:, :], pk, [(kx[:, 2 * hh + kc, mb * 128:(mb + 1) * 128], qx[:, 2 * hh + kc, :],
                                            [("kx", 2 * hh + kc), ("qx", 2 * hh + kc)]) for kc in range(2)])
                    pt = ptx[pti % 4]
                    ptk = ("ptx", pti % 4)
                    pti += 1
                    S.add(ACT, lambda e, ps=ps, pt=pt: e.activation(out=pt, in_=ps[:, :], func=AF.Exp, scale=SCALE_XA),
                          reads=[pk], writes=[ptk])
                    pp.append((pt, ptk))
                pd, pdk = self.bank()
                self.mm(pd[:, :], pdk, [(self.ones_bf[:, :], pp[mb][0], [pp[mb][1], "const"]) for mb in range(2)])
                S.add(ACT, lambda e, pd=pd: e.activation(out=rdx, in_=pd[:, :], func=AF.Ln), reads=[pdk], writes=["rdx"])
                S.add(ACT, lambda e: e.activation(out=rdx, in_=rdx, func=AF.Exp, scale=-1.0), reads=["rdx"], writes=["rdx"])
                for dvc in range(2):
                    c = 2 * hh + dvc
                    po, pok = self.bank()
                    self.mm(po[:, :], pok, [(vx[:, mb, c * 128:(c + 1) * 128], pp[mb][0],
                                             [("vx", mb, c // 4), pp[mb][1]]) for mb in range(2)])
                    S.add(DVE, lambda e, po=po, c=c: e.tensor_tensor(out=ox[:, c, :], in0=po[:, :], in1=rdx, op=ALU.mult),
                          reads=[pok, "rdx"], writes=[("ox", c)])
            oacts = [(ox[:, c, :], ("ox", c)) for c in range(8)]
            for c in range(8):
                wb, wk = self.wload(W["w_xo"][c], 1024)
                ps, pk = self.bank()
                self.lin(ps[:, :], pk, wb, wk, 8, 128, oacts)
                S.add(ACT, lambda e, ps=ps, c=c: e.activation(out=hx[:, c, :], in_=ps[:, :], func=AF.Copy),
                      reads=[pk], writes=[("hx", c)])
            self.rms_rstd([(hx[:, c, :], ("hx", c)) for c in range(8)], D, sq, "sq", rstd, "rstd")
            for c in range(8):
                self.norm_apply(hx[:, c, :], ("hx", c), hx[:, c, :], ("hx", c), self.v("xa_post_g", c), rstd, "rstd")
                S.add(DVE, lambda e, c=c, t0=t0: e.tensor_tensor(out=x[:, c, t0:t0 + TT], in0=x[:, c, t0:t0 + TT],
                                                                in1=hx[:, c, :], op=ALU.add),
                      reads=[("hx", c), ("x", c, t)], writes=[("x", c, t)])
        AH.release()
        S.barrier()

    def build(self):
        nc, S = self.nc, self.S
        nseq = self.nseq
        xT = self.inp("xT", [nseq, D, SEQ])
        self.memT = self.inp("memT", [nseq, D, MEM])
        self.pos_d = self.inp("pos", [nseq, SEQ], I32)
        outT = nc.dram_tensor("outT", [nseq, D, SEQ], F32, kind="ExternalOutput").ap()
        self.xsp = nc.dram_tensor("xsp", [D, SEQ], F32, kind="Internal").ap()
        self.xn_hbm = nc.dram_tensor("xn_hbm", [D, SEQ], BF16, kind="Internal").ap()
        self.mg_hbm = nc.dram_tensor("mg_hbm", [D, SEQ], BF16, kind="Internal").ap()
        vecs_d = self.inp("vecs", [128, NV])
        cst_d = self.inp("cst", [128, 256])
        w = {}
        for f in ("ffn1", "ffn2"):
            w[f + "_wg"] = self.inp(f + "_wg", [NFF, 128, 1024])
            w[f + "_wu"] = self.inp(f + "_wu", [NFF, 128, 1024])
            w[f + "_wd"] = self.inp(f + "_wd", [8, 2, 128, 1408])
        for n, shp in (("w_z", [8, 128, 1024]), ("w_xbc", [12, 128, 1024]), ("w_gs", [8, 128, 1024]), ("w_gm", [8, 128, 1024]),
                       ("w_qc", [3, 128, 1024]), ("w_kvc", [2, 128, 1024]), ("w_kra", [1, 128, 1024]), ("w_krb", [1, 128, 1024]),
                       ("w_dt", [1, 128, 128]), ("w_uqn", [8, 128, 384]), ("w_uqa", [4, 128, 384]), ("w_uqb", [4, 128, 384]),
                       ("w_uk", [8, 128, 256]), ("w_uv", [2, 128, 1024]), ("w_ssd", [8, 128, 1024]), ("w_mla", [8, 128, 1024]),
                       ("w_out", [8, 128, 1024]), ("w_xq", [8, 128, 1024]), ("w_xk", [8, 128, 1024]), ("w_xo", [8, 128, 1024]),
                       ("w_xv", [2, 4, 128, 1024])):
            w[n] = self.inp(n, shp)
        self.w = w

        with contextlib.ExitStack() as es:
            sb = lambda n, s, d: es.enter_context(nc.sbuf_tensor(n, s, d))
            self.vecs = sb("vecs_sb", [128, NV], F32)
            self.cst = sb("cst_sb", [128, 256], F32)
            self.ones_bf = sb("ones_bf", [128, 128], BF16)
            self.tri_bf = sb("tri_bf", [128, 128], BF16)
            self.ident_bf = sb("ident_bf", [128, 128], BF16)
            self.ones_f = sb("ones_f", [128, 128], F32)
            self.eps_ap = sb("eps", [128, 1], F32)
            self.a_rep = sb("a_rep", [128, 16], F32)
            self.NWB = 6
            self.wbufs = [sb("wb%d" % i, [128, WSLOT], BF16) for i in range(self.NWB)]
            XW = 8 * SEQ
            ARW = 45 * 1024
            arena = sb("arena", [128, ARW], F32)
            self.x = arena[:, 0:XW].rearrange("p (c t) -> p c t", c=8)
            self.AX = Arena(arena[:, 0:XW], XW)
            self.AH = Arena(arena[:, XW:ARW], ARW - XW)
            self.ps = [es.enter_context(nc.psum_tensor("ps%d" % i, [128, 512], F32)) for i in range(8)]

            S.add(DVE, lambda e: e.memset(self.ones_bf[:, :], 1.0), writes=["ones"])
            S.add(DVE, lambda e: e.memset(self.ones_f[:, :], 1.0), writes=["onesf"])
            S.add(DVE, lambda e: e.memset(self.eps_ap[:, :], EPS), writes=["eps"])
            S.add(SP, lambda e: e.dma_start(out=self.vecs[:, :], in_=vecs_d), writes=["vecs"], dma=True)
            S.add(SP, lambda e: e.dma_start(out=self.cst[:, :], in_=cst_d), writes=["cst"], dma=True)
            S.add(DVE, lambda e: e.tensor_copy(out=self.tri_bf[:, :], in_=self.cst[:, 0:128]), reads=["cst"], writes=["tribf"])
            S.add(DVE, lambda e: e.tensor_copy(out=self.ident_bf[:, :], in_=self.cst[:, 128:256]), reads=["cst"], writes=["idbf"])
            S.add(ACT, lambda e: e.activation(out=self.a_rep[:, :], in_=self.v("a_log", 0, 16), func=AF.Exp),
                  reads=["vecs"], writes=["a_rep"])
            S.add(DVE, lambda e: e.tensor_scalar(out=self.a_rep[:, :], in0=self.a_rep[:, :], scalar1=-1.0, scalar2=None,
                                                 op0=ALU.mult), reads=["a_rep"], writes=["a_rep"])
            S.barrier()
            outs = []
            for s in range(nseq):
                for c in range(8):
                    S.add(SP, lambda e, c=c, s=s: e.dma_start(out=self.x[:, c, :], in_=xT[s, c * 128:(c + 1) * 128, :]),
                          writes=[("x", c, t) for t in range(4)], dma=True)
                if self.stop_after >= 1:
                    self.ffn(w["ffn1_wg"], w["ffn1_wu"], w["ffn1_wd"], "ffn1_pre_g", "ffn1_post_g")
                if self.stop_after >= 2:
                    self.mixer(s)
                if self.stop_after >= 3:
                    self.xattn(s)
                if self.stop_after >= 4:
                    self.ffn(w["ffn2_wg"], w["ffn2_wu"], w["ffn2_wd"], "ffn2_pre_g", "ffn2_post_g")
                for c in range(8):
                    outs.append(S.add(SP, lambda e, c=c, s=s: e.dma_start(
                        out=outT[s, c * 128:(c + 1) * 128, :], in_=self.x[:, c, :]),
                        reads=[("x", c, t) for t in range(4)], dma=True))
                S.barrier()
            S.emit(outs)
        return nc


def prep_shared(inp):
    f = lambda n: np.asarray(inp[n][0], np.float32)
    sh = {}
    vec = np.zeros((128, NV), np.float32)

    def put(name, arr):
        vec[:, VO[name]:VO[name] + arr.shape[1]] = arr

    for n in ("ffn1_pre_g", "ffn1_post_g", "mix_pre_g", "mix_post_g", "xa_pre_g", "mem_norm_g", "xa_post_g",
              "ffn2_pre_g", "ffn2_post_g", "ssd_norm_g", "q_norm_g", "kv_norm_g", "conv_b"):
        put(n, colvec(f(n)))
    put("conv_w", np.transpose(f("conv_w").reshape(4, 12, 128), (2, 1, 0)).reshape(128, 48))
    gb = f("gate_bias")
    put("gb_ssd", colvec(gb[:1024]))
    put("gb_mla", colvec(gb[1024:]))
    put("d_skip", colvec(np.repeat(f("d_skip"), 64)))
    put("dt_bias", np.tile(f("dt_bias")[None, :], (128, 1)))
    put("a_log", np.tile(f("a_log")[None, :], (128, 1)))
    r = np.arange(128) % 32
    inv = (np.float32(10000.0) ** (-(np.arange(0, 32, 2, dtype=np.float32)) / np.float32(32))).astype(np.float32)
    put("invf", inv[r % 16][:, None])
    put("sgn", np.where(r < 16, -1.0, 1.0).astype(np.float32)[:, None])
    sh["vecs"] = vec
    k = np.arange(128)
    tri = (k[:, None] <= k[None, :]).astype(np.float32)
    sh["cst"] = np.ascontiguousarray(np.concatenate([tri, np.eye(128, dtype=np.float32)], axis=1))
    for p in ("ffn1", "ffn2"):
        sh[p + "_wg"] = blockify(f(p + "_w_gate"), 128).reshape(NFF, 128, 1024)
        sh[p + "_wu"] = blockify(f(p + "_w_up"), 128).reshape(NFF, 128, 1024)
        sh[p + "_wd"] = np.ascontiguousarray(
            blockify(f(p + "_w_down"), 128).reshape(8, 128, 2, 1408).transpose(0, 2, 1, 3))
    win = f("w_in")
    b128 = lambda m: blockify(np.ascontiguousarray(m), 128).reshape(m.shape[1] // 128, 128, -1)
    sh["w_z"] = b128(win[:, 0:1024])
    sh["w_xbc"] = b128(win[:, 1024:2560])
    sh["w_dt"] = blockify(np.ascontiguousarray(win[:, 2560:2576]), 16).reshape(1, 128, 128)
    sh["w_qc"] = b128(win[:, 2576:2960])
    sh["w_kvc"] = b128(win[:, 2960:3216])
    kr = win[:, 3216:3248]
    sh["w_kra"] = b128(np.tile(kr, (1, 4)))
    sh["w_krb"] = b128(np.tile(np.concatenate([kr[:, 16:32], kr[:, 0:16]], axis=1), (1, 4)))
    sh["w_gs"] = b128(win[:, 3248:4272])
    sh["w_gm"] = b128(win[:, 4272:5296])
    wuq = f("w_uq").reshape(384, 16, 96)
    sh["w_uqn"] = b128(wuq[:, :, 0:64].reshape(384, 1024))
    sh["w_uqa"] = b128(wuq[:, :, 64:96].reshape(384, 512))
    sh["w_uqb"] = b128(np.concatenate([wuq[:, :, 80:96], wuq[:, :, 64:80]], axis=2).reshape(384, 512))
    sh["w_uk"] = b128(f("w_uk"))
    sh["w_uv"] = blockify(f("w_uv"), 512).reshape(2, 128, 1024)
    for n, src in (("w_ssd", "w_ssd_proj"), ("w_mla", "w_mla_proj"), ("w_out", "w_out"), ("w_xq", "w_xq"),
                   ("w_xk", "w_xk"), ("w_xo", "w_xo")):
        sh[n] = b128(f(src))
    sh["w_xv"] = np.ascontiguousarray(blockify(f("w_xv"), 512).reshape(2, 128, 4, 1024).transpose(0, 2, 1, 3))
    return sh


def run(inp, nseq, cores, stop_after=99):
    k = K(nseq, stop_after)
    nc = k.build()
    sh = prep_shared(inp)
    maps = []
    for ci in range(cores):
        m = dict(sh)
        sl = slice(ci * nseq, (ci + 1) * nseq)
        m["xT"] = np.ascontiguousarray(np.transpose(np.asarray(inp["x"][sl], np.float32), (0, 2, 1)))
        m["memT"] = np.ascontiguousarray(np.transpose(np.asarray(inp["mem"][sl], np.float32), (0, 2, 1)))
        m["pos"] = np.ascontiguousarray(np.asarray(inp["positions"][sl], np.int32))
        maps.append({n: m[n] for n in k.din})
    res = run_bass_kernel_spmd(nc, maps, core_ids=list(range(cores)))
    outs = [np.transpose(r["outT"], (0, 2, 1)) for r in res.results]
    return np.ascontiguousarray(np.concatenate(outs, axis=0)).astype(np.float32)


def kernel(**inputs):
    inp = {k: np.asarray(v) for k, v in inputs.items()}
    return run(inp, 2, NCORES)
```

```python
import contextlib
import numpy as np
import concourse.bass as bass
import concourse.mybir as mybir
from concourse.bass_utils import run_bass_kernel_spmd

F32 = mybir.dt.float32
BF16 = mybir.dt.bfloat16
I32 = mybir.dt.int32
AF = mybir.ActivationFunctionType
ALU = mybir.AluOpType
PE, ACT, DVE, POOL, SP = "tensor", "scalar", "vector", "gpsimd", "sync"

D = 1024
SEQ = 2048
TT = 512
DFF = 2816
NFF = DFF // 128
MEM = 256
EPS = 1e-6
NCORES = 8


class Op:
    __slots__ = ("eng", "fn", "deps", "alldeps", "signal", "count", "dma", "dsem", "dval", "prev_dval",
                 "cost", "lat", "seq", "bar", "t_end", "done")

    def __init__(self, eng, fn, dma):
        self.eng = eng
        self.fn = fn
        self.deps = []
        self.alldeps = []
        self.signal = False
        self.count = 0
        self.dma = dma
        self.dsem = None
        self.dval = 0
        self.prev_dval = 0
        self.cost = 0.0
        self.lat = 0.0
        self.seq = 0
        self.bar = False
        self.t_end = 0.0
        self.done = False


DEF_COST = {PE: 0.22, ACT: 0.55, DVE: 0.6, POOL: 1.0, SP: 0.2}


class Sched:
    NPOOL = 12
    import os as _os
    WINDOW = int(_os.environ.get('SWIN', '48'))

    def __init__(self, nc):
        self.nc = nc
        self.ops = {PE: [], ACT: [], DVE: [], POOL: [], SP: []}
        self.lastw = {}
        self.readers = {}
        self.seq = 0
        self.seg_dmas = []
        import os
        self.reorder = os.environ.get('NOREORDER') is None

    def add(self, eng, fn, reads=(), writes=(), dma=False, cost=None):
        op = Op(eng, fn, dma)
        self.seq += 1
        op.seq = self.seq
        if dma:
            op.cost = 1.2 if eng == POOL else 0.15
            op.lat = 3.0 if cost is None else cost
        else:
            op.cost = DEF_COST[eng] if cost is None else cost
        deps = {}
        for k in reads:
            w = self.lastw.get(k)
            if w is not None:
                deps[id(w)] = w
            if isinstance(k, tuple) and k[0] == "ps":
                for r in self.readers.get(k, ()):
                    if r.eng != eng:
                        deps[id(r)] = r
        for k in writes:
            w = self.lastw.get(k)
            if w is not None:
                deps[id(w)] = w
            for r in self.readers.get(k, ()):
                deps[id(r)] = r
        for d in deps.values():
            if d is op:
                continue
            op.alldeps.append(d)
            if eng == PE and d.eng == PE and not d.dma and not dma:
                continue
            op.deps.append(d)
            if not d.dma:
                d.signal = True
        for k in reads:
            self.readers.setdefault(k, []).append(op)
        for k in writes:
            self.lastw[k] = op
            self.readers[k] = []
        if dma:
            self.seg_dmas.append(op)
        self.ops[eng].append(op)
        return op

    def barrier(self):
        lasts = []
        for e, lst in self.ops.items():
            for o in reversed(lst):
                if o.bar:
                    break
                if o.fn is not None and not o.dma:
                    lasts.append(o)
                    break
        lasts += self.seg_dmas
        self.seg_dmas = []
        for e in self.ops:
            op = Op(e, None, False)
            op.bar = True
            for d in lasts:
                op.deps.append(d)
                if not d.dma:
                    d.signal = True
            self.ops[e].append(op)
        self.lastw = {}
        self.readers = {}

    def schedule(self):
        import os
        self._seng = os.environ.get('SENG').split(',') if os.environ.get('SENG') else None
        engs = list(self.ops.keys())
        segs = {e: [] for e in engs}
        for e in engs:
            cur = []
            for o in self.ops[e]:
                if o.bar:
                    segs[e].append((cur, o))
                    cur = []
                else:
                    cur.append(o)
            segs[e].append((cur, None))
        nseg = len(segs[engs[0]])
        assert all(len(segs[e]) == nseg for e in engs)
        new = {e: [] for e in engs}
        for si in range(nseg):
            pend = {e: list(segs[e][si][0]) for e in engs}
            free = {e: 0.0 for e in engs}
            for e in engs:
                for o in pend[e]:
                    o.done = False
            inseg = set()
            for e in engs:
                for o in pend[e]:
                    inseg.add(id(o))
            total = sum(len(v) for v in pend.values())
            heads = {e: 0 for e in engs}
            while total:
                best = None
                for e in engs:
                    lst = pend[e]
                    h = heads[e]
                    while h < len(lst) and lst[h].done:
                        h += 1
                    heads[e] = h
                    cnt = 0
                    i = h
                    cand = None
                    win = self.WINDOW if (self._seng is None or e in self._seng) else 1
                    while i < len(lst) and cnt < win:
                        o = lst[i]
                        i += 1
                        if o.done:
                            continue
                        cnt += 1
                        ok = True
                        rdy = 0.0
                        for d in o.alldeps:
                            if id(d) in inseg:
                                if not d.done:
                                    ok = False
                                    break
                                t = d.t_end + (0.0 if d.eng == e and not d.dma else 0.12)
                                if t > rdy:
                                    rdy = t
                        if not ok:
                            continue
                        st = rdy if rdy > free[e] else free[e]
                        if cand is None or st < cand[0] - 1e-9:
                            cand = (st, o)
                        if st <= free[e] + 1e-9:
                            break
                    if cand is not None and (best is None or cand[0] < best[0] - 1e-9):
                        best = (cand[0], cand[1], e)
                assert best is not None, "scheduler stuck"
                st, o, e = best
                o.done = True
                free[e] = st + o.cost
                o.t_end = st + o.cost + o.lat
                new[e].append(o)
                total -= 1
            lasts = []
            for e in engs:
                for o in reversed(pend[e]):
                    pass
                seg_ops = [o for o in new[e] if id(o) in inseg]
                for o in reversed(seg_ops):
                    if o.fn is not None and not o.dma:
                        lasts.append(o)
                        o.signal = True
                        break
                lasts += [o for o in seg_ops if o.dma]
            for e in engs:
                b = segs[e][si][1]
                if b is not None:
                    b.deps = list(lasts)
                    new[e].append(b)
        self.ops = new

    def emit(self, final_waits=()):
        nc = self.nc
        if self.reorder:
            self.schedule()
        for e, lst in self.ops.items():
            rr = 0
            cnt = {}
            for o in lst:
                if o.dma:
                    slot = (e, rr % self.NPOOL)
                    rr += 1
                    o.dsem = slot
                    o.prev_dval = cnt.get(slot, 0)
                    o.dval = o.prev_dval + 16
                    cnt[slot] = o.dval
        with contextlib.ExitStack() as es:
            engsem = {e: es.enter_context(nc.semaphore("s_" + e)) for e in (PE, ACT, DVE, POOL)}
            dsems = {}
            for e in self.ops:
                if any(o.dma for o in self.ops[e]):
                    for i in range(self.NPOOL):
                        dsems[(e, i)] = es.enter_context(nc.semaphore("d_%s_%d" % (e, i)))
            for e, lst in self.ops.items():
                c = 0
                for o in lst:
                    if o.signal and not o.dma:
                        c += 1
                        o.count = c
            block = es.enter_context(nc.Block())

            def run(e, lst, extra):
                def body(eng):
                    waited = {}

                    def need(sem, val):
                        if waited.get(id(sem), 0) >= val:
                            return
                        waited[id(sem)] = val
                        eng.wait_ge(sem, val)

                    for o in lst:
                        for d in o.deps:
                            if d.dma:
                                need(dsems[d.dsem], d.dval)
                            else:
                                need(engsem[d.eng], d.count)
                        if o.dma and o.prev_dval:
                            need(dsems[o.dsem], o.prev_dval)
                        if o.fn is None:
                            continue
                        inst = o.fn(eng)
                        if o.dma:
                            inst.then_inc(dsems[o.dsem], 16)
                        elif o.signal:
                            inst.then_inc(engsem[e], 1)
                    for d in extra:
                        need(dsems[d.dsem], d.dval)
                return body

            for e in (PE, ACT, DVE, POOL):
                getattr(block, e)(run(e, self.ops[e], ()))
            getattr(block, SP)(run(SP, self.ops[SP], list(final_waits)))


class Arena:
    def __init__(self, ap, nwords):
        self.ap = ap
        self.n = nwords
        self.top = 0
        self.stack = []
        self.peak = 0

    def mark(self):
        self.stack.append(self.top)

    def release(self):
        self.top = self.stack.pop()

    def _take(self, words):
        a = self.top
        self.top += words
        self.peak = max(self.peak, self.top)
        assert self.top <= self.n, ("arena overflow", self.top, self.n)
        return self.ap[:, a:a + words]

    def f32(self, shape):
        n = int(np.prod(shape))
        v = self._take(n)
        if len(shape) == 2:
            v = v.rearrange("p (a b) -> p a b", a=shape[0])
        elif len(shape) == 3:
            v = v.rearrange("p (a b c) -> p a b c", a=shape[0], b=shape[1])
        return v

    def bf16(self, shape):
        n = int(np.prod(shape))
        assert n % 2 == 0
        v = self._take(n // 2).bitcast(BF16)
        if len(shape) == 2:
            v = v.rearrange("p (a b) -> p a b", a=shape[0])
        elif len(shape) == 3:
            v = v.rearrange("p (a b c) -> p a b c", a=shape[0], b=shape[1])
        return v


def blockify(w, mb):
    k, m = w.shape
    assert k % 128 == 0 and m % mb == 0
    return np.ascontiguousarray(w.reshape(k // 128, 128, m // mb, mb).transpose(2, 1, 0, 3))


def colvec(v):
    return np.ascontiguousarray(v.reshape(-1, 128).T)


VEC_SPEC = [("ffn1_pre_g", 8), ("ffn1_post_g", 8), ("mix_pre_g", 8), ("mix_post_g", 8), ("xa_pre_g", 8),
            ("mem_norm_g", 8), ("xa_post_g", 8), ("ffn2_pre_g", 8), ("ffn2_post_g", 8),
            ("ssd_norm_g", 8), ("q_norm_g", 3), ("kv_norm_g", 2), ("conv_w", 48), ("conv_b", 12),
            ("gb_ssd", 8), ("gb_mla", 8), ("d_skip", 8), ("dt_bias", 16), ("a_log", 16),
            ("invf", 1), ("sgn", 1)]
VO = {}
_o = 0
for _n, _c in VEC_SPEC:
    VO[_n] = _o
    _o += _c
NV = _o
WSLOT = 1408
PI = float(np.pi)
SCALE_MLA = 96.0 ** -0.5
SCALE_XA = 256.0 ** -0.5


class K:
    def __init__(self, nseq, stop_after=99):
        self.nseq = nseq
        self.stop_after = stop_after
        self.nc = bass.Bass("TRN2", target_bir_lowering=False)
        self.S = Sched(self.nc)
        self.din = {}
        self.bank_i = 0
        self.wb_i = 0

    def inp(self, name, shape, dtype=F32):
        t = self.nc.dram_tensor(name, list(shape), dtype, kind="ExternalInput").ap()
        self.din[name] = t
        return t

    def bank(self):
        i = self.bank_i % 4
        self.bank_i += 1
        return self.ps[i], ("ps", i)

    def held(self, i):
        return self.ps[i], ("ps", i)

    def v(self, name, c=0, n=1):
        o = VO[name] + c
        return self.vecs[:, o:o + n]

    def wload(self, src, nelem):
        i = self.wb_i % self.NWB
        self.wb_i += 1
        key = ("wb", i)
        dst = self.wbufs[i][:, 0:nelem]
        self.S.add(POOL, lambda e: e.dma_start(out=dst, in_=src, max_dma_last_dim=4096),
                   writes=[key], dma=True)
        return dst, key

    def wload_multi(self, src, nb, n):
        i = self.wb_i % self.NWB
        self.wb_i += 1
        key = ("wb", i)
        dst = self.wbufs[i][:, 0:nb * n].rearrange("p (b n) -> p b n", b=nb)
        self.S.add(POOL, lambda e: e.dma_start(out=dst, in_=src.rearrange("b p n -> p b n"), max_dma_last_dim=4096),
                   writes=[key], dma=True)
        return dst, key

    def mm(self, out, pk, pairs):
        n = len(pairs)
        for i, (l, r, ks) in enumerate(pairs):
            self.S.add(PE, lambda e, l=l, r=r, i=i: e.matmul(out, lhsT=l, rhs=r, start=(i == 0), stop=(i == n - 1)),
                       reads=ks, writes=[pk])

    def lin(self, out, pk, wb, wk, kcn, mb, acts, m0=0, m=128):
        self.mm(out, pk, [(wb[:, kc * mb + m0:kc * mb + m0 + m], acts[kc][0], [wk, acts[kc][1]]) for kc in range(kcn)])

    def rms_rstd(self, srcs, nfeat, sq, sqname, rstd, rkey):
        S = self.S
        fc = len(srcs)
        T = rstd.shape[-1]
        sk = sqname if callable(sqname) else (lambda c: (sqname, c))
        for c, (ap, k) in enumerate(srcs):
            S.add(ACT, lambda e, c=c, ap=ap: e.activation(out=sq[:, c, :], in_=ap, func=AF.Square),
                  reads=[k], writes=[sk(c)])
        ps, pk = self.bank()
        self.mm(ps[:, 0:T], pk, [(self.ones_bf[:, :], sq[:, c, :], [sk(c), "const"]) for c in range(fc)])
        S.add(ACT, lambda e: e.activation(out=rstd, in_=ps[:, 0:T], func=AF.Ln, scale=1.0 / nfeat,
                                          bias=self.eps_ap[:, 0:1]),
              reads=[pk, "const"], writes=[rkey])
        S.add(ACT, lambda e: e.activation(out=rstd, in_=rstd, func=AF.Exp, scale=-0.5), reads=[rkey], writes=[rkey])

    def norm_apply(self, out, okey, src, skey, gcol, rstd, rkey):
        self.S.add(DVE, lambda e: e.scalar_tensor_tensor(out=out, in0=src, scalar=gcol, in1=rstd,
                                                         op0=ALU.mult, op1=ALU.mult),
                   reads=[skey, rkey, "vecs"], writes=[okey])

    def ffn(self, wg, wu, wd, gpre, gpost):
        S, A = self.S, self.AH
        x = self.x
        HT = 1024
        NH = HT // TT
        NHALF = SEQ // HT
        A.mark()
        xn = A.bf16([8, HT])
        h = A.bf16([NFF, HT])
        y = A.f32([8, HT])
        sq = A.bf16([8, TT])
        rstd = A.f32([TT])
        sg = [A.f32([TT]) for _ in range(2)]
        HF = NFF // 2

        def prenorm(half):
            for tt in range(NH):
                t0 = half * HT + tt * TT
                gt = t0 // TT
                self.rms_rstd([(x[:, c, t0:t0 + TT], ("x", c, gt)) for c in range(8)], D, sq, "sq", rstd, "rstd")
                for c in range(8):
                    self.norm_apply(xn[:, c, tt * TT:(tt + 1) * TT], ("xn", c, tt), x[:, c, t0:t0 + TT], ("x", c, gt),
                                    self.v(gpre, c), rstd, "rstd")

        def gateup(half):
            for fb in range(NFF):
                wgb, wgk = self.wload(wg[fb], 1024)
                wub, wuk = self.wload(wu[fb], 1024)
                for tt in range(NH):
                    acts = [(xn[:, kc, tt * TT:(tt + 1) * TT], ("xn", kc, tt)) for kc in range(8)]
                    pg, pgk = self.bank()
                    pu, puk = self.bank()
                    self.lin(pg[:, :], pgk, wgb, wgk, 8, 128, acts)
                    self.lin(pu[:, :], puk, wub, wuk, 8, 128, acts)
                    sgb = sg[(fb * NH + tt) % 2]
                    sgk = ("sg", (fb * NH + tt) % 2)
                    S.add(ACT, lambda e, pg=pg, sgb=sgb: e.activation(out=sgb, in_=pg[:, :], func=AF.Silu),
                          reads=[pgk], writes=[sgk])
                    S.add(DVE, lambda e, pu=pu, sgb=sgb, fb=fb, tt=tt: e.tensor_tensor(
                        out=h[:, fb, tt * TT:(tt + 1) * TT], in0=pu[:, :], in1=sgb, op=ALU.mult),
                        reads=[puk, sgk], writes=[("h", fb, tt)])

        def down(half):
            for dc in range(8):
                wd0, wdk0 = self.wload(wd[dc, 0], HF * 128)
                wd1, wdk1 = self.wload(wd[dc, 1], HF * 128)
                for tt in range(NH):
                    py, pyk = self.bank()
                    pairs = []
                    for fc in range(NFF):
                        wb, wk = (wd0, wdk0) if fc < HF else (wd1, wdk1)
                        f = fc % HF
                        pairs.append((wb[:, f * 128:(f + 1) * 128], h[:, fc, tt * TT:(tt + 1) * TT], [wk, ("h", fc, tt)]))
                    self.mm(py[:, :], pyk, pairs)
                    S.add(ACT, lambda e, py=py, dc=dc, tt=tt: e.activation(
                        out=y[:, dc, tt * TT:(tt + 1) * TT], in_=py[:, :], func=AF.Copy),
                        reads=[pyk], writes=[("y", dc, tt)])

        def postnorm(half):
            for tt in range(NH):
                t0 = half * HT + tt * TT
                gt = t0 // TT
                self.rms_rstd([(y[:, c, tt * TT:(tt + 1) * TT], ("y", c, tt)) for c in range(8)], D, sq, "sq", rstd, "rstd")
                for c in range(8):
                    ysl = y[:, c, tt * TT:(tt + 1) * TT]
                    self.norm_apply(ysl, ("y", c, tt), ysl, ("y", c, tt), self.v(gpost, c), rstd, "rstd")
                    S.add(DVE, lambda e, c=c, ysl=ysl, t0=t0: e.scalar_tensor_tensor(
                        out=x[:, c, t0:t0 + TT], in0=ysl, scalar=0.5, in1=x[:, c, t0:t0 + TT],
                        op0=ALU.mult, op1=ALU.add),
                        reads=[("y", c, tt), ("x", c, gt)], writes=[("x", c, gt)])

        prenorm(0)
        for half in range(NHALF):
            gateup(half)
            if half + 1 < NHALF:
                prenorm(half + 1)
            down(half)
            postnorm(half)
        A.release()
        S.barrier()

    def rope_tables(self, s, t0, bufs, cosb, sinb):
        S = self.S
        posi, posf, ang, kf, tmp = bufs
        S.add(SP, lambda e: e.dma_start(out=posi, in_=self.pos_d[s:s + 1, t0:t0 + TT].partition_broadcast(128)),
              writes=["posi"], dma=True)
        S.add(DVE, lambda e: e.tensor_copy(out=posf, in_=posi), reads=["posi"], writes=["posf"])
        S.add(DVE, lambda e: e.tensor_scalar(out=ang, in0=posf, scalar1=self.v("invf"), scalar2=None, op0=ALU.mult),
              reads=["posf", "vecs"], writes=["ang"])
        ki = posi
        S.add(DVE, lambda e: e.tensor_scalar(out=ki, in0=ang, scalar1=1.0 / (2 * PI), scalar2=None, op0=ALU.mult),
              reads=["ang", "posf"], writes=["posi"])
        S.add(DVE, lambda e: e.tensor_copy(out=kf, in_=ki), reads=["posi"], writes=["kf"])
        S.add(DVE, lambda e: e.scalar_tensor_tensor(out=ang, in0=kf, scalar=-2 * PI, in1=ang, op0=ALU.mult, op1=ALU.add),
              reads=["kf", "ang"], writes=["ang"])
        for which, shift, dst, dk in (("sin", 0.0, sinb, "sinb"), ("cos", PI / 2, cosb, "cosb")):
            y = kf
            S.add(DVE, lambda e, shift=shift: e.tensor_scalar(out=y, in0=ang, scalar1=shift, scalar2=None, op0=ALU.add),
                  reads=["ang"], writes=["kf"])
            S.add(DVE, lambda e: e.tensor_scalar(out=tmp, in0=y, scalar1=PI, scalar2=2 * PI, op0=ALU.is_gt, op1=ALU.mult),
                  reads=["kf"], writes=["ropetmp"])
            S.add(DVE, lambda e: e.tensor_tensor(out=y, in0=y, in1=tmp, op=ALU.subtract),
                  reads=["kf", "ropetmp"], writes=["kf"])
            S.add(DVE, lambda e: e.tensor_scalar(out=tmp, in0=y, scalar1=-PI, scalar2=2 * PI, op0=ALU.is_lt, op1=ALU.mult),
                  reads=["kf"], writes=["ropetmp"])
            S.add(DVE, lambda e: e.tensor_tensor(out=y, in0=y, in1=tmp, op=ALU.add),
                  reads=["kf", "ropetmp"], writes=["kf"])
            S.add(DVE, lambda e: e.tensor_scalar(out=y, in0=y, scalar1=PI, scalar2=-PI, op0=ALU.min, op1=ALU.max),
                  reads=["kf"], writes=["kf"])
            if which == "sin":
                S.add(ACT, lambda e, dst=dst: e.activation(out=dst, in_=y, func=AF.Sin, scale=self.v("sgn")),
                      reads=["kf", "vecs"], writes=[dk])
            else:
                S.add(ACT, lambda e, dst=dst: e.activation(out=dst, in_=y, func=AF.Sin),
                      reads=["kf"], writes=[dk])

    def rope_apply(self, out, okey, psA, pkA, psB, pkB, cosb, sinb, t1, t2):
        S = self.S
        S.add(DVE, lambda e: e.tensor_tensor(out=t1, in0=psA, in1=cosb, op=ALU.mult), reads=[pkA, "cosb"], writes=["posf"])
        S.add(DVE, lambda e: e.tensor_tensor(out=t2, in0=psB, in1=sinb, op=ALU.mult), reads=[pkB, "sinb"], writes=["ropetmp"])
        S.add(DVE, lambda e: e.tensor_tensor(out=out, in0=t1, in1=t2, op=ALU.add), reads=["posf", "ropetmp"], writes=[okey])

    def mixer(self, s):
        S, AH, AX = self.S, self.AH, self.AX
        x = self.x
        W = self.w
        NT = SEQ // TT
        AH.mark()
        xnt = AH.bf16([8, TT])
        mgt = AH.bf16([8, TT])
        xnh = self.xn_hbm.rearrange("(c p) t -> p c t", p=128)
        mgh = self.mg_hbm.rearrange("(c p) t -> p c t", p=128)
        XK = [("xnt", c) for c in range(8)]
        AH.mark()
        sq = AH.bf16([8, TT])
        rstd = AH.f32([TT])
        for t in range(NT):
            t0 = t * TT
            self.rms_rstd([(x[:, c, t0:t0 + TT], ("x", c, t)) for c in range(8)], D, sq, "sq", rstd, "rstd")
            for c in range(8):
                self.norm_apply(xnt[:, c, :], ("xnt", c), x[:, c, t0:t0 + TT], ("x", c, t),
                                self.v("mix_pre_g", c), rstd, "rstd")
            S.add(SP, lambda e, t0=t0: e.dma_start(out=xnh[:, :, t0:t0 + TT], in_=xnt[:, :, :]),
                  reads=XK, writes=[("xnh", t)], dma=True)
        for c in range(8):
            S.add(SP, lambda e, c=c: e.dma_start(out=self.xsp[c * 128:(c + 1) * 128, :], in_=x[:, c, :]),
                  reads=[("x", c, t) for t in range(NT)], writes=[("xsp", c, t) for t in range(NT)], dma=True)
        AH.release()
        S.barrier()

        def load_xnt(t):
            S.add(SP, lambda e, t=t: e.dma_start(out=xnt[:, :, :], in_=xnh[:, :, t * TT:(t + 1) * TT]),
                  reads=[("xnh", t)], writes=XK, dma=True)

        AH.mark()
        AX.mark()
        Kh = AX.bf16([16, SEQ])
        qn = AH.bf16([3, SEQ])
        V = AH.bf16([16, 16, 64])
        cosb = AH.f32([TT])
        sinb = AH.f32([TT])
        rbufs = (AH.f32([TT]).bitcast(I32), AH.f32([TT]), AH.f32([TT]), AH.f32([TT]), AH.f32([TT]))
        t1 = rbufs[1]
        t2 = rbufs[4]
        rt = [AH.bf16([TT]) for _ in range(2)]
        AH.mark()
        kvc = AH.f32([2, TT])
        qc = AH.f32([3, TT])
        sq = AH.bf16([3, TT])
        rstd = AH.f32([TT])
        kvn = AH.bf16([2, TT])
        for t in range(NT):
            t0 = t * TT
            load_xnt(t)
            acts = [(xnt[:, kc, :], ("xnt", kc)) for kc in range(8)]
            self.rope_tables(s, t0, rbufs, cosb, sinb)
            for c in range(2):
                wb, wk = self.wload(W["w_kvc"][c], 1024)
                ps, pk = self.bank()
                self.lin(ps[:, :], pk, wb, wk, 8, 128, acts)
                S.add(ACT, lambda e, ps=ps, c=c: e.activation(out=kvc[:, c, :], in_=ps[:, :], func=AF.Copy),
                      reads=[pk], writes=[("kvc", c)])
            self.rms_rstd([(kvc[:, c, :], ("kvc", c)) for c in range(2)], 256, sq, "sq", rstd, "rstd")
            for c in range(2):
                self.norm_apply(kvn[:, c, :], ("kvn", c), kvc[:, c, :], ("kvc", c), self.v("kv_norm_g", c), rstd, "rstd")
            kacts = [(kvn[:, kc, :], ("kvn", kc)) for kc in range(2)]
            for i in range(2):
                wb, wk = self.wload_multi(W["w_uk"][4 * i:4 * i + 4], 4, 256)
                for b in range(4):
                    c = 4 * i + b
                    ps, pk = self.bank()
                    self.mm(ps[:, :], pk, [(wb[:, b, kc * 128:(kc + 1) * 128], kacts[kc][0], [wk, kacts[kc][1]]) for kc in range(2)])
                    S.add(ACT, lambda e, ps=ps, c=c, t0=t0: e.activation(out=Kh[0:64, 2 * c, t0:t0 + TT], in_=ps[0:64, :], func=AF.Copy),
                          reads=[pk], writes=[("Kn", 2 * c, t)])
                    S.add(DVE, lambda e, ps=ps, c=c, t0=t0: e.tensor_copy(out=Kh[0:64, 2 * c + 1, t0:t0 + TT], in_=ps[64:128, :]),
                          reads=[pk], writes=[("Kn", 2 * c + 1, t)])
            for half in range(2):
                wb, wk = self.wload(W["w_uv"][half], 1024)
                for j in range(4):
                    ps, pk = self.bank()
                    self.mm(ps[:, :], pk, [(kvn[:, kc, j * 128:(j + 1) * 128], wb[:, kc * 512:(kc + 1) * 512], [wk, ("kvn", kc)]) for kc in range(2)])
                    blk = t * 4 + j
                    S.add(DVE, lambda e, ps=ps, blk=blk, half=half: e.tensor_copy(
                        out=V[:, blk, half * 8:(half + 1) * 8, :], in_=ps[:, :].rearrange("p (h d) -> p h d", h=8)),
                        reads=[pk], writes=[("V", blk)])
            wa, wak = self.wload(W["w_kra"][0], 1024)
            wbb, wbk = self.wload(W["w_krb"][0], 1024)
            psA, pkA = self.bank()
            psB, pkB = self.bank()
            self.lin(psA[:, :], pkA, wa, wak, 8, 128, acts)
            self.lin(psB[:, :], pkB, wbb, wbk, 8, 128, acts)
            self.rope_apply(rt[0], ("rt", 0), psA[:, :], pkA, psB[:, :], pkB, cosb, sinb, t1, t2)
            S.add(DVE, lambda e, t0=t0: e.tensor_copy(out=Kh[64:96, :, t0:t0 + TT],
                                                      in_=rt[0][64:96, :].unsqueeze(1).to_broadcast([32, 16, TT])),
                  reads=[("rt", 0)], writes=[("Kr", t)])
            for c in range(3):
                wb, wk = self.wload(W["w_qc"][c], 1024)
                ps, pk = self.bank()
                self.lin(ps[:, :], pk, wb, wk, 8, 128, acts)
                S.add(ACT, lambda e, ps=ps, c=c: e.activation(out=qc[:, c, :], in_=ps[:, :], func=AF.Copy),
                      reads=[pk], writes=[("qc", c)])
            self.rms_rstd([(qc[:, c, :], ("qc", c)) for c in range(3)], 384, sq, "sq", rstd, "rstd")
            for c in range(3):
                self.norm_apply(qn[:, c, t0:t0 + TT], ("qn", c, t), qc[:, c, :], ("qc", c), self.v("q_norm_g", c), rstd, "rstd")
        AH.release()
        S.barrier()

        AH.mark()
        qh = AH.bf16([16, TT])
        pts = [AH.bf16([TT]) for _ in range(4)]
        o = AH.bf16([8, TT])
        rb = AH.f32([TT])
        gbuf = [AH.f32([TT]) for _ in range(2)]
        pti = 0
        for qt in range(NT):
            t0 = qt * TT
            self.rope_tables(s, t0, rbufs, cosb, sinb)
            qacts = [(qn[:, kc, t0:t0 + TT], ("qn", kc, qt)) for kc in range(3)]
            for (b0, nb) in ((0, 3), (3, 3), (6, 2)):
                wb, wk = self.wload_multi(W["w_uqn"][b0:b0 + nb], nb, 384)
                for b in range(nb):
                    c = b0 + b
                    ps, pk = self.bank()
                    self.mm(ps[:, :], pk, [(wb[:, b, kc * 128:(kc + 1) * 128], qacts[kc][0], [wk, qacts[kc][1]]) for kc in range(3)])
                    S.add(ACT, lambda e, ps=ps, c=c: e.activation(out=qh[0:64, 2 * c, :], in_=ps[0:64, :], func=AF.Copy),
                          reads=[pk], writes=[("qhn", 2 * c)])
                    S.add(DVE, lambda e, ps=ps, c=c: e.tensor_copy(out=qh[0:64, 2 * c + 1, :], in_=ps[64:128, :]),
                          reads=[pk], writes=[("qhn", 2 * c + 1)])
            for i in range(2):
                wa, wak = self.wload_multi(W["w_uqa"][2 * i:2 * i + 2], 2, 384)
                wbb, wbk = self.wload_multi(W["w_uqb"][2 * i:2 * i + 2], 2, 384)
                for b in range(2):
                    j = 2 * i + b
                    psA, pkA = self.bank()
                    psB, pkB = self.bank()
                    self.mm(psA[:, :], pkA, [(wa[:, b, kc * 128:(kc + 1) * 128], qacts[kc][0], [wak, qacts[kc][1]]) for kc in range(3)])
                    self.mm(psB[:, :], pkB, [(wbb[:, b, kc * 128:(kc + 1) * 128], qacts[kc][0], [wbk, qacts[kc][1]]) for kc in range(3)])
                    rtb = rt[j % 2]
                    self.rope_apply(rtb, ("rt", j % 2), psA[:, :], pkA, psB[:, :], pkB, cosb, sinb, t1, t2)
                    for ii in range(4):
                        S.add(DVE, lambda e, rtb=rtb, ii=ii, j=j: e.tensor_copy(out=qh[64:96, 4 * j + ii, :],
                                                                                in_=rtb[32 * ii:32 * ii + 32, :]),
                              reads=[("rt", j % 2)], writes=[("qhr", 4 * j + ii)])
            nkb = 4 * qt + 4
            blocks = [(h, kb) for h in range(16) for kb in range(nkb)]
            info = {}

            def s_ops(h, kb):
                nonlocal pti
                jj = kb - 4 * qt
                c0 = 0 if jj < 0 else jj * 128
                kt = kb // 4
                ps, pk = self.bank()
                S.add(PE, lambda e, ps=ps, h=h, kb=kb, c0=c0: e.matmul(
                    ps[:, c0:TT], lhsT=Kh[0:96, h, kb * 128:(kb + 1) * 128], rhs=qh[0:96, h, c0:TT], start=True, stop=True),
                    reads=[("Kn", h, kt), ("Kr", kt), ("qhn", h), ("qhr", h)], writes=[pk])
                pt = pts[pti % 4]
                ptk = ("pt", pti % 4)
                pti += 1
                S.add(ACT, lambda e, ps=ps, pt=pt, c0=c0: e.activation(out=pt[:, c0:TT], in_=ps[:, c0:TT], func=AF.Exp,
                                                                     scale=SCALE_MLA),
                      reads=[pk], writes=[ptk])
                if jj >= 0:
                    S.add(DVE, lambda e, pt=pt, c0=c0: e.tensor_tensor(out=pt[:, c0:c0 + 128], in0=pt[:, c0:c0 + 128],
                                                                     in1=self.tri_bf[:, :], op=ALU.mult),
                          reads=[ptk, "const"], writes=[ptk], cost=0.25)
                    if c0 > 0:
                        S.add(DVE, lambda e, pt=pt, c0=c0: e.memset(pt[:, 0:c0], 0.0), writes=[ptk], cost=0.15)
                info[(h, kb)] = (pt, ptk, c0)

            def pv_ops(h, kb):
                pt, ptk, c0 = info.pop((h, kb))
                hb = (h % 2) * 64
                hc = h // 2
                po, pok = self.held(4 + h % 2)
                pd, pdk = self.held(6 + h % 2)
                S.add(PE, lambda e, po=po, pt=pt, kb=kb, hc=hc: e.matmul(
                    po[:, :], lhsT=V[:, kb, 2 * hc:2 * hc + 2, :].rearrange("p a d -> p (a d)"), rhs=pt[:, :],
                    start=(kb == 0), stop=(kb == nkb - 1)),
                    reads=[("V", kb), ptk], writes=[pok])
                S.add(PE, lambda e, pd=pd, pt=pt, kb=kb: e.matmul(
                    pd[:, :], lhsT=self.ones_bf[:, :], rhs=pt[:, :], start=(kb == 0), stop=(kb == nkb - 1)),
                    reads=["const", ptk], writes=[pdk])
                if kb == nkb - 1:
                    S.add(ACT, lambda e, pd=pd, hb=hb: e.activation(out=rb[hb:hb + 64, :], in_=pd[hb:hb + 64, :], func=AF.Ln),
                          reads=[pdk], writes=[("rb", h % 2)])
                    S.add(ACT, lambda e, hb=hb: e.activation(out=rb[hb:hb + 64, :], in_=rb[hb:hb + 64, :], func=AF.Exp, scale=-1.0),
                          reads=[("rb", h % 2)], writes=[("rb", h % 2)])
                    S.add(DVE, lambda e, po=po, hb=hb, hc=hc: e.tensor_tensor(out=o[hb:hb + 64, hc, :], in0=po[hb:hb + 64, :],
                                                                            in1=rb[hb:hb + 64, :], op=ALU.mult),
                          reads=[pok, ("rb", h % 2)], writes=[("o", hc)])

            AHEAD = 2
            for i, (h, kb) in enumerate(blocks):
                if i == 0:
                    for a in range(min(AHEAD, len(blocks))):
                        s_ops(*blocks[a])
                pv_ops(h, kb)
                if i + AHEAD < len(blocks):
                    s_ops(*blocks[i + AHEAD])
            load_xnt(qt)
            xacts = [(xnt[:, kc, :], ("xnt", kc)) for kc in range(8)]
            oacts = [(o[:, c, :], ("o", c)) for c in range(8)]
            for c in range(8):
                wb, wk = self.wload(W["w_mla"][c], 1024)
                wg_, wgk = self.wload(W["w_gm"][c], 1024)
                ps, pk = self.bank()
                ps2, pk2 = self.bank()
                self.lin(ps2[:, :], pk2, wg_, wgk, 8, 128, xacts)
                self.lin(ps[:, :], pk, wb, wk, 8, 128, oacts)
                g = gbuf[c % 2]
                gk = ("g", c % 2)
                S.add(ACT, lambda e, ps2=ps2, g=g, c=c: e.activation(out=g, in_=ps2[:, :], func=AF.Sigmoid,
                                                                    bias=self.v("gb_mla", c)),
                      reads=[pk2, "vecs"], writes=[gk])
                S.add(DVE, lambda e, ps=ps, g=g, c=c: e.tensor_tensor(out=mgt[:, c, :], in0=ps[:, :], in1=g, op=ALU.mult),
                      reads=[pk, gk], writes=[("mgt", c)])
            S.add(SP, lambda e, t0=t0: e.dma_start(out=mgh[:, :, t0:t0 + TT], in_=mgt[:, :, :]),
                  reads=[("mgt", c) for c in range(8)], writes=[("mgh", qt)], dma=True)
        AH.release()
        AX.release()
        AH.release()
        S.barrier()

        self.ssd(s, xnt, mgt, xnh, mgh)
        AH.release()
        S.barrier()
        for c in range(8):
            S.add(SP, lambda e, c=c: e.dma_start(out=x[:, c, :], in_=self.xsp[c * 128:(c + 1) * 128, :]),
                  writes=[("x", c, t) for t in range(4)], dma=True)

    def ssd(self, s, xnt0, mgt0, xnh, mgh):
        S, AH, AX = self.S, self.AH, self.AX
        W = self.w
        NT = SEQ // TT
        AX.mark()
        AH.mark()
        halo = AX.f32([12, 4])
        prevT = AX.f32([1024])
        prevTb = AX.bf16([1024])
        BA = AX.f32([8, TT])
        BB = AX.f32([8, TT])
        BC = AX.f32([8, TT])
        BT = AX.bf16([2, TT])
        CT = AX.bf16([2, TT])
        Btok = AX.bf16([4, 2, 128])
        U1 = AH.bf16([4, 1024])
        U2 = AH.bf16([4, 1024])
        U3 = AH.f32([4, TT])
        U4 = AH.f32([4, 516])
        mg = U1[:, :, :].rearrange("p j (a t) -> p (j a) t", a=2)
        yn = U2[:, :, :].rearrange("p j (a t) -> p (j a) t", a=2)
        sqv = U3[:, :, :].rearrange("p a t -> p (a t)").bitcast(BF16).rearrange("p (c t) -> p c t", c=8)
        sqk = lambda c: ("U3", c // 2)
        adt_rep = U3[:, :, :].rearrange("p a (b l) -> p (a b) l", b=4)
        dtr = AH.f32([4, 16])
        dt = AH.f32([4, 16])
        adt = AH.f32([4, 16])
        cs_sb = AH.f32([4, 16])
        ecl = AH.f32([4, 16])
        d1 = AH.f32([4, 16])
        wst = AH.f32([4, 16])
        Gm = AH.f32([2, 128])
        arg = [AH.f32([4, 128]) for _ in range(2)]
        dec = [AH.f32([4, 128]) for _ in range(2)]
        Mb = [AH.bf16([4, 128]) for _ in range(2)]
        ecs = [AH.f32([4, 128]) for _ in range(2)]
        Cp = [AH.bf16([4, 128]) for _ in range(2)]
        rstd = AH.f32([TT])
        gbuf = [AH.f32([TT]) for _ in range(2)]
        ttmp = AH.f32([TT])
        tri_f = self.cst[:, 0:128]
        ident_f = self.cst[:, 128:256]
        xnt2 = [xnt0, AH.bf16([8, TT])]
        mgt2 = [mgt0, AH.bf16([8, TT])]

        S.add(DVE, lambda e: e.memset(halo[:, :, :], 0.0), writes=[("halo", c) for c in range(12)])
        S.add(DVE, lambda e: e.memset(prevT[:, :], 0.0), writes=[("prevT", 0), ("prevT", 1)])
        S.add(DVE, lambda e: e.memset(prevTb[:, :], 0.0), writes=[("prevTb", 0), ("prevTb", 1)])

        for t in range(NT):
            t0 = t * TT
            pb = t % 2
            xnt, mgt = xnt2[pb], mgt2[pb]
            S.add(SP, lambda e, t=t, xnt=xnt: e.dma_start(out=xnt[:, :, :], in_=xnh[:, :, t * TT:(t + 1) * TT]),
                  reads=[("xnh", t)], writes=[("xnt", pb, c) for c in range(8)], dma=True)
            S.add(SP, lambda e, t=t, mgt=mgt: e.dma_start(out=mgt[:, :, :], in_=mgh[:, :, t * TT:(t + 1) * TT]),
                  reads=[("mgh", t)], writes=[("mgt", pb, c) for c in range(8)], dma=True)
            xacts = [(xnt[:, kc, :], ("xnt", pb, kc)) for kc in range(8)]
            for c in range(8):
                wb, wk = self.wload(W["w_z"][c], 1024)
                ps, pk = self.bank()
                self.lin(ps[:, :], pk, wb, wk, 8, 128, xacts)
                S.add(ACT, lambda e, ps=ps, c=c: e.activation(out=BA[:, c, :], in_=ps[:, :], func=AF.Silu),
                      reads=[pk], writes=[("BA", c)])
            for gq in range(3):
                for i in range(4):
                    c = gq * 4 + i
                    wb, wk = self.wload(W["w_xbc"][c], 1024)
                    ps, pk = self.bank()
                    self.lin(ps[:, :], pk, wb, wk, 8, 128, xacts)
                    S.add(DVE, lambda e, i=i, c=c: e.tensor_copy(out=U4[:, i, 0:3], in_=halo[:, c, 0:3]),
                          reads=[("halo", c)], writes=[("U4", i)])
                    S.add(ACT, lambda e, ps=ps, i=i: e.activation(out=U4[:, i, 3:515], in_=ps[:, :], func=AF.Copy),
                          reads=[pk], writes=[("U4", i)])
                for i in range(4):
                    c = gq * 4 + i
                    S.add(ACT, lambda e, i=i, c=c: e.activation(
                        out=U3[:, i, :], in_=U4[:, i, 0:512], func=AF.Identity, scale=self.v("conv_w", c * 4 + 0),
                        bias=self.v("conv_b", c)),
                        reads=[("U4", i), "vecs"], writes=[("U3", i)])
                for k in range(1, 4):
                    for i in range(4):
                        c = gq * 4 + i
                        S.add(DVE, lambda e, i=i, c=c, k=k: e.scalar_tensor_tensor(
                            out=U3[:, i, :], in0=U4[:, i, k:k + 512], scalar=self.v("conv_w", c * 4 + k),
                            in1=U3[:, i, :], op0=ALU.mult, op1=ALU.add),
                            reads=[("U4", i), ("U3", i), "vecs"], writes=[("U3", i)])
                for i in range(4):
                    c = gq * 4 + i
                    S.add(DVE, lambda e, i=i, c=c: e.tensor_copy(out=halo[:, c, 0:3], in_=U4[:, i, 512:515]),
                          reads=[("U4", i)], writes=[("halo", c)])
                for i in range(4):
                    c = gq * 4 + i
                    if c < 8:
                        dst, dk = BB[:, c, :], ("BB", c)
                    elif c < 10:
                        dst, dk = BT[:, c - 8, :], ("BT", c - 8)
                    else:
                        dst, dk = CT[:, c - 10, :], ("CT", c - 10)
                    S.add(ACT, lambda e, i=i, dst=dst: e.activation(out=dst, in_=U3[:, i, :], func=AF.Silu),
                          reads=[("U3", i)], writes=[dk])
            wdt, wdtk = self.wload(W["w_dt"][0], 128)
            ps, pk = self.bank()
            for j in range(4):
                self.mm(ps[:, j * 16:(j + 1) * 16], pk,
                        [(xnt[:, kc, j * 128:(j + 1) * 128], wdt[:, kc * 16:(kc + 1) * 16], [wdtk, ("xnt", pb, kc)])
                         for kc in range(8)])
            S.add(DVE, lambda e, ps=ps: e.tensor_tensor(
                out=dtr[:, :, :], in0=ps[:, 0:64].rearrange("p (j h) -> p j h", j=4),
                in1=self.v("dt_bias", 0, 16).unsqueeze(1).to_broadcast([128, 4, 16]), op=ALU.add),
                reads=[pk, "vecs"], writes=["dtr"])
            S.add(ACT, lambda e: e.activation(out=dtr[:, :, :], in_=dtr[:, :, :], func=AF.Exp), reads=["dtr"], writes=["dtr"])
            S.add(ACT, lambda e: e.activation(out=dt[:, :, :], in_=dtr[:, :, :], func=AF.Ln, bias=self.ones_f[:, 0:1]),
                  reads=["dtr", "const"], writes=["dt"])
            S.add(DVE, lambda e: e.tensor_tensor(out=adt[:, :, :], in0=dt[:, :, :],
                                                 in1=self.a_rep[:, :].unsqueeze(1).to_broadcast([128, 4, 16]), op=ALU.mult),
                  reads=["dt", "a_rep"], writes=["adt"])
            for j in range(4):
                ps, pk = self.bank()
                self.mm(ps[:, 0:16], pk, [(tri_f, adt[:, j, :], ["adt", "const"])])
                self.mm(ps[:, 16:32], pk, [(self.ones_f[:, :], adt[:, j, :], ["adt", "const"])])
                S.add(DVE, lambda e, ps=ps, j=j: e.tensor_copy(out=cs_sb[:, j, :], in_=ps[:, 0:16]), reads=[pk], writes=[("cs", j)])
                S.add(ACT, lambda e, ps=ps, j=j: e.activation(out=ecl[:, j, :], in_=ps[:, 16:32], func=AF.Exp),
                      reads=[pk], writes=[("ecl", j)])
                S.add(DVE, lambda e, ps=ps, j=j: e.tensor_tensor(out=d1[:, j, :], in0=ps[:, 16:32], in1=cs_sb[:, j, :],
                                                               op=ALU.subtract),
                      reads=[pk, ("cs", j)], writes=[("d1", j)])
                S.add(ACT, lambda e, j=j: e.activation(out=d1[:, j, :], in_=d1[:, j, :], func=AF.Exp),
                      reads=[("d1", j)], writes=[("d1", j)])
                S.add(DVE, lambda e, j=j: e.tensor_tensor(out=wst[:, j, :], in0=d1[:, j, :], in1=dt[:, j, :], op=ALU.mult),
                      reads=[("d1", j), "dt"], writes=[("wst", j)])
            for j in range(4):
                for half in range(2):
                    ps, pk = self.bank()
                    for i in range(4):
                        c = half * 4 + i
                        S.add(PE, lambda e, ps=ps, i=i, c=c, j=j: e.transpose(
                            out=ps[:, i * 128:(i + 1) * 128], in_=BB[:, c, j * 128:(j + 1) * 128], identity=ident_f),
                            reads=[("BB", c), "const"], writes=[pk])
                    psv = ps[:, :].rearrange("p (h d) -> p h d", h=8)
                    S.add(DVE, lambda e, psv=psv, j=j, half=half: e.tensor_tensor(
                        out=U1[:, j, half * 512:(half + 1) * 512].rearrange("p (h d) -> p h d", h=8), in0=psv,
                        in1=dt[:, j, half * 8:(half + 1) * 8].unsqueeze(2).to_broadcast([128, 8, 64]), op=ALU.mult),
                        reads=[pk, "dt"], writes=[("U1", j)])
                    S.add(DVE, lambda e, psv=psv, j=j, half=half: e.tensor_tensor(
                        out=U2[:, j, half * 512:(half + 1) * 512].rearrange("p (h d) -> p h d", h=8), in0=psv,
                        in1=wst[:, j, half * 8:(half + 1) * 8].unsqueeze(2).to_broadcast([128, 8, 64]), op=ALU.mult),
                        reads=[pk, ("wst", j)], writes=[("U2", j)])
            ps, pk = self.bank()
            psb = ps[:, :].bitcast(BF16)
            for j in range(4):
                for g in range(2):
                    S.add(PE, lambda e, psb=psb, j=j, g=g: e.transpose(
                        out=psb[:, (j * 2 + g) * 128:(j * 2 + g + 1) * 128], in_=BT[:, g, j * 128:(j + 1) * 128],
                        identity=self.ident_bf[:, :]),
                        reads=[("BT", g), "const"], writes=[pk])
            S.add(ACT, lambda e, psb=psb: e.activation(out=Btok[:, :, :, :].rearrange("p j g n -> p (j g n)"),
                                                       in_=psb[:, 0:1024], func=AF.Copy),
                  reads=[pk], writes=["Btok"])
            for j in range(4):
                cols = slice(j * 128, (j + 1) * 128)
                ps, pk = self.bank()
                for g in range(2):
                    self.mm(ps[:, g * 128:(g + 1) * 128], pk, [(BT[:, g, cols], CT[:, g, cols], [("BT", g), ("CT", g)])])
                S.add(DVE, lambda e, ps=ps: e.tensor_tensor(
                    out=Gm[:, :, :], in0=ps[:, 0:256].rearrange("p (g l) -> p g l", g=2),
                    in1=tri_f.unsqueeze(1).to_broadcast([128, 2, 128]), op=ALU.mult),
                    reads=[pk, "const"], writes=["Gm"])
                S.add(ACT, lambda e, j=j: e.activation(out=adt_rep, in_=adt[:, j, :].unsqueeze(2).to_broadcast([128, 16, 128]), func=AF.Copy),
                      reads=["adt"], writes=[("U3", i) for i in range(4)])
                for hq in range(4):
                    h0 = hq * 4
                    g = hq // 2
                    b = hq % 2
                    pc, pck = self.bank()
                    for i in range(4):
                        self.mm(pc[:, i * 128:(i + 1) * 128], pck,
                                [(adt_rep[:, h0 + i, :], tri_f, [("U3", (h0 + i) // 4), "const"])])
                    pcv = pc[:, :].rearrange("p (i l) -> p i l", i=4)
                    for i in range(4):
                        S.add(DVE, lambda e, pc=pc, b=b, j=j, h0=h0, i=i: e.scalar_tensor_tensor(
                            out=arg[b][:, i, :], in0=pc[:, i * 128:(i + 1) * 128], scalar=cs_sb[:, j, h0 + i:h0 + i + 1],
                            in1=tri_f, op0=ALU.subtract, op1=ALU.mult),
                            reads=[pck, ("cs", j), "const"], writes=[("arg", b, i)], cost=0.25)
                    S.add(ACT, lambda e, b=b: e.activation(out=dec[b][:, :, :], in_=arg[b][:, :, :], func=AF.Exp),
                          reads=[("arg", b, i) for i in range(4)], writes=[("dec", b)])
                    S.add(DVE, lambda e, b=b, g=g: e.tensor_tensor(
                        out=Mb[b][:, :, :], in0=dec[b][:, :, :],
                        in1=Gm[:, g, :].unsqueeze(1).to_broadcast([128, 4, 128]), op=ALU.mult),
                        reads=[("dec", b), "Gm"], writes=[("Mb", b)])
                    S.add(ACT, lambda e, b=b, pcv=pcv: e.activation(out=ecs[b][:, :, :], in_=pcv, func=AF.Exp),
                          reads=[pck], writes=[("ecs", b)])
                    S.add(DVE, lambda e, b=b, g=g, cols=cols: e.tensor_tensor(
                        out=Cp[b][:, :, :], in0=ecs[b][:, :, :],
                        in1=CT[:, g, cols].unsqueeze(1).to_broadcast([128, 4, 128]), op=ALU.mult),
                        reads=[("ecs", b), ("CT", g)], writes=[("Cp", b)])
                    yb, ybk = self.held(4 + hq // 2)
                    for i in range(4):
                        h = h0 + i
                        hb = (h % 2) * 64
                        pl = (h // 2) % 4
                        self.mm(yb[hb:hb + 64, pl * 128:(pl + 1) * 128], ybk,
                                [(U1[:, j, h * 64:(h + 1) * 64], Mb[b][:, i, :], [("U1", j), ("Mb", b)]),
                                 (prevTb[:, h * 64:(h + 1) * 64], Cp[b][:, i, :], [("prevTb", g), ("Cp", b)])])
                    if hq % 2 == 1:
                        for pl in range(4):
                            c = (hq // 2) * 4 + pl
                            S.add(DVE, lambda e, yb=yb, pl=pl, c=c, cols=cols: e.scalar_tensor_tensor(
                                out=BC[:, c, cols], in0=BB[:, c, cols], scalar=self.v("d_skip", c),
                                in1=yb[:, pl * 128:(pl + 1) * 128], op0=ALU.mult, op1=ALU.add),
                                reads=[ybk, ("BB", c), "vecs"], writes=[("BC", c)])
                for g in range(2):
                    ps, pk = self.bank()
                    self.mm(ps[:, :], pk, [(Btok[:, j, g, :], U2[:, j, g * 512:(g + 1) * 512], ["Btok", ("U2", j)])])
                    pv = prevT[:, g * 512:(g + 1) * 512]
                    S.add(DVE, lambda e, pv=pv, j=j, g=g: e.tensor_tensor(
                        out=pv.rearrange("p (h d) -> p h d", h=8), in0=pv.rearrange("p (h d) -> p h d", h=8),
                        in1=ecl[:, j, g * 8:(g + 1) * 8].unsqueeze(2).to_broadcast([128, 8, 64]), op=ALU.mult),
                        reads=[("prevT", g), ("ecl", j)], writes=[("prevT", g)])
                    S.add(DVE, lambda e, pv=pv, ps=ps: e.tensor_tensor(out=pv, in0=ps[:, :], in1=pv, op=ALU.add),
                          reads=[pk, ("prevT", g)], writes=[("prevT", g)])
                    S.add(ACT, lambda e, pv=pv, g=g: e.activation(out=prevTb[:, g * 512:(g + 1) * 512], in_=pv, func=AF.Copy),
                          reads=[("prevT", g)], writes=[("prevTb", g)])
            for c in range(8):
                S.add(DVE, lambda e, c=c: e.tensor_tensor(out=BC[:, c, :], in0=BC[:, c, :], in1=BA[:, c, :], op=ALU.mult),
                      reads=[("BC", c), ("BA", c)], writes=[("BC", c)])
            for g in range(2):
                self.rms_rstd([(BC[:, 4 * g + i, :], ("BC", 4 * g + i)) for i in range(4)], 512, sqv, sqk, rstd, "rstd")
                for i in range(4):
                    c = 4 * g + i
                    self.norm_apply(yn[:, c, :], ("U2", c // 2), BC[:, c, :], ("BC", c), self.v("ssd_norm_g", c), rstd, "rstd")
            yacts = [(yn[:, c, :], ("U2", c // 2)) for c in range(8)]
            for c in range(8):
                wb, wk = self.wload(W["w_ssd"][c], 1024)
                wg_, wgk = self.wload(W["w_gs"][c], 1024)
                ps, pk = self.bank()
                ps2, pk2 = self.bank()
                self.lin(ps2[:, :], pk2, wg_, wgk, 8, 128, xacts)
                self.lin(ps[:, :], pk, wb, wk, 8, 128, yacts)
                gb = gbuf[c % 2]
                gk = ("g", c % 2)
                S.add(ACT, lambda e, ps2=ps2, gb=gb, c=c: e.activation(out=gb, in_=ps2[:, :], func=AF.Sigmoid,
                                                                      bias=self.v("gb_ssd", c)),
                      reads=[pk2, "vecs"], writes=[gk])
                S.add(DVE, lambda e, ps=ps, gb=gb: e.tensor_tensor(out=ttmp, in0=ps[:, :], in1=gb, op=ALU.mult),
                      reads=[pk, gk], writes=["ttmp"])
                S.add(DVE, lambda e, c=c, mgt=mgt: e.tensor_tensor(out=mg[:, c, :], in0=ttmp, in1=mgt[:, c, :], op=ALU.add),
                      reads=["ttmp", ("mgt", pb, c)], writes=[("U1", c // 2)])
            macts = [(mg[:, c, :], ("U1", c // 2)) for c in range(8)]
            for c in range(8):
                wb, wk = self.wload(W["w_out"][c], 1024)
                ps, pk = self.bank()
                self.lin(ps[:, :], pk, wb, wk, 8, 128, macts)
                S.add(ACT, lambda e, ps=ps, c=c: e.activation(out=BA[:, c, :], in_=ps[:, :], func=AF.Copy),
                      reads=[pk], writes=[("BA", c)])
            self.rms_rstd([(BA[:, c, :], ("BA", c)) for c in range(8)], D, sqv, sqk, rstd, "rstd")
            xspv = self.xsp.rearrange("(c p) t -> p c t", p=128)
            S.add(SP, lambda e, t0=t0: e.dma_start(out=BB[:, :, :], in_=xspv[:, :, t0:t0 + TT]),
                  reads=[("xsp", c, t) for c in range(8)], writes=[("BB", c) for c in range(8)], dma=True)
            for c in range(8):
                self.norm_apply(BA[:, c, :], ("BA", c), BA[:, c, :], ("BA", c), self.v("mix_post_g", c), rstd, "rstd")
                S.add(DVE, lambda e, c=c: e.tensor_tensor(out=BB[:, c, :], in0=BB[:, c, :], in1=BA[:, c, :], op=ALU.add),
                      reads=[("BB", c), ("BA", c)], writes=[("BB", c)])
            S.add(SP, lambda e, t0=t0: e.dma_start(out=xspv[:, :, t0:t0 + TT], in_=BB[:, :, :]),
                  reads=[("BB", c) for c in range(8)], writes=[("xsp", c, t) for c in range(8)], dma=True)
        AH.release()
        AX.release()

    def xattn(self, s):
        S, AH = self.S, self.AH
        x = self.x
        W = self.w
        NT = SEQ // TT
        AH.mark()
        memf = AH.f32([8, MEM])
        memn = AH.bf16([8, MEM])
        kx = AH.bf16([8, MEM])
        vx = AH.bf16([2, 1024])
        sq2 = [AH.bf16([8, TT]) for _ in range(2)]
        rstd2 = [AH.f32([TT]) for _ in range(2)]
        xnt2 = [AH.bf16([8, TT]) for _ in range(2)]
        qx2 = [AH.bf16([8, TT]) for _ in range(2)]
        ptx = [AH.bf16([TT]) for _ in range(4)]
        rdx2 = [AH.f32([TT]) for _ in range(2)]
        ox2 = [AH.bf16([8, TT]) for _ in range(2)]
        hx = AH.f32([8, TT])
        sq, rstd = sq2[0], rstd2[0]
        S.add(SP, lambda e: e.dma_start(out=memf[:, :, :], in_=self.memT[s].rearrange("(c p) m -> p c m", p=128)),
              writes=[("memf", c) for c in range(8)], dma=True)
        self.rms_rstd([(memf[:, c, :], ("memf", c)) for c in range(8)], D, sq[:, :, 0:MEM], lambda c: ("sq", 0, c),
                      rstd[:, 0:MEM], ("rstd", 0))
        for c in range(8):
            self.norm_apply(memn[:, c, :], ("memn", c), memf[:, c, :], ("memf", c), self.v("mem_norm_g", c), rstd[:, 0:MEM],
                            ("rstd", 0))
        macts = [(memn[:, c, :], ("memn", c)) for c in range(8)]
        for c in range(8):
            wb, wk = self.wload(W["w_xk"][c], 1024)
            ps, pk = self.bank()
            self.lin(ps[:, 0:MEM], pk, wb, wk, 8, 128, macts)
            S.add(ACT, lambda e, ps=ps, c=c: e.activation(out=kx[:, c, :], in_=ps[:, 0:MEM], func=AF.Copy),
                  reads=[pk], writes=[("kx", c)])
        for half in range(2):
            wbs = [self.wload(W["w_xv"][half, kp], 1024) for kp in range(4)]
            for mb in range(2):
                ps, pk = self.bank()
                self.mm(ps[:, :], pk, [(memn[:, kc, mb * 128:(mb + 1) * 128],
                                        wbs[kc // 2][0][:, (kc % 2) * 512:(kc % 2 + 1) * 512],
                                        [wbs[kc // 2][1], ("memn", kc)]) for kc in range(8)])
                S.add(ACT, lambda e, ps=ps, mb=mb, half=half: e.activation(
                    out=vx[:, mb, half * 512:(half + 1) * 512], in_=ps[:, :], func=AF.Copy),
                    reads=[pk], writes=[("vx", mb, half)])
        pti = 0
        for t in range(NT):
            t0 = t * TT
            pb = t % 2
            sq, rstd, xnt, qx, rdx, ox = sq2[pb], rstd2[pb], xnt2[pb], qx2[pb], rdx2[pb], ox2[pb]
            sqn, rsn = ("sq", pb), ("rstd", pb)
            self.rms_rstd([(x[:, c, t0:t0 + TT], ("x", c, t)) for c in range(8)], D, sq, lambda c, pb=pb: ("sq", pb, c), rstd, rsn)
            for c in range(8):
                self.norm_apply(xnt[:, c, :], ("xnt", pb, c), x[:, c, t0:t0 + TT], ("x", c, t), self.v("xa_pre_g", c), rstd, rsn)
            xacts = [(xnt[:, c, :], ("xnt", pb, c)) for c in range(8)]
            for c in range(8):
                wb, wk = self.wload(W["w_xq"][c], 1024)
                ps, pk = self.bank()
                self.lin(ps[:, :], pk, wb, wk, 8, 128, xacts)
                S.add(ACT, lambda e, ps=ps, c=c, qx=qx: e.activation(out=qx[:, c, :], in_=ps[:, :], func=AF.Copy),
                      reads=[pk], writes=[("qx", pb, c)])
            for hh in range(4):
                pp = []
                for mb in range(2):
                    ps, pk = self.bank()
                    self.mm(ps[:, :], pk, [(kx[:, 2 * hh + kc, mb * 128:(mb + 1) * 128], qx[:, 2 * hh + kc, :],
                                            [("kx", 2 * hh + kc), ("qx", pb, 2 * hh + kc)]) for kc in range(2)])
                    pt = ptx[pti % 4]
                    ptk = ("ptx", pti % 4)
                    pti += 1
                    S.add(ACT, lambda e, ps=ps, pt=pt: e.activation(out=pt, in_=ps[:, :], func=AF.Exp, scale=SCALE_XA),
                          reads=[pk], writes=[ptk])
                    pp.append((pt, ptk))
                pd, pdk = self.bank()
                self.mm(pd[:, :], pdk, [(self.ones_bf[:, :], pp[mb][0], [pp[mb][1], "const"]) for mb in range(2)])
                S.add(ACT, lambda e, pd=pd, rdx=rdx: e.activation(out=rdx, in_=pd[:, :], func=AF.Ln), reads=[pdk], writes=[("rdx", pb)])
                S.add(ACT, lambda e, rdx=rdx: e.activation(out=rdx, in_=rdx, func=AF.Exp, scale=-1.0), reads=[("rdx", pb)], writes=[("rdx", pb)])
                for dvc in range(2):
                    c = 2 * hh + dvc
                    po, pok = self.bank()
                    self.mm(po[:, :], pok, [(vx[:, mb, c * 128:(c + 1) * 128], pp[mb][0],
                                             [("vx", mb, c // 4), pp[mb][1]]) for mb in range(2)])
                    S.add(DVE, lambda e, po=po, c=c, ox=ox, rdx=rdx: e.tensor_tensor(out=ox[:, c, :], in0=po[:, :], in1=rdx, op=ALU.mult),
                          reads=[pok, ("rdx", pb)], writes=[("ox", pb, c)])
            oacts = [(ox[:, c, :], ("ox", pb, c)) for c in range(8)]
            for c in range(8):
                wb, wk = self.wload(W["w_xo"][c], 1024)
                ps, pk = self.bank()
                self.lin(ps[:, :], pk, wb, wk, 8, 128, oacts)
                S.add(ACT, lambda e, ps=ps, c=c: e.activation(out=hx[:, c, :], in_=ps[:, :], func=AF.Copy),
                      reads=[pk], writes=[("hx", c)])
            self.rms_rstd([(hx[:, c, :], ("hx", c)) for c in range(8)], D, sq, lambda c, pb=pb: ("sq", pb, c), rstd, rsn)
            for c in range(8):
                self.norm_apply(hx[:, c, :], ("hx", c), hx[:, c, :], ("hx", c), self.v("xa_post_g", c), rstd, rsn)
                S.add(DVE, lambda e, c=c, t0=t0: e.tensor_tensor(out=x[:, c, t0:t0 + TT], in0=x[:, c, t0:t0 + TT],
                                                                in1=hx[:, c, :], op=ALU.add),
                      reads=[("hx", c), ("x", c, t)], writes=[("x", c, t)])
        AH.release()
        S.barrier()

    def build(self):
        nc, S = self.nc, self.S
        nseq = self.nseq
        xT = self.inp("xT", [nseq, D, SEQ])
        self.memT = self.inp("memT", [nseq, D, MEM])
        self.pos_d = self.inp("pos", [nseq, SEQ], I32)
        outT = nc.dram_tensor("outT", [nseq, D, SEQ], F32, kind="ExternalOutput").ap()
        self.xsp = nc.dram_tensor("xsp", [D, SEQ], F32, kind="Internal").ap()
        self.xn_hbm = nc.dram_tensor("xn_hbm", [D, SEQ], BF16, kind="Internal").ap()
        self.mg_hbm = nc.dram_tensor("mg_hbm", [D, SEQ], BF16, kind="Internal").ap()
        vecs_d = self.inp("vecs", [128, NV])
        cst_d = self.inp("cst", [128, 256])
        w = {}
        for f in ("ffn1", "ffn2"):
            w[f + "_wg"] = self.inp(f + "_wg", [NFF, 128, 1024])
            w[f + "_wu"] = self.inp(f + "_wu", [NFF, 128, 1024])
            w[f + "_wd"] = self.inp(f + "_wd", [8, 2, 128, 1408])
        for n, shp in (("w_z", [8, 128, 1024]), ("w_xbc", [12, 128, 1024]), ("w_gs", [8, 128, 1024]), ("w_gm", [8, 128, 1024]),
                       ("w_qc", [3, 128, 1024]), ("w_kvc", [2, 128, 1024]), ("w_kra", [1, 128, 1024]), ("w_krb", [1, 128, 1024]),
                       ("w_dt", [1, 128, 128]), ("w_uqn", [8, 128, 384]), ("w_uqa", [4, 128, 384]), ("w_uqb", [4, 128, 384]),
                       ("w_uk", [8, 128, 256]), ("w_uv", [2, 128, 1024]), ("w_ssd", [8, 128, 1024]), ("w_mla", [8, 128, 1024]),
                       ("w_out", [8, 128, 1024]), ("w_xq", [8, 128, 1024]), ("w_xk", [8, 128, 1024]), ("w_xo", [8, 128, 1024]),
                       ("w_xv", [2, 4, 128, 1024])):
            w[n] = self.inp(n, shp)
        self.w = w

        with contextlib.ExitStack() as es:
            sb = lambda n, s, d: es.enter_context(nc.sbuf_tensor(n, s, d))
            self.vecs = sb("vecs_sb", [128, NV], F32)
            self.cst = sb("cst_sb", [128, 256], F32)
            self.ones_bf = sb("ones_bf", [128, 128], BF16)
            self.tri_bf = sb("tri_bf", [128, 128], BF16)
            self.ident_bf = sb("ident_bf", [128, 128], BF16)
            self.ones_f = sb("ones_f", [128, 128], F32)
            self.eps_ap = sb("eps", [128, 1], F32)
            self.a_rep = sb("a_rep", [128, 16], F32)
            self.NWB = 6
            self.wbufs = [sb("wb%d" % i, [128, WSLOT], BF16) for i in range(self.NWB)]
            XW = 8 * SEQ
            ARW = 45 * 1024
            arena = sb("arena", [128, ARW], F32)
            self.x = arena[:, 0:XW].rearrange("p (c t) -> p c t", c=8)
            self.AX = Arena(arena[:, 0:XW], XW)
            self.AH = Arena(arena[:, XW:ARW], ARW - XW)
            self.ps = [es.enter_context(nc.psum_tensor("ps%d" % i, [128, 512], F32)) for i in range(8)]

            S.add(DVE, lambda e: e.memset(self.ones_bf[:, :], 1.0), writes=["ones"])
            S.add(DVE, lambda e: e.memset(self.ones_f[:, :], 1.0), writes=["onesf"])
            S.add(DVE, lambda e: e.memset(self.eps_ap[:, :], EPS), writes=["eps"])
            S.add(SP, lambda e: e.dma_start(out=self.vecs[:, :], in_=vecs_d), writes=["vecs"], dma=True)
            S.add(SP, lambda e: e.dma_start(out=self.cst[:, :], in_=cst_d), writes=["cst"], dma=True)
            S.add(DVE, lambda e: e.tensor_copy(out=self.tri_bf[:, :], in_=self.cst[:, 0:128]), reads=["cst"], writes=["tribf"])
            S.add(DVE, lambda e: e.tensor_copy(out=self.ident_bf[:, :], in_=self.cst[:, 128:256]), reads=["cst"], writes=["idbf"])
            S.add(ACT, lambda e: e.activation(out=self.a_rep[:, :], in_=self.v("a_log", 0, 16), func=AF.Exp),
                  reads=["vecs"], writes=["a_rep"])
            S.add(DVE, lambda e: e.tensor_scalar(out=self.a_rep[:, :], in0=self.a_rep[:, :], scalar1=-1.0, scalar2=None,
                                                 op0=ALU.mult), reads=["a_rep"], writes=["a_rep"])
            S.barrier()
            outs = []
            for s in range(nseq):
                for c in range(8):
                    S.add(SP, lambda e, c=c, s=s: e.dma_start(out=self.x[:, c, :], in_=xT[s, c * 128:(c + 1) * 128, :]),
                          writes=[("x", c, t) for t in range(4)], dma=True)
                if self.stop_after >= 1:
                    self.ffn(w["ffn1_wg"], w["ffn1_wu"], w["ffn1_wd"], "ffn1_pre_g", "ffn1_post_g")
                if self.stop_after >= 2:
                    self.mixer(s)
                if self.stop_after >= 3:
                    self.xattn(s)
                if self.stop_after >= 4:
                    self.ffn(w["ffn2_wg"], w["ffn2_wu"], w["ffn2_wd"], "ffn2_pre_g", "ffn2_post_g")
                for c in range(8):
                    outs.append(S.add(SP, lambda e, c=c, s=s: e.dma_start(
                        out=outT[s, c * 128:(c + 1) * 128, :], in_=self.x[:, c, :]),
                        reads=[("x", c, t) for t in range(4)], dma=True))
                S.barrier()
            S.emit(outs)
        return nc


def prep_shared(inp):
    f = lambda n: np.asarray(inp[n][0], np.float32)
    sh = {}
    vec = np.zeros((128, NV), np.float32)

    def put(name, arr):
        vec[:, VO[name]:VO[name] + arr.shape[1]] = arr

    for n in ("ffn1_pre_g", "ffn1_post_g", "mix_pre_g", "mix_post_g", "xa_pre_g", "mem_norm_g", "xa_post_g",
              "ffn2_pre_g", "ffn2_post_g", "ssd_norm_g", "q_norm_g", "kv_norm_g", "conv_b"):
        put(n, colvec(f(n)))
    put("conv_w", np.transpose(f("conv_w").reshape(4, 12, 128), (2, 1, 0)).reshape(128, 48))
    gb = f("gate_bias")
    put("gb_ssd", colvec(gb[:1024]))
    put("gb_mla", colvec(gb[1024:]))
    put("d_skip", colvec(np.repeat(f("d_skip"), 64)))
    put("dt_bias", np.tile(f("dt_bias")[None, :], (128, 1)))
    put("a_log", np.tile(f("a_log")[None, :], (128, 1)))
    r = np.arange(128) % 32
    inv = (np.float32(10000.0) ** (-(np.arange(0, 32, 2, dtype=np.float32)) / np.float32(32))).astype(np.float32)
    put("invf", inv[r % 16][:, None])
    put("sgn", np.where(r < 16, -1.0, 1.0).astype(np.float32)[:, None])
    sh["vecs"] = vec
    k = np.arange(128)
    tri = (k[:, None] <= k[None, :]).astype(np.float32)
    sh["cst"] = np.ascontiguousarray(np.concatenate([tri, np.eye(128, dtype=np.float32)], axis=1))
    for p in ("ffn1", "ffn2"):
        sh[p + "_wg"] = blockify(f(p + "_w_gate"), 128).reshape(NFF, 128, 1024)
        sh[p + "_wu"] = blockify(f(p + "_w_up"), 128).reshape(NFF, 128, 1024)
        sh[p + "_wd"] = np.ascontiguousarray(
            blockify(f(p + "_w_down"), 128).reshape(8, 128, 2, 1408).transpose(0, 2, 1, 3))
    win = f("w_in")
    b128 = lambda m: blockify(np.ascontiguousarray(m), 128).reshape(m.shape[1] // 128, 128, -1)
    sh["w_z"] = b128(win[:, 0:1024])
    sh["w_xbc"] = b128(win[:, 1024:2560])
    sh["w_dt"] = blockify(np.ascontiguousarray(win[:, 2560:2576]), 16).reshape(1, 128, 128)
    sh["w_qc"] = b128(win[:, 2576:2960])
    sh["w_kvc"] = b128(win[:, 2960:3216])
    kr = win[:, 3216:3248]
    sh["w_kra"] = b128(np.tile(kr, (1, 4)))
    sh["w_krb"] = b128(np.tile(np.concatenate([kr[:, 16:32], kr[:, 0:16]], axis=1), (1, 4)))
    sh["w_gs"] = b128(win[:, 3248:4272])
    sh["w_gm"] = b128(win[:, 4272:5296])
    wuq = f("w_uq").reshape(384, 16, 96)
    sh["w_uqn"] = b128(wuq[:, :, 0:64].reshape(384, 1024))
    sh["w_uqa"] = b128(wuq[:, :, 64:96].reshape(384, 512))
    sh["w_uqb"] = b128(np.concatenate([wuq[:, :, 80:96], wuq[:, :, 64:80]], axis=2).reshape(384, 512))
    sh["w_uk"] = b128(f("w_uk"))
    sh["w_uv"] = blockify(f("w_uv"), 512).reshape(2, 128, 1024)
    for n, src in (("w_ssd", "w_ssd_proj"), ("w_mla", "w_mla_proj"), ("w_out", "w_out"), ("w_xq", "w_xq"),
                   ("w_xk", "w_xk"), ("w_xo", "w_xo")):
        sh[n] = b128(f(src))
    sh["w_xv"] = np.ascontiguousarray(blockify(f("w_xv"), 512).reshape(2, 128, 4, 1024).transpose(0, 2, 1, 3))
    return sh


def run(inp, nseq, cores, stop_after=99):
    k = K(nseq, stop_after)
    nc = k.build()
    sh = prep_shared(inp)
    maps = []
    for ci in range(cores):
        m = dict(sh)
        sl = slice(ci * nseq, (ci + 1) * nseq)
        m["xT"] = np.ascontiguousarray(np.transpose(np.asarray(inp["x"][sl], np.float32), (0, 2, 1)))
        m["memT"] = np.ascontiguousarray(np.transpose(np.asarray(inp["mem"][sl], np.float32), (0, 2, 1)))
        m["pos"] = np.ascontiguousarray(np.asarray(inp["positions"][sl], np.int32))
        maps.append({n: m[n] for n in k.din})
    res = run_bass_kernel_spmd(nc, maps, core_ids=list(range(cores)))
    outs = [np.transpose(r["outT"], (0, 2, 1)) for r in res.results]
    return np.ascontiguousarray(np.concatenate(outs, axis=0)).astype(np.float32)


def kernel(**inputs):
    inp = {k: np.asarray(v) for k, v in inputs.items()}
    return run(inp, 2, NCORES)
```
